# Optimizing a Trainium2 kernel written in Bass

```python
import math
import jax, jax.numpy as jnp
from jax import lax
import numpy as np

D_MODEL = 4096
BATCH = 32
SEQ = 256
DEPTH = 1
DEC_BATCH = 4
DEC_SEQ = 4096
PAST_LEN = 512

GRID_W = 64
N_HEADS = 32
QK_NOPE = 128
QK_ROPE = 64
QK_HEAD = QK_NOPE + QK_ROPE
V_HEAD = 128
Q_LORA = 1024
KV_LORA = 512
S5_WIDTH = 2048
S5_GROUP = 16
S5_GROUPS = S5_WIDTH // S5_GROUP
S5_STATE = 64
D_FF = 4 * D_MODEL
ROPE_THETA = 10000.0
EPS = 1e-6
Q_BLOCK = 128
N_MOD = 6
IN_COLS = Q_LORA + KV_LORA + QK_ROPE + S5_WIDTH + 2 * D_MODEL
IN_SPLITS = (Q_LORA, Q_LORA + KV_LORA, Q_LORA + KV_LORA + QK_ROPE,
             Q_LORA + KV_LORA + QK_ROPE + S5_WIDTH,
             Q_LORA + KV_LORA + QK_ROPE + S5_WIDTH + D_MODEL)

kernel_name = 'hybrid_mla_s5_diffusion_step'


def rms_norm(x, g):
    xf = x.astype(jnp.float32)
    y = xf * lax.rsqrt(jnp.mean(jnp.square(xf), axis=-1, keepdims=True) + EPS)
    return (y * g.astype(jnp.float32)).astype(x.dtype)


def axial_rope(n_tokens):
    rows = n_tokens // GRID_W
    row = jnp.repeat(jnp.arange(rows, dtype=jnp.float32), GRID_W)
    col = jnp.tile(jnp.arange(GRID_W, dtype=jnp.float32), rows)
    n_freq = QK_ROPE // 4
    inv_freq = ROPE_THETA ** (-jnp.arange(n_freq, dtype=jnp.float32) / n_freq)
    ang = jnp.concatenate([row[:, None] * inv_freq, col[:, None] * inv_freq], axis=-1)
    ang = jnp.concatenate([ang, ang], axis=-1)
    return jnp.cos(ang), jnp.sin(ang)


def apply_rope(x, cos, sin):
    xf = x.astype(jnp.float32)
    x1, x2 = jnp.split(xf, 2, axis=-1)
    rot = jnp.concatenate([-x2, x1], axis=-1)
    return (xf * cos + rot * sin).astype(x.dtype)


def adaln(cond, w_mod, b_mod):
    m = jax.nn.silu(cond) @ w_mod + b_mod
    return m.reshape(cond.shape[0], 1, N_MOD, D_MODEL)


def modulated_norm(x, g, shift, scale):
    return rms_norm(x, g) * (1 + scale) + shift


def block_attention(q, k, v):
    b, l, h, dqk = q.shape
    nb = l // Q_BLOCK
    qb = q.reshape(b, nb, Q_BLOCK, h, dqk).transpose(1, 0, 2, 3, 4)
    scale = dqk ** -0.5

    def one_block(q_blk):
        s = jnp.einsum('bqhd,bkhd->bhqk', q_blk, k).astype(jnp.float32) * scale
        p = jax.nn.softmax(s, axis=-1).astype(v.dtype)
        return jnp.einsum('bhqk,bkhd->bqhd', p, v)

    o = lax.map(one_block, qb)
    return o.transpose(1, 0, 2, 3, 4).reshape(b, l, h * V_HEAD)


def mla_queries(q_lat, lp):
    q = rms_norm(q_lat, lp['g_q_lat']) @ lp['w_uq']
    return q.reshape(q.shape[0], q.shape[1], N_HEADS, QK_HEAD)


def decompress_kv(ckv_n, k_rope, lp):
    b, lk, _ = ckv_n.shape
    kv = (ckv_n @ lp['w_ukv']).reshape(b, lk, N_HEADS, QK_NOPE + V_HEAD)
    k_nope, v = kv[..., :QK_NOPE], kv[..., QK_NOPE:]
    k_r = jnp.broadcast_to(k_rope[:, :, None, :], (b, lk, N_HEADS, QK_ROPE))
    return jnp.concatenate([k_nope, k_r], axis=-1), v


def diag_scan(a_bar, bu, h0, reverse):
    if reverse:
        bu = jnp.flip(bu, axis=1)
    bu = bu.at[:, 0].add(a_bar * h0)
    a = jnp.broadcast_to(a_bar, (1,) + bu.shape[1:])

    def combine(e1, e2):
        a1, b1 = e1
        a2, b2 = e2
        return a1 * a2, a2 * b1 + b2

    _, h = lax.associative_scan(combine, (a, bu), axis=1)
    h_last = h[:, -1]
    if reverse:
        h = jnp.flip(h, axis=1)
    return h, h_last


def s5_branch(u, h0, lp):
    b, l, _ = u.shape
    f32 = jnp.float32
    uf = u.astype(f32)
    uc = uf.reshape(b, l, S5_GROUPS, S5_GROUP).astype(jnp.complex64)
    y = uf * lp['s5_d'].astype(f32)
    finals = []
    for d, reverse in ((0, False), (1, True)):
        lam = lax.complex(lp['s5_lam_re'][d].astype(f32), lp['s5_lam_im'][d].astype(f32))
        dt = jnp.exp(lp['s5_log_dt'][d].astype(f32))[:, None]
        a_bar = jnp.exp(lam * dt)
        b_mat = lax.complex(lp['s5_b_re'][d].astype(f32), lp['s5_b_im'][d].astype(f32))
        b_bar = ((a_bar - 1.0) / lam)[..., None] * b_mat
        c_mat = lax.complex(lp['s5_c_re'][d].astype(f32), lp['s5_c_im'][d].astype(f32))
        bu = jnp.einsum('blgj,gpj->blgp', uc, b_bar)
        h, h_last = diag_scan(a_bar, bu, h0[:, d], reverse)
        y = y + jnp.einsum('blgp,gjp->blgj', h, c_mat).real.reshape(b, l, S5_WIDTH)
        finals.append(h_last)
    z = jax.nn.gelu(y).astype(u.dtype) @ lp['w_glu']
    za, zb = jnp.split(z, 2, axis=-1)
    return za * jax.nn.sigmoid(zb), jnp.stack(finals, axis=1)


def merge_branches(o_a, o_b, g_a, g_b, lp):
    m = jax.nn.sigmoid(g_a) * o_a + jax.nn.sigmoid(g_b) * o_b
    return m @ lp['w_out']


def context_mixer(h, lp):
    b = h.shape[0]
    q_lat, ckv, k_rope, u, g_a, g_b = jnp.split(h @ lp['w_in'], IN_SPLITS, axis=-1)
    q = mla_queries(q_lat, lp)
    ckv_n = rms_norm(ckv, lp['g_kv_lat'])
    k, v = decompress_kv(ckv_n, k_rope, lp)
    o_a = block_attention(q, k, v)
    h0 = jnp.zeros((b, 2, S5_GROUPS, S5_STATE), jnp.complex64)
    o_b, s5_last = s5_branch(u, h0, lp)
    return merge_branches(o_a, o_b, g_a, g_b, lp), ckv_n, k_rope, s5_last


def latent_mixer(h, ctx_ckv, ctx_krope, ctx_s5, cos, sin, lp):
    q_lat, ckv, k_rope, u, g_a, g_b = jnp.split(h @ lp['w_in'], IN_SPLITS, axis=-1)
    q = mla_queries(q_lat, lp)
    q = jnp.concatenate([q[..., :QK_NOPE], apply_rope(q[..., QK_NOPE:], cos[:, None], sin[:, None])], axis=-1)
    k_rope = apply_rope(k_rope, cos, sin)
    ckv_all = jnp.concatenate([ctx_ckv.astype(h.dtype), rms_norm(ckv, lp['g_kv_lat'])], axis=1)
    krope_all = jnp.concatenate([ctx_krope.astype(h.dtype), k_rope], axis=1)
    k, v = decompress_kv(ckv_all, krope_all, lp)
    o_a = block_attention(q, k, v)
    o_b, _ = s5_branch(u, ctx_s5, lp)
    return merge_branches(o_a, o_b, g_a, g_b, lp)


def mixer_input(x, mod, lp):
    return modulated_norm(x, lp['g_pre_mix'], mod[:, :, 0], mod[:, :, 1])


def add_mixer(x, out, mod, lp):
    return x + mod[:, :, 2] * rms_norm(out, lp['g_post_mix'])


def add_ffn(x, mod, lp):
    h = modulated_norm(x, lp['g_pre_mlp'], mod[:, :, 3], mod[:, :, 4])
    a = jnp.square(jax.nn.relu(h @ lp['w_ff1'])) @ lp['w_ff2']
    return x + mod[:, :, 5] * rms_norm(a, lp['g_post_mlp'])


def setup_inputs(seed: int = 0) -> dict:
    key = jax.random.key(seed)
    ks = jax.random.split(key, 32)
    f32 = jnp.float32

    def nrm(k, shape, scale=1.0):
        return jax.random.normal(k, shape, f32) * scale

    def gain(k, shape):
        return 1.0 + 0.05 * jax.random.normal(k, shape, f32)

    G, P, J = S5_GROUPS, S5_STATE, S5_GROUP
    n_idx = jnp.arange(P, dtype=f32)
    return {
        'x_prompt': nrm(ks[0], (BATCH, SEQ, D_MODEL)),
        'x_sample': nrm(ks[1], (DEC_BATCH, DEC_SEQ, D_MODEL)),
        'c': nrm(ks[2], (DEC_BATCH, D_MODEL)),
        'cache_ckv': nrm(ks[3], (DEC_BATCH, DEPTH, PAST_LEN, KV_LORA)),
        'cache_krope': nrm(ks[4], (DEC_BATCH, DEPTH, PAST_LEN, QK_ROPE)),
        'state_s5': nrm(ks[5], (DEC_BATCH, DEPTH, 2, 2, G, P), 0.1),
        'c_ctx': nrm(ks[6], (D_MODEL,)),
        'w_mod': nrm(ks[7], (DEPTH, D_MODEL, N_MOD * D_MODEL), 0.5 * D_MODEL ** -0.5),
        'b_mod': nrm(ks[8], (DEPTH, N_MOD * D_MODEL), 0.01),
        'g_pre_mix': gain(ks[9], (DEPTH, D_MODEL)),
        'w_in': nrm(ks[10], (DEPTH, D_MODEL, IN_COLS), D_MODEL ** -0.5),
        'g_q_lat': gain(ks[11], (DEPTH, Q_LORA)),
        'g_kv_lat': gain(ks[12], (DEPTH, KV_LORA)),
        'w_uq': nrm(ks[13], (DEPTH, Q_LORA, N_HEADS * QK_HEAD), Q_LORA ** -0.5),
        'w_ukv': nrm(ks[14], (DEPTH, KV_LORA, N_HEADS * (QK_NOPE + V_HEAD)), KV_LORA ** -0.5),
        's5_lam_re': -0.5 + 0.02 * nrm(ks[15], (DEPTH, 2, G, P)),
        's5_lam_im': math.pi * n_idx + 0.01 * nrm(ks[16], (DEPTH, 2, G, P)),
        's5_log_dt': jax.random.uniform(ks[17], (DEPTH, 2, G), f32, math.log(1e-3), math.log(1e-1)),
        's5_b_re': nrm(ks[18], (DEPTH, 2, G, P, J), (2 * J) ** -0.5),
        's5_b_im': nrm(ks[19], (DEPTH, 2, G, P, J), (2 * J) ** -0.5),
        's5_c_re': nrm(ks[20], (DEPTH, 2, G, J, P), (2 * P) ** -0.5),
        's5_c_im': nrm(ks[21], (DEPTH, 2, G, J, P), (2 * P) ** -0.5),
        's5_d': nrm(ks[22], (DEPTH, S5_WIDTH)),
        'w_glu': nrm(ks[23], (DEPTH, S5_WIDTH, 2 * D_MODEL), S5_WIDTH ** -0.5),
        'w_out': nrm(ks[24], (DEPTH, D_MODEL, D_MODEL), D_MODEL ** -0.5),
        'g_post_mix': gain(ks[25], (DEPTH, D_MODEL)),
        'g_pre_mlp': gain(ks[26], (DEPTH, D_MODEL)),
        'w_ff1': nrm(ks[27], (DEPTH, D_MODEL, D_FF), D_MODEL ** -0.5),
        'w_ff2': nrm(ks[28], (DEPTH, D_FF, D_MODEL), D_FF ** -0.5),
        'g_post_mlp': gain(ks[29], (DEPTH, D_MODEL)),
    }


def reference(x_prompt, x_sample, c, cache_ckv, cache_krope, state_s5, c_ctx, w_mod, b_mod,
              g_pre_mix, w_in, g_q_lat, g_kv_lat, w_uq, w_ukv, s5_lam_re, s5_lam_im, s5_log_dt,
              s5_b_re, s5_b_im, s5_c_re, s5_c_im, s5_d, w_glu, w_out, g_post_mix, g_pre_mlp,
              w_ff1, w_ff2, g_post_mlp):
    f32 = jnp.float32
    cos, sin = axial_rope(x_sample.shape[1])
    xp, xs = x_prompt, x_sample
    new_ckv, new_krope, new_s5 = [], [], []
    for l in range(DEPTH):
        lp = {
            'g_pre_mix': g_pre_mix[l], 'w_in': w_in[l], 'g_q_lat': g_q_lat[l], 'g_kv_lat': g_kv_lat[l],
            'w_uq': w_uq[l], 'w_ukv': w_ukv[l], 's5_lam_re': s5_lam_re[l], 's5_lam_im': s5_lam_im[l],
            's5_log_dt': s5_log_dt[l], 's5_b_re': s5_b_re[l], 's5_b_im': s5_b_im[l],
            's5_c_re': s5_c_re[l], 's5_c_im': s5_c_im[l], 's5_d': s5_d[l], 'w_glu': w_glu[l],
            'w_out': w_out[l], 'g_post_mix': g_post_mix[l], 'g_pre_mlp': g_pre_mlp[l],
            'w_ff1': w_ff1[l], 'w_ff2': w_ff2[l], 'g_post_mlp': g_post_mlp[l],
        }
        mod_p = adaln(c_ctx[None, :], w_mod[l], b_mod[l])
        out_p, ckv_n, k_rope_p, s5_last = context_mixer(mixer_input(xp, mod_p, lp), lp)
        xp = add_ffn(add_mixer(xp, out_p, mod_p, lp), mod_p, lp)
        new_ckv.append(ckv_n)
        new_krope.append(k_rope_p)
        new_s5.append(jnp.stack([s5_last.real, s5_last.imag], axis=2))
        mod_s = adaln(c, w_mod[l], b_mod[l])
        ctx_s5 = lax.complex(state_s5[:, l, :, 0].astype(f32), state_s5[:, l, :, 1].astype(f32))
        out_s = latent_mixer(mixer_input(xs, mod_s, lp), cache_ckv[:, l], cache_krope[:, l], ctx_s5, cos, sin, lp)
        xs = add_ffn(add_mixer(xs, out_s, mod_s, lp), mod_s, lp)
    new_cache_ckv = jnp.stack(new_ckv, axis=1)
    new_cache_krope = jnp.stack(new_krope, axis=1)
    new_state_s5 = jnp.stack(new_s5, axis=1).astype(x_prompt.dtype)
    return (xp, xs, new_cache_ckv, new_cache_krope, new_state_s5)
```

```python
import contextlib
import math
import os
import numpy as np
import concourse.bass as bass
import concourse.mybir as mybir
from concourse.bass_utils import run_bass_kernel_spmd

F32 = mybir.dt.float32
BF16 = mybir.dt.bfloat16
AF = mybir.ActivationFunctionType
ALU = mybir.AluOpType

D = 4096
NQ, NKV, NR, NU = 1024, 512, 64, 2048
H = 32
DFF = 16384
EPS = 1e-6
TB = 512
NOTH, NOWN, NPR = [int(v) for v in os.environ.get('MK_SIZES', '2048,2048,1024').split(',')]
T5 = NOTH + NOWN + NPR
TO = NOWN + NPR
NCTX = 512
NKEY = NCTX + NOTH + NOWN + NPR
DEBUG = bool(int(os.environ.get("MK_DEBUG", "0")))
STOP_AFTER = int(os.environ.get("MK_STOP", "99"))


class Tok:
    __slots__ = ("key", "sem", "val")

    def __init__(self, key, sem, val):
        self.key, self.sem, self.val = key, sem, val


class Buf:
    def __init__(self, k, ap, name):
        self.k, self.ap, self.name = k, ap, name
        self.w, self.r = {}, {}
        self.sem = None
        self.cnt = 0

    def __getitem__(self, key):
        return self.ap[key]


class Eng:
    def __init__(self, k, name, e, own_wait):
        self.k, self.name, self.e = k, name, e
        self.key, self.sem = k.new_sem("e_" + name)
        self.n = 0
        self.seen = {}
        self.own_wait = own_wait
        self.pending = []

    def wait(self, tok):
        if tok.key == self.key and not self.own_wait:
            return
        if self.seen.get(tok.key, 0) >= tok.val:
            return
        self.e.wait_ge(tok.sem, tok.val)
        self.seen[tok.key] = tok.val


class K:
    def __init__(self, nc, es):
        self.nc, self.es = nc, es
        self.nsem = 0
        self.pe = Eng(self, "pe", nc.tensor, False)
        self.act = Eng(self, "act", nc.scalar, True)
        self.dve = Eng(self, "dve", nc.vector, True)
        self.pool = Eng(self, "pool", nc.gpsimd, True)
        self.sp = Eng(self, "sp", nc.sync, False)
        self.store_toks = {}
        self.nbuf = 0

    def new_sem(self, name):
        self.nsem += 1
        s = self.es.enter_context(self.nc.semaphore("s%d_%s" % (self.nsem, name)))
        return self.nsem, s

    def sb(self, shape, dt, name):
        self.nbuf += 1
        t = self.es.enter_context(self.nc.sbuf_tensor("%s_%d" % (name, self.nbuf), list(shape), dt))
        return Buf(self, t, name)

    def ring(self, n, shape, dt, name):
        return Ring([self.sb(shape, dt, name + str(i)) for i in range(n)])

    def _deps(self, eng, reads, writes):
        for b in reads:
            for t in b.w.values():
                eng.wait(t)
        for b in writes:
            for t in b.w.values():
                eng.wait(t)
            for t in b.r.values():
                eng.wait(t)

    def op(self, eng, fn, reads=(), writes=(), signal=True, **kw):
        self._deps(eng, reads, writes)
        inst = fn(**kw)
        if not signal:
            eng.pending.append((reads, writes))
            return None
        eng.n += 1
        inst.then_inc(eng.sem, 1)
        tok = Tok(eng.key, eng.sem, eng.n)
        for (rs, ws) in eng.pending + [(reads, writes)]:
            for b in rs:
                b.r[tok.key] = tok
            for b in ws:
                b.w = {tok.key: tok}
                b.r = {}
        eng.pending = []
        return tok

    def dma(self, q, out, in_, reads=(), writes=(), sembuf=None, store=False, **kw):
        self._deps(q, reads, writes)
        sbf = sembuf
        if sbf.sem is None:
            sbf.key, sbf.sem = self.new_sem("b_" + sbf.name)
        inst = q.e.dma_start(out=out, in_=in_, **kw)
        sbf.cnt += 16
        inst.then_inc(sbf.sem, 16)
        tok = Tok(sbf.key, sbf.sem, sbf.cnt)
        for b in reads:
            b.r[tok.key] = tok
        for b in writes:
            b.w = {tok.key: tok}
            b.r = {}
        if store:
            self.store_toks[tok.key] = tok
        return tok

    def load(self, q, buf, out, in_, **kw):
        return self.dma(q, out, in_, reads=(), writes=(buf,), sembuf=buf, **kw)

    def loadp(self, q, buf, out, in_, **kw):
        self._deps(q, (), ())
        tok = self.dma(q, out, in_, reads=(), writes=(), sembuf=buf, **kw)
        buf.w = {tok.key: tok}
        return tok

    def store(self, q, buf, out, in_, **kw):
        return self.dma(q, out, in_, reads=(buf,), writes=(), sembuf=buf, store=True, **kw)

    def dram_barrier(self):
        for q in (self.sp, self.pool):
            for t in self.store_toks.values():
                q.wait(t)

    def phase_barrier(self, bufs):
        nc = self.nc
        tok = self.op(self.dve, lambda: nc.vector.memset(self.scr[:, 0:1], 0.0), reads=(), writes=tuple(bufs) + (self.scr,))
        for e in (self.pe, self.act, self.pool, self.sp):
            e.wait(tok)

    def finish(self):
        for t in self.store_toks.values():
            self.sp.wait(t)


class Ring:
    def __init__(self, bufs):
        self.bufs = bufs
        self.i = 0

    def next(self):
        b = self.bufs[self.i % len(self.bufs)]
        self.i += 1
        return b


def _chunks(tiles, maxw):
    out, cur, w = [], [], 0
    for t in tiles:
        if cur and (w + t[1] > maxw or cur[-1][0] + cur[-1][1] != t[0]):
            out.append(cur)
            cur, w = [], 0
        cur.append(t)
        w += t[1]
    if cur:
        out.append(cur)
    return out


def build():
    nc = bass.Bass("TRN2", target_bir_lowering=False)
    es = contextlib.ExitStack()
    with es:
        _build(nc, es)
    return nc


def _build(nc, es):
    k = K(nc, es)
    pe, act, dve, pool, sp = k.pe, k.act, k.dve, k.pool, k.sp

    def din(name, shape, dt=F32):
        return nc.dram_tensor(name, list(shape), dt, kind="ExternalInput").ap()

    def dout(name, shape, dt=F32):
        return nc.dram_tensor(name, list(shape), dt, kind="ExternalOutput").ap()

    def dscr(name, shape, dt):
        return nc.dram_tensor(name, list(shape), dt, kind=("ExternalOutput" if DEBUG else "Internal")).ap()

    x_all = din("x_all", [T5, D])
    cvec = din("cvec", [2, D])
    ckv_ctx = din("ckv_ctx", [NCTX, NKV])
    krope_ctx = din("krope_ctx", [NCTX, NR])
    cosT = din("cosT", [NR, NOTH + NOWN])
    sinT = din("sinT", [NR, NOTH + NOWN])
    cident = din("cident", [128, 128])
    cperm = din("cperm", [64, 64])
    w_mod = din("w_mod", [D, 6 * D])
    b_mod = din("b_mod", [1, 6 * D])
    g_pre_mix = din("g_pre_mix", [D])
    w_in = din("w_in", [D, 11840])
    g_q_lat = din("g_q_lat", [NQ])
    g_kv_lat = din("g_kv_lat", [NKV])
    w_uq = din("w_uq", [NQ, H * 192])
    w_ukv = din("w_ukv", [NKV, H * 256])
    w_glu = din("w_glu", [NU, 2 * D])
    w_out = din("w_out", [D, D])
    g_post_mix = din("g_post_mix", [D])
    g_pre_mlp = din("g_pre_mlp", [D])
    w_ff1 = din("w_ff1", [D, DFF])
    w_ff2 = din("w_ff2", [DFF, D])
    g_post_mlp = din("g_post_mlp", [D])
    s5_d = din("s5_d", [NU])
    s5P = {
        "lam_re": din("s5_lam_re", [2, 128, 64]), "lam_im": din("s5_lam_im", [2, 128, 64]), "log_dt": din("s5_log_dt", [2, 128]),
        "b_re": din("s5_b_re", [2, 128, 64, 16]), "b_im": din("s5_b_im", [2, 128, 64, 16]),
        "c_re": din("s5_c_re", [2, 128, 16, 64]), "c_im": din("s5_c_im", [2, 128, 16, 64]),
        "h0": din("s5_h0", [2, 2, 128, 64]),
    }

    y_own = dout("y_own", [TO, D])
    ckv_new = dout("ckv_new", [NPR, NKV])
    krope_new = dout("krope_new", [NPR, NR])
    s5_new = dout("s5_new", [max(1, NPR // 256), 2, 2, 128, 64])

    modv = dscr("modv", [2, 6 * D], F32)
    xT = dscr("xT", [D, TO], F32)
    qT = dscr("qT", [H * 192, TO], BF16)
    ckvT = dscr("ckvT", [NKV, NKEY], BF16)
    kropeT = dscr("kropeT", [NR, NKEY], BF16)
    uT = dscr("uT", [NU, T5], BF16)
    sgaT = dscr("sgaT", [D, TO], BF16)
    sgbT = dscr("sgbT", [D, TO], BF16)
    maT = dscr("maT", [D, TO], BF16)
    yT = dscr("yT", [NU, TO], BF16)
    mT = dscr("mT", [D, TO], BF16)
    x1T = dscr("x1T", [D, TO], F32)
    a1T = dscr("a1T", [DFF, TO], BF16)
    s5P["s5_new"] = s5_new
    s5P["Kst"] = dscr("s5Kst", [2, 16, 16, 128, 128], BF16)
    s5P["M1"] = dscr("s5M1", [2, 16, 16, 2, 128, 128], BF16)
    s5P["M2"] = dscr("s5M2", [2, 17, 2, 128, 2048], BF16)

    ident = k.sb([128, 128], F32, "ident")
    perm = k.sb([64, 64], F32, "perm")
    ones_f = k.sb([128, 128], F32, "ones_f")
    ones_b = k.sb([128, 128], BF16, "ones_b")
    epsb = k.sb([128, 1], F32, "epsb")
    k.scr = k.sb([128, 4], F32, "scr")
    k.load(sp, ident, ident[:], cident)
    k.load(sp, perm, perm[:], cperm)
    k.op(dve, lambda: nc.vector.memset(ones_f[:], 1.0), writes=(ones_f,))
    k.op(dve, lambda: nc.vector.memset(ones_b[:], 1.0), writes=(ones_b,))
    k.op(dve, lambda: nc.vector.memset(epsb[:], EPS), writes=(epsb,))
    sc_a_mix = k.sb([128, 2, 32], F32, "a_mix")
    sc_sh_mix = k.sb([128, 2, 32], F32, "sh_mix")
    sc_gg_mix = k.sb([128, 2, 32], F32, "gg_mix")
    sc_a_mlp = k.sb([128, 2, 32], F32, "a_mlp")
    sc_sh_mlp = k.sb([128, 2, 32], F32, "sh_mlp")
    sc_gg_mlp = k.sb([128, 2, 32], F32, "gg_mlp")
    gq = k.sb([128, 8], F32, "gq")
    gkv = k.sb([128, 4], F32, "gkv")

    psum = es.enter_context(nc.psum_tensor("psum", [128, 8, 512], F32))
    PS = [Buf(k, psum[:, i, :], "ps%d" % i) for i in range(8)]
    gbank = Ring(PS[0:4])
    PR = PS[4]
    PM = Ring(PS[5:8])

    WR = {}
    st_bf = k.ring(4, [128, TB], BF16, "stbf")

    nslow = dict(allow_slow_non_contiguous=True)

    def gemm(W, KT, tiles, xview, xbufs, cb, ntok=TB, maxw=None):
        maxw = maxw or (8192 // KT)
        Wv = W.rearrange("(t p) n -> p t n", p=128)
        chunks = _chunks(tiles, maxw)

        def issue_load(ch):
            slot = WR['ring'].next()
            cw = sum(t[1] for t in ch)
            view = slot.ap[:, 0:KT * cw].rearrange("p (t c) -> p t c", c=cw)
            c0 = ch[0][0]
            k.load(pool, slot, view, Wv[:, :, c0:c0 + cw])
            return slot, view

        pend = [issue_load(chunks[0])]
        for ci, ch in enumerate(chunks):
            if ci + 1 < len(chunks):
                pend.append(issue_load(chunks[ci + 1]))
            slot, view = pend.pop(0)
            off = 0
            for (c0, M, tag) in ch:
                bank = gbank.next()
                for kt in range(KT):
                    k.op(pe, lambda kt=kt, off=off, M=M, bank=bank, view=view: nc.tensor.matmul(
                        bank.ap[0:M, 0:ntok], view[:, kt, off:off + M], xview(kt),
                        start=(kt == 0), stop=(kt == KT - 1)),
                        reads=(slot,) + tuple(xbufs), writes=(bank,), signal=(kt == KT - 1))
                cb(bank, M, tag)
                off += M

    def rstd_from_bank(bank, F, out_buf, ntok=TB):
        k.op(act, lambda: nc.scalar.activation(out=out_buf.ap[:, 0:ntok], in_=bank.ap[:, 0:ntok], func=AF.Sqrt,
                                               bias=epsb[:, 0:1], scale=1.0 / F),
             reads=(bank, epsb), writes=(out_buf,))
        k.op(dve, lambda: nc.vector.reciprocal(out=out_buf.ap[:, 0:ntok], in_=out_buf.ap[:, 0:ntok]),
             reads=(out_buf,), writes=(out_buf,))

    def sumsq_tile(src_ap, src_buf, first, last, scratch, ntok=TB):
        k.op(act, lambda: nc.scalar.activation(out=scratch.ap[:, 0:ntok], in_=src_ap, func=AF.Square),
             reads=(src_buf,), writes=(scratch,))
        k.op(pe, lambda: nc.tensor.matmul(PR.ap[:, 0:ntok], ones_f[:], scratch.ap[:, 0:ntok], start=first, stop=last),
             reads=(ones_f, scratch), writes=(PR,), signal=True)

    k.s5 = None
    if STOP_AFTER >= 4 and not int(os.environ.get("MK_NOS5", "0")):
        k.s5 = _S5(k, nc, dict(psb=k.sb, ident=ident, PS=PS, PM=PM, gbank=gbank, s5P=s5P))
        k.s5.alloc_persist()
    with contextlib.ExitStack() as ph:
        def psb(shape, dt, name):
            k.nbuf += 1
            t = ph.enter_context(nc.sbuf_tensor("%s_%d" % (name, k.nbuf), list(shape), dt))
            return Buf(k, t, name)
        cT = psb([128, 2, 32], F32, "cT")
        scT = psb([128, 2, 32], BF16, "scT")
        WR['ring'] = Ring([psb([128, 8192], BF16, "wslot%d" % i) for i in range(3)])
        msr = Ring([psb([2, 256], F32, "msr%d" % i) for i in range(3)])
        s5gen = k.s5.pre(s5P) if k.s5 is not None else None
        bmr = Ring([psb([2, 512], F32, "bmr%d" % i) for i in range(3)])
        for j in range(2):
            (k.load if j == 0 else k.loadp)(sp, cT, cT[:, j, :], cvec[j].rearrange("(t p) -> p t", p=128), **nslow)
        k.op(act, lambda: nc.scalar.activation(out=scT[:], in_=cT[:], func=AF.Silu), reads=(cT,), writes=(scT,))
        wmv = w_mod.rearrange("(t p) n -> p t n", p=128)
        CWM = 256
        NCH = 6 * D // CWM

        def ldwm(ci):
            slot = WR['ring'].next()
            view = slot.ap[:, 0:32 * CWM].rearrange("p (t c) -> p t c", c=CWM)
            k.load(pool, slot, view, wmv[:, :, ci * CWM:(ci + 1) * CWM])
            return slot, view
        pend = [ldwm(0), ldwm(1)]
        for ci in range(NCH):
            if ci + 2 < NCH:
                pend.append(ldwm(ci + 2))
            wb, wview = pend.pop(0)
            bank = gbank.next()
            for kt in range(32):
                k.op(pe, lambda kt=kt, wview=wview, bank=bank: nc.tensor.matmul(
                    bank.ap[0:2, 0:CWM], scT[:, :, kt], wview[:, kt, :], start=(kt == 0), stop=(kt == 31)),
                    reads=(wb, scT), writes=(bank,), signal=(kt == 31))
            bm = bmr.next()
            k.load(sp, bm, bm[:, 0:CWM], b_mod[0, ci * CWM:(ci + 1) * CWM].partition_broadcast(2))
            ms = msr.next()
            k.op(dve, lambda bank=bank, ci=ci, bm=bm, ms=ms: nc.vector.tensor_tensor(
                out=ms[:, 0:CWM], in0=bank.ap[0:2, 0:CWM], in1=bm[:, 0:CWM],
                op=ALU.add), reads=(bank, bm), writes=(ms,))
            k.store(sp, ms, modv[:, ci * CWM:(ci + 1) * CWM], ms[:, 0:CWM])
            if s5gen is not None:
                for _ in range(6):
                    next(s5gen, None)
        if s5gen is not None:
            for _ in s5gen:
                pass
        k.dram_barrier()
        modT = psb([128, 2, 6, 32], F32, "modT")
        gT = psb([128, 4, 32], F32, "gT")
        for j in range(2):
            for i in range(6):
                (k.load if (j == 0 and i == 0) else k.loadp)(sp, modT, modT[:, j, i, :],
                                                             modv[j, i * D:(i + 1) * D].rearrange("(t p) -> p t", p=128), **nslow)
        for gi, g in enumerate((g_pre_mix, g_post_mix, g_pre_mlp, g_post_mlp)):
            (k.load if gi == 0 else k.loadp)(sp, gT, gT[:, gi, :], g.rearrange("(t p) -> p t", p=128), **nslow)
        k.load(sp, gq, gq[:], g_q_lat.rearrange("(t p) -> p t", p=128), **nslow)
        k.load(sp, gkv, gkv[:], g_kv_lat.rearrange("(t p) -> p t", p=128), **nslow)
        for j in range(2):
            for (dst, gi, isc) in ((sc_a_mix, 0, 1), (sc_a_mlp, 2, 4)):
                k.op(dve, lambda dst=dst, gi=gi, isc=isc, j=j: nc.vector.scalar_tensor_tensor(
                    out=dst[:, j, :], in0=modT[:, j, isc, :], scalar=1.0, in1=gT[:, gi, :], op0=ALU.add, op1=ALU.mult),
                    reads=(modT, gT), writes=(dst,))
            for (dst, ish) in ((sc_sh_mix, 0), (sc_sh_mlp, 3)):
                k.op(dve, lambda dst=dst, ish=ish, j=j: nc.vector.tensor_copy(out=dst[:, j, :], in_=modT[:, j, ish, :]),
                     reads=(modT,), writes=(dst,))
            for (dst, gi, ig) in ((sc_gg_mix, 1, 2), (sc_gg_mlp, 3, 5)):
                k.op(dve, lambda dst=dst, gi=gi, ig=ig, j=j: nc.vector.tensor_tensor(
                    out=dst[:, j, :], in0=modT[:, j, ig, :], in1=gT[:, gi, :], op=ALU.mult),
                    reads=(modT, gT), writes=(dst,))
        k.phase_barrier((modT, gT, cT, scT) + tuple(bmr.bufs) + tuple(WR['ring'].bufs) + tuple(msr.bufs))

    if STOP_AFTER <= 0:
        k.finish()
        return

    with contextlib.ExitStack() as ph:
        def psb(shape, dt, name):
            k.nbuf += 1
            t = ph.enter_context(nc.sbuf_tensor("%s_%d" % (name, k.nbuf), list(shape), dt))
            return Buf(k, t, name)
        WR['ring'] = Ring([psb([128, 8192], BF16, "wslot%d" % i) for i in range(3)])
        xtok = Ring([psb([128, D], F32, "xtok%d" % i) for i in range(1)])
        xTst = Ring([psb([128, 32, 128], F32, "xTst%d" % i) for i in range(1)])
        hT = psb([128, 32, TB], BF16, "hT")
        Rsb = psb([128, TB], F32, "Rsb")
        R2 = psb([128, TB], F32, "R2")
        tmpf = Ring([psb([128, 128], F32, "tmpf%d" % i) for i in range(2)])
        qlat = psb([128, 8, TB], F32, "qlat")
        qn = psb([128, 8, TB], BF16, "qn")
        ckvf = psb([128, 4, TB], F32, "ckvf")
        krf = psb([64, TB], F32, "krf")
        cs_t = psb([64, 2, TB], F32, "cs_t")
        sq = Ring([psb([128, TB], F32, "sq%d" % i) for i in range(2)])
        ropef = Ring([psb([64, TB], F32, "ropef%d" % i) for i in range(2)])
        t1r = Ring([psb([64, TB], F32, "t1r%d" % i) for i in range(2)])
        t2r = Ring([psb([64, TB], F32, "t2r%d" % i) for i in range(2)])
        tokst = Ring([psb([128, NKV], F32, "tokst%d" % i) for i in range(2)])

        def rope(src, dst_st):
            bank = PM.next()
            k.op(pe, lambda: nc.tensor.matmul(bank.ap[0:64, :], perm[:], src.ap[0:64, :], start=True, stop=True),
                 reads=(perm, src), writes=(bank,))
            t1, t2 = t1r.next(), t2r.next()
            k.op(dve, lambda: nc.vector.tensor_tensor(out=t1[:], in0=src.ap[0:64, :], in1=cs_t[:, 0, :], op=ALU.mult),
                 reads=(src, cs_t), writes=(t1,))
            k.op(dve, lambda: nc.vector.tensor_tensor(out=t2[:], in0=bank.ap[0:64, :], in1=cs_t[:, 1, :], op=ALU.mult),
                 reads=(bank, cs_t), writes=(t2,))
            k.op(dve, lambda: nc.vector.tensor_tensor(out=dst_st.ap[0:64, :], in0=t1[:], in1=t2[:], op=ALU.add),
                 reads=(t1, t2), writes=(dst_st,))

        for blk in range(T5 // TB):
            tok0 = blk * TB
            own = tok0 >= NOTH
            prompt = tok0 >= NOTH + NOWN
            j = 1 if prompt else 0
            o0 = tok0 - NOTH
            kv0 = NCTX + tok0
            if not prompt:
                k.load(sp, cs_t, cs_t[:, 0, :], cosT[:, tok0:tok0 + TB])
                k.loadp(sp, cs_t, cs_t[:, 1, :], sinT[:, tok0:tok0 + TB])
            for tt in range(4):
                xt = xtok.next()
                k.load(sp, xt, xt[:], x_all[tok0 + tt * 128: tok0 + (tt + 1) * 128, :])
                xs = xTst.next()
                for t in range(32):
                    bank = PM.next()
                    k.op(pe, lambda xt=xt, t=t, bank=bank: nc.tensor.transpose(
                        bank.ap[:, 0:128], xt[:, t * 128:(t + 1) * 128], ident[:]),
                        reads=(xt, ident), writes=(bank,))
                    k.op(act, lambda xs=xs, t=t, bank=bank: nc.scalar.copy(out=xs[:, t, :], in_=bank.ap[:, 0:128]),
                         reads=(bank,), writes=(xs,) if t == 0 else ())
                    s = sq.next()
                    k.op(act, lambda s=s, bank=bank: nc.scalar.activation(out=s.ap[:, 0:128], in_=bank.ap[:, 0:128], func=AF.Square),
                         reads=(bank,), writes=(s,))
                    k.op(pe, lambda s=s, t=t: nc.tensor.matmul(PR.ap[:, 0:128], ones_f[:], s.ap[:, 0:128], start=(t == 0), stop=(t == 31)),
                         reads=(ones_f, s), writes=(PR,), signal=True)
                xs.w = {act.key: Tok(act.key, act.sem, act.n)}
                rstd_from_bank(PR, D, Rsb, ntok=128)
                for t in range(32):
                    tf = tmpf.next()
                    k.op(dve, lambda tf=tf, xs=xs, t=t: nc.vector.tensor_tensor(
                        out=tf[:], in0=xs[:, t, :], in1=Rsb[:, 0:128], op=ALU.mult),
                        reads=(xs, Rsb), writes=(tf,))
                    k.op(act, lambda tf=tf, t=t, tt=tt, j=j: nc.scalar.activation(
                        out=hT[:, t, tt * 128:(tt + 1) * 128], in_=tf[:], func=AF.Identity,
                        bias=sc_sh_mix[:, j, t:t + 1], scale=sc_a_mix[:, j, t:t + 1]),
                        reads=(tf, sc_sh_mix, sc_a_mix), writes=(hT,) if (t == 0 and tt == 0) else ())
                if own:
                    k.store(sp, xs, xT.rearrange("(t p) n -> p t n", p=128)[:, :, o0 + tt * 128:o0 + (tt + 1) * 128], xs[:])
            hT.w = {act.key: Tok(act.key, act.sem, act.n)}
            hT.r = {}

            tiles = []
            if own:
                tiles += [(i * 128, 128, ("ql", i)) for i in range(8)]
            tiles += [(NQ + i * 128, 128, ("ckv", i)) for i in range(4)]
            tiles += [(NQ + NKV, 64, ("kr", 0))]
            tiles += [(NQ + NKV + NR + i * 128, 128, ("u", i)) for i in range(16)]
            if own:
                tiles += [(NQ + NKV + NR + NU + i * 128, 128, ("ga", i)) for i in range(32)]
                tiles += [(NQ + NKV + NR + NU + D + i * 128, 128, ("gb", i)) for i in range(32)]

            def cb(bank, M, tag, tok0=tok0, o0=o0):
                kind, i = tag
                if kind == "ql":
                    k.op(act, lambda: nc.scalar.copy(out=qlat[:, i, :], in_=bank.ap[:, :]), reads=(bank,),
                         writes=(qlat,) if i == 0 else ())
                elif kind == "ckv":
                    k.op(act, lambda: nc.scalar.copy(out=ckvf[:, i, :], in_=bank.ap[:, :]), reads=(bank,),
                         writes=(ckvf,) if i == 0 else ())
                elif kind == "kr":
                    k.op(act, lambda: nc.scalar.copy(out=krf[:, :], in_=bank.ap[0:64, :]), reads=(bank,), writes=(krf,))
                elif kind == "u":
                    st = st_bf.next()
                    k.op(act, lambda: nc.scalar.copy(out=st[:], in_=bank.ap[:, :]), reads=(bank,), writes=(st,))
                    k.store(sp, st, uT[i * 128:(i + 1) * 128, tok0:tok0 + TB], st[:])
                else:
                    st = st_bf.next()
                    k.op(act, lambda: nc.scalar.activation(out=st[:], in_=bank.ap[:, :], func=AF.Sigmoid),
                         reads=(bank,), writes=(st,))
                    dstT = sgaT if kind == "ga" else sgbT
                    k.store(sp, st, dstT[i * 128:(i + 1) * 128, o0:o0 + TB], st[:])

            gemm(w_in, 32, tiles, lambda kt: hT[:, kt, :], (hT,), cb)
            if own:
                qlat.w = {act.key: Tok(act.key, act.sem, act.n)}
            ckvf.w = {act.key: Tok(act.key, act.sem, act.n)}

            for i in range(4):
                s = sq.next()
                sumsq_tile(ckvf[:, i, :], ckvf, i == 0, i == 3, s)
            rstd_from_bank(PR, NKV, R2)
            for i in range(4):
                k.op(dve, lambda i=i: nc.vector.scalar_tensor_tensor(
                    out=ckvf[:, i, :], in0=ckvf[:, i, :], scalar=gkv[:, i:i + 1], in1=R2[:], op0=ALU.mult, op1=ALU.mult),
                    reads=(ckvf, gkv, R2), writes=(ckvf,))
                st = st_bf.next()
                k.op(act, lambda i=i, st=st: nc.scalar.copy(out=st[:], in_=ckvf[:, i, :]), reads=(ckvf,), writes=(st,))
                k.store(sp, st, ckvT[i * 128:(i + 1) * 128, kv0:kv0 + TB], st[:])
            if prompt:
                p0 = tok0 - NOTH - NOWN
                for tt in range(4):
                    bank = PM.next()
                    for i in range(4):
                        k.op(pe, lambda i=i, tt=tt, bank=bank: nc.tensor.transpose(
                            bank.ap[:, i * 128:(i + 1) * 128], ckvf[:, i, tt * 128:(tt + 1) * 128], ident[:]),
                            reads=(ckvf, ident), writes=(bank,), signal=(i == 3))
                    ts = tokst.next()
                    k.op(act, lambda ts=ts, bank=bank: nc.scalar.copy(out=ts[:], in_=bank.ap[:, :]), reads=(bank,), writes=(ts,))
                    k.store(sp, ts, ckv_new[p0 + tt * 128:p0 + (tt + 1) * 128, :], ts[:])
            st = st_bf.next()
            if prompt:
                k.op(act, lambda st=st: nc.scalar.copy(out=st.ap[0:64, :], in_=krf[:, :]), reads=(krf,), writes=(st,))
                for tt in range(4):
                    bank = PM.next()
                    k.op(pe, lambda tt=tt, bank=bank: nc.tensor.transpose(
                        bank.ap[:, 0:64], krf[:, tt * 128:(tt + 1) * 128], ident[0:64, 0:64]),
                        reads=(krf, ident), writes=(bank,))
                    ts = tokst.next()
                    k.op(act, lambda ts=ts, bank=bank: nc.scalar.copy(out=ts.ap[:, 0:64], in_=bank.ap[:, 0:64]),
                         reads=(bank,), writes=(ts,))
                    k.store(sp, ts, krope_new[p0 + tt * 128:p0 + (tt + 1) * 128, :], ts.ap[:, 0:64])
            else:
                rope(krf, st)
            k.store(sp, st, kropeT[:, kv0:kv0 + TB], st.ap[0:64, :])

            if own:
                for i in range(8):
                    s = sq.next()
                    sumsq_tile(qlat[:, i, :], qlat, i == 0, i == 7, s)
                rstd_from_bank(PR, NQ, R2)
                for i in range(8):
                    k.op(dve, lambda i=i: nc.vector.scalar_tensor_tensor(
                        out=qn[:, i, :], in0=qlat[:, i, :], scalar=gq[:, i:i + 1], in1=R2[:], op0=ALU.mult, op1=ALU.mult),
                        reads=(qlat, gq, R2), writes=(qn,) if i == 0 else ())
                qn.w = {dve.key: Tok(dve.key, dve.sem, dve.n)}
                qtiles = []
                for h in range(H):
                    qtiles += [(h * 192, 128, ("qn", h)), (h * 192 + 128, 64, ("qr", h))]

                def cbq(bank, M, tag, o0=o0):
                    kind, h = tag
                    st = st_bf.next()
                    if kind == "qn":
                        k.op(act, lambda: nc.scalar.copy(out=st[:], in_=bank.ap[:, :]), reads=(bank,), writes=(st,))
                        k.store(sp, st, qT[h * 192:h * 192 + 128, o0:o0 + TB], st[:])
                    else:
                        if prompt:
                            k.op(act, lambda: nc.scalar.copy(out=st.ap[0:64, :], in_=bank.ap[0:64, :]), reads=(bank,), writes=(st,))
                        else:
                            rf = ropef.next()
                            k.op(act, lambda: nc.scalar.copy(out=rf[:], in_=bank.ap[0:64, :]), reads=(bank,), writes=(rf,))
                            rope(rf, st)
                        k.store(sp, st, qT[h * 192 + 128:h * 192 + 192, o0:o0 + TB], st.ap[0:64, :])

                gemm(w_uq, 8, qtiles, lambda kt: qn[:, kt, :], (qn,), cbq, maxw=768)
        k.phase_barrier(tuple(xtok.bufs) + (hT, Rsb, R2, qlat, qn, ckvf, krf, cs_t) + tuple(xTst.bufs)
                        + tuple(tmpf.bufs) + tuple(sq.bufs) + tuple(ropef.bufs) + tuple(t1r.bufs) + tuple(t2r.bufs) + tuple(tokst.bufs)
                        + tuple(WR['ring'].bufs))


    if STOP_AFTER <= 1:
        k.finish()
        return
    k.dram_barrier()
    NPS = NPR // 256
    SKV = NCTX + NOTH + NOWN

    with contextlib.ExitStack() as ph:
        def psb(shape, dt, name):
            k.nbuf += 1
            t = ph.enter_context(nc.sbuf_tensor("%s_%d" % (name, k.nbuf), list(shape), dt))
            return Buf(k, t, name)
        allb = []

        def prg(n, shape, dt, name):
            r = Ring([psb(shape, dt, name + str(i)) for i in range(n)])
            allb.extend(r.bufs)
            return r
        ctok = prg(2, [128, NKV], F32, "ctok")
        krtok = prg(2, [128, NR], F32, "krtok")
        ckv_sb = prg(1, [128, 4, SKV], BF16, "ckv_sb").bufs[0]
        kr_sb = prg(1, [64, SKV], BF16, "kr_sb").bufs[0]
        KhTr = prg(2, [128, SKV], BF16, "KhT")
        Vhr = prg(2, [128, SKV // 128, 128], BF16, "Vh")
        wkvr = prg(2, [128, 4, 256], BF16, "wkv")
        qnr = prg(2, [128, NOWN], BF16, "qnh")
        qrr = prg(2, [64, NOWN], BF16, "qrh")
        PTr = prg(3, [128, TB], BF16, "PT")
        recr = prg(2, [128, TB], F32, "rec")
        ofr = prg(2, [128, TB], F32, "of")
        sgar = prg(2, [128, TB], BF16, "sga")
        PO = Ring([PS[5], PS[6]])
        PSUMS = Ring([PS[7], PS[4]])
        for t4 in range(NCTX // 128):
            ct = ctok.next()
            k.load(sp, ct, ct[:], ckv_ctx[t4 * 128:(t4 + 1) * 128, :])
            bank = gbank.next()
            for i in range(4):
                k.op(pe, lambda i=i, ct=ct, bank=bank: nc.tensor.transpose(bank.ap[:, i * 128:(i + 1) * 128], ct[:, i * 128:(i + 1) * 128], ident[:]),
                     reads=(ct, ident), writes=(bank,), signal=(i == 3))
            st = st_bf.next()
            k.op(act, lambda st=st, bank=bank: nc.scalar.copy(out=st[:], in_=bank.ap[:, :]), reads=(bank,), writes=(st,))
            k.store(sp, st, ckvT.rearrange("(i p) n -> p i n", p=128)[:, :, t4 * 128:(t4 + 1) * 128],
                    st.ap[:, :].rearrange("p (i n) -> p i n", n=128))
            kt_ = krtok.next()
            k.load(sp, kt_, kt_[:], krope_ctx[t4 * 128:(t4 + 1) * 128, :])
            bank = gbank.next()
            k.op(pe, lambda kt_=kt_, bank=bank: nc.tensor.transpose(bank.ap[0:64, 0:128], kt_[:, :], ident[:]),
                 reads=(kt_, ident), writes=(bank,))
            st = st_bf.next()
            k.op(act, lambda st=st, bank=bank: nc.scalar.copy(out=st.ap[0:64, 0:128], in_=bank.ap[0:64, 0:128]), reads=(bank,), writes=(st,))
            k.store(sp, st, kropeT[:, t4 * 128:(t4 + 1) * 128], st.ap[0:64, 0:128])
        k.dram_barrier()
        SCALE = 192.0 ** -0.5
        wkv_v = w_ukv.rearrange("(t p) n -> p t n", p=128)

        def attn_seq(q0, Lq, kv0, Lk):
            nkt = Lk // 128
            k.load(sp, ckv_sb, ckv_sb[:, :, 0:Lk], ckvT.rearrange("(i p) n -> p i n", p=128)[:, :, kv0:kv0 + Lk])
            k.load(sp, kr_sb, kr_sb[:, 0:Lk], kropeT[:, kv0:kv0 + Lk])
            for h in range(H):
                wkv = wkvr.next()
                k.load(pool, wkv, wkv[:], wkv_v[:, :, h * 256:(h + 1) * 256])
                qn_, qr_ = qnr.next(), qrr.next()
                k.load(sp, qn_, qn_[:, 0:Lq], qT[h * 192:h * 192 + 128, q0:q0 + Lq])
                k.load(sp, qr_, qr_[:, 0:Lq], qT[h * 192 + 128:h * 192 + 192, q0:q0 + Lq])
                KhT, Vh = KhTr.next(), Vhr.next()
                for kb in range(0, Lk, 512):
                    w = min(512, Lk - kb)
                    bank = gbank.next()
                    for kt in range(4):
                        k.op(pe, lambda kt=kt, kb=kb, w=w, bank=bank, wkv=wkv: nc.tensor.matmul(
                            bank.ap[:, 0:w], wkv[:, kt, 0:128], ckv_sb[:, kt, kb:kb + w], start=(kt == 0), stop=(kt == 3)),
                            reads=(wkv, ckv_sb), writes=(bank,), signal=(kt == 3))
                    k.op(act, lambda kb=kb, w=w, bank=bank, KhT=KhT: nc.scalar.copy(out=KhT[:, kb:kb + w], in_=bank.ap[:, 0:w]),
                         reads=(bank,), writes=(KhT,) if kb == 0 else ())
                KhT.w = {act.key: Tok(act.key, act.sem, act.n)}
                for g4 in range(0, nkt, 4):
                    n4 = min(4, nkt - g4)
                    bank = gbank.next()
                    for i in range(n4):
                        for kt in range(4):
                            k.op(pe, lambda kt=kt, i=i, g4=g4, bank=bank, wkv=wkv: nc.tensor.matmul(
                                bank.ap[:, i * 128:(i + 1) * 128], ckv_sb[:, kt, (g4 + i) * 128:(g4 + i + 1) * 128], wkv[:, kt, 128:256],
                                start=(kt == 0), stop=(kt == 3)),
                                reads=(wkv, ckv_sb), writes=(bank,), signal=(kt == 3 and i == n4 - 1))
                    k.op(dve, lambda g4=g4, n4=n4, bank=bank, Vh=Vh: nc.vector.tensor_copy(
                        out=Vh[:, g4:g4 + n4, :], in_=bank.ap[:, 0:n4 * 128].rearrange("p (i n) -> p i n", n=128)),
                        reads=(bank,), writes=(Vh,) if g4 == 0 else ())
                Vh.w = {dve.key: Tok(dve.key, dve.sem, dve.n)}
                for qb in range(0, Lq, TB):
                    nq = min(TB, Lq - qb)
                    po, psm = PO.next(), PSUMS.next()

                    def s_mm(kt):
                        bank = gbank.next()
                        k.op(pe, lambda: nc.tensor.matmul(bank.ap[:, 0:nq], KhT[:, kt * 128:(kt + 1) * 128], qn_[:, qb:qb + nq],
                                                          start=True, stop=False),
                             reads=(KhT, qn_), writes=(bank,), signal=False)
                        k.op(pe, lambda: nc.tensor.matmul(bank.ap[:, 0:nq], kr_sb[:, kt * 128:(kt + 1) * 128], qr_[:, qb:qb + nq],
                                                          start=False, stop=True),
                             reads=(kr_sb, qr_), writes=(bank,), signal=True)
                        pt = PTr.next()
                        k.op(act, lambda: nc.scalar.activation(out=pt[:, 0:nq], in_=bank.ap[:, 0:nq], func=AF.Exp, scale=SCALE),
                             reads=(bank,), writes=(pt,))
                        return pt
                    pts = [s_mm(0)]
                    for kt in range(nkt):
                        if kt + 1 < nkt:
                            pts.append(s_mm(kt + 1))
                        pt = pts.pop(0)
                        k.op(pe, lambda kt=kt, pt=pt: nc.tensor.matmul(po.ap[:, 0:nq], Vh[:, kt, :], pt[:, 0:nq],
                                                                     start=(kt == 0), stop=(kt == nkt - 1)),
                             reads=(Vh, pt), writes=(po,), signal=False)
                        k.op(pe, lambda kt=kt, pt=pt: nc.tensor.matmul(psm.ap[:, 0:nq], ones_b[:], pt[:, 0:nq],
                                                                     start=(kt == 0), stop=(kt == nkt - 1)),
                             reads=(ones_b, pt), writes=(psm,), signal=True)
                    rec, of, sga = recr.next(), ofr.next(), sgar.next()
                    k.load(sp, sga, sga[:, 0:nq], sgaT[h * 128:(h + 1) * 128, q0 + qb:q0 + qb + nq])
                    k.op(dve, lambda: nc.vector.reciprocal(out=rec[:, 0:nq], in_=psm.ap[:, 0:nq]), reads=(psm,), writes=(rec,))
                    k.op(dve, lambda: nc.vector.tensor_tensor(out=of[:, 0:nq], in0=po.ap[:, 0:nq], in1=rec[:, 0:nq], op=ALU.mult),
                         reads=(po, rec), writes=(of,))
                    st = st_bf.next()
                    k.op(dve, lambda: nc.vector.tensor_tensor(out=st[:, 0:nq], in0=of[:, 0:nq], in1=sga[:, 0:nq], op=ALU.mult),
                         reads=(of, sga), writes=(st,))
                    k.store(sp, st, maT[h * 128:(h + 1) * 128, q0 + qb:q0 + qb + nq], st[:, 0:nq])

        attn_seq(0, NOWN, 0, SKV)
        for i in range(NPS):
            attn_seq(NOWN + 256 * i, 256, SKV + 256 * i, 256)
        k.phase_barrier(allb)

    if STOP_AFTER <= 3:
        k.finish()
        return
    k.dram_barrier()

    with contextlib.ExitStack() as ph:
        def psb(shape, dt, name):
            k.nbuf += 1
            t = ph.enter_context(nc.sbuf_tensor("%s_%d" % (name, k.nbuf), list(shape), dt))
            return Buf(k, t, name)
        allb = []

        def prg(n, shape, dt, name):
            r = Ring([psb(shape, dt, name + str(i)) for i in range(n)])
            allb.extend(r.bufs)
            return r
        s5_ctx = _s5_tile(k, nc, locals()) if k.s5 is not None else None
        Dv = prg(1, [128, 16], F32, "Dv").bufs[0]
        k.load(sp, Dv, Dv[:], s5_d.rearrange("(t p) -> p t", p=128), **nslow)
        u_sb = prg(2, [128, T5], BF16, "u_sb")
        yacc = prg(2, [128, TO], F32, "yacc")
        gt1 = prg(1, [128, TO], F32, "gt1").bufs[0]
        gt2 = prg(1, [128, TO], F32, "gt2").bufs[0]
        yst = prg(1, [128, TO], BF16, "yst")
        def gelu_store(G, ya):
            k.op(act, lambda ya=ya: nc.scalar.activation(out=gt1[:], in_=ya[:], func=AF.Square), reads=(ya,), writes=(gt1,))
            k.op(dve, lambda: nc.vector.tensor_scalar(out=gt1[:], in0=gt1[:], scalar1=0.044715, scalar2=1.0, op0=ALU.mult, op1=ALU.add),
                 reads=(gt1,), writes=(gt1,))
            k.op(dve, lambda ya=ya: nc.vector.tensor_tensor(out=gt2[:], in0=gt1[:], in1=ya[:], op=ALU.mult), reads=(gt1, ya), writes=(gt2,))
            k.op(act, lambda: nc.scalar.activation(out=gt2[:], in_=gt2[:], func=AF.Sigmoid, scale=1.5957691216), reads=(gt2,), writes=(gt2,))
            ys = yst.next()
            k.op(dve, lambda ya=ya, ys=ys: nc.vector.tensor_tensor(out=ys[:], in0=gt2[:], in1=ya[:], op=ALU.mult), reads=(gt2, ya), writes=(ys,))
            k.store(sp, ys, yT[G * 128:(G + 1) * 128, :], ys[:])

        prev = None
        for G in range(16):
            ub = u_sb.next()
            k.load(sp, ub, ub[:], uT[G * 128:(G + 1) * 128, :])
            ya = yacc.next()
            k.op(dve, lambda ub=ub, ya=ya, G=G: nc.vector.tensor_scalar(out=ya[:], in0=ub[:, NOTH:T5], scalar1=Dv[:, G:G + 1],
                                                                      scalar2=None, op0=ALU.mult),
                 reads=(ub, Dv), writes=(ya,))
            for d in range(2 if s5_ctx is not None else 0):
                s5_ctx.stage(1, G, d, ub, ya)
                if prev is not None:
                    s5_ctx.stage(2, *prev)
                    if prev[1] == 1:
                        gelu_store(prev[0], prev[3])
                prev = (G, d, ub, ya)
            if s5_ctx is None:
                gelu_store(G, ya)
        if prev is not None:
            s5_ctx.stage(2, *prev)
            gelu_store(prev[0], prev[3])
        if s5_ctx is not None:
            s5_ctx.finish()
        k.phase_barrier(allb + (s5_ctx.bufs if s5_ctx is not None else []))

    if STOP_AFTER <= 4:
        k.finish()
        return
    k.dram_barrier()

    with contextlib.ExitStack() as ph:
        def psb(shape, dt, name):
            k.nbuf += 1
            t = ph.enter_context(nc.sbuf_tensor("%s_%d" % (name, k.nbuf), list(shape), dt))
            return Buf(k, t, name)
        allb = []

        def prg(n, shape, dt, name):
            r = Ring([psb(shape, dt, name + str(i)) for i in range(n)])
            allb.extend(r.bufs)
            return r
        WR['ring'] = prg(3, [128, 8192], BF16, "wslot")
        yb = prg(1, [128, 16, TB], BF16, "yb").bufs[0]
        zar = prg(4, [128, TB], F32, "za")
        sigr = prg(2, [128, TB], F32, "sig")
        sgbr = prg(2, [128, TB], BF16, "sgb")
        mar = prg(2, [128, TB], BF16, "ma")
        for blk in range(TO // TB):
            o0 = blk * TB
            k.load(sp, yb, yb[:], yT.rearrange("(t p) n -> p t n", p=128)[:, :, o0:o0 + TB])
            tiles = []
            for i in range(0, 32, 2):
                tiles += [(i * 128, 128, ("za", i)), ((i + 1) * 128, 128, ("za", i + 1)),
                          (D + i * 128, 128, ("zb", i)), (D + (i + 1) * 128, 128, ("zb", i + 1))]
            zas = {}

            def cb(bank, M, tag, o0=o0):
                kind, i = tag
                if kind == "za":
                    z = zar.next()
                    k.op(act, lambda: nc.scalar.copy(out=z[:], in_=bank.ap[:, :]), reads=(bank,), writes=(z,))
                    zas[i] = z
                else:
                    z = zas.pop(i)
                    sg = sigr.next()
                    k.op(act, lambda: nc.scalar.activation(out=sg[:], in_=bank.ap[:, :], func=AF.Sigmoid), reads=(bank,), writes=(sg,))
                    sgb, ma = sgbr.next(), mar.next()
                    k.load(sp, sgb, sgb[:], sgbT[i * 128:(i + 1) * 128, o0:o0 + TB])
                    k.load(sp, ma, ma[:], maT[i * 128:(i + 1) * 128, o0:o0 + TB])
                    k.op(dve, lambda: nc.vector.tensor_tensor(out=sg[:], in0=sg[:], in1=z[:], op=ALU.mult), reads=(sg, z), writes=(sg,))
                    k.op(dve, lambda: nc.vector.tensor_tensor(out=sg[:], in0=sg[:], in1=sgb[:], op=ALU.mult), reads=(sg, sgb), writes=(sg,))
                    st = st_bf.next()
                    k.op(dve, lambda: nc.vector.tensor_tensor(out=st[:], in0=sg[:], in1=ma[:], op=ALU.add), reads=(sg, ma), writes=(st,))
                    k.store(sp, st, mT[i * 128:(i + 1) * 128, o0:o0 + TB], st[:])

            gemm(w_glu, 16, tiles, lambda kt: yb[:, kt, :], (yb,), cb, maxw=256)
        k.phase_barrier(allb)

    if STOP_AFTER <= 5:
        k.finish()
        return
    k.dram_barrier()

    with contextlib.ExitStack() as ph:
        def psb(shape, dt, name):
            k.nbuf += 1
            t = ph.enter_context(nc.sbuf_tensor("%s_%d" % (name, k.nbuf), list(shape), dt))
            return Buf(k, t, name)
        allb = []

        def prg(n, shape, dt, name):
            r = Ring([psb(shape, dt, name + str(i)) for i in range(n)])
            allb.extend(r.bufs)
            return r
        WR['ring'] = prg(3, [128, 8192], BF16, "wslot")
        actb = prg(1, [128, 32, TB], BF16, "actb").bufs[0]
        acc = prg(1, [128, 32, TB], F32, "acc").bufs[0]
        Rsb = prg(1, [128, TB], F32, "Rsb5").bufs[0]
        sq = prg(2, [128, TB], F32, "sq5")
        xtr = prg(2, [128, TB], F32, "xt5")
        tfr = prg(2, [128, TB], F32, "tf5")
        ytok = prg(1, [128, D], F32, "ytok").bufs[0]

        def acc_sumsq(i):
            s_ = sq.next()
            sumsq_tile(acc[:, i, :], acc, i == 0, i == 31, s_)

        def resid(j, ggv, srcT, store_to):
            for i in range(32):
                xt = xtr.next()
                k.load(sp, xt, xt[:], srcT[i * 128:(i + 1) * 128, o0:o0 + TB])
                k.op(dve, lambda i=i: nc.vector.scalar_tensor_tensor(out=acc[:, i, :], in0=acc[:, i, :], scalar=ggv[:, j, i:i + 1],
                                                                   in1=Rsb[:], op0=ALU.mult, op1=ALU.mult),
                     reads=(acc, ggv, Rsb), writes=(acc,))
                k.op(dve, lambda i=i, xt=xt: nc.vector.tensor_tensor(out=acc[:, i, :], in0=acc[:, i, :], in1=xt[:], op=ALU.add),
                     reads=(acc, xt), writes=(acc,))
                if store_to is not None:
                    k.store(sp, acc, store_to[i * 128:(i + 1) * 128, o0:o0 + TB], acc[:, i, :])

        for blk in range(TO // TB):
            o0 = blk * TB
            j = 0 if o0 < NOWN else 1
            k.load(sp, actb, actb[:], mT.rearrange("(t p) n -> p t n", p=128)[:, :, o0:o0 + TB])

            def cb_out(bank, M, tag):
                i = tag
                k.op(act, lambda: nc.scalar.copy(out=acc[:, i, :], in_=bank.ap[:, :]), reads=(bank,), writes=(acc,) if i == 0 else ())
            gemm(w_out, 32, [(i * 128, 128, i) for i in range(32)], lambda kt: actb[:, kt, :], (actb,), cb_out)
            acc.w = {act.key: Tok(act.key, act.sem, act.n)}
            for i in range(32):
                acc_sumsq(i)
            rstd_from_bank(PR, D, Rsb)
            resid(j, sc_gg_mix, xT, x1T)
            for i in range(32):
                acc_sumsq(i)
            rstd_from_bank(PR, D, Rsb)
            for i in range(32):
                tf = tfr.next()
                k.op(dve, lambda i=i, tf=tf: nc.vector.tensor_tensor(out=tf[:], in0=acc[:, i, :], in1=Rsb[:], op=ALU.mult),
                     reads=(acc, Rsb), writes=(tf,))
                k.op(act, lambda i=i, tf=tf, j=j: nc.scalar.activation(out=actb[:, i, :], in_=tf[:], func=AF.Identity,
                                                                      bias=sc_sh_mlp[:, j, i:i + 1], scale=sc_a_mlp[:, j, i:i + 1]),
                     reads=(tf, sc_sh_mlp, sc_a_mlp), writes=(actb,) if i == 0 else ())
            actb.w = {act.key: Tok(act.key, act.sem, act.n)}

            def cb_ff1(bank, M, tag, o0=o0):
                i = tag
                tf = tfr.next()
                k.op(act, lambda: nc.scalar.activation(out=tf[:], in_=bank.ap[:, :], func=AF.Relu), reads=(bank,), writes=(tf,))
                st = st_bf.next()
                k.op(dve, lambda: nc.vector.tensor_tensor(out=st[:], in0=tf[:], in1=tf[:], op=ALU.mult), reads=(tf,), writes=(st,))
                k.store(sp, st, a1T[i * 128:(i + 1) * 128, o0:o0 + TB], st[:])
            gemm(w_ff1, 32, [(i * 128, 128, i) for i in range(DFF // 128)], lambda kt: actb[:, kt, :], (actb,), cb_ff1)
            k.dram_barrier()
            for kc in range(4):
                k.load(sp, actb, actb[:], a1T.rearrange("(t p) n -> p t n", p=128)[:, kc * 32:(kc + 1) * 32, o0:o0 + TB])

                def cb_ff2(bank, M, tag, kc=kc):
                    i = tag
                    if kc == 0:
                        k.op(act, lambda: nc.scalar.copy(out=acc[:, i, :], in_=bank.ap[:, :]), reads=(bank,), writes=(acc,))
                    else:
                        k.op(dve, lambda: nc.vector.tensor_tensor(out=acc[:, i, :], in0=acc[:, i, :], in1=bank.ap[:, :], op=ALU.add),
                             reads=(acc, bank), writes=(acc,))
                gemm(w_ff2[kc * D:(kc + 1) * D, :], 32, [(i * 128, 128, i) for i in range(32)], lambda kt: actb[:, kt, :], (actb,), cb_ff2)
            for i in range(32):
                acc_sumsq(i)
            rstd_from_bank(PR, D, Rsb)
            k.dram_barrier()
            resid(j, sc_gg_mlp, x1T, None)
            for tt in range(4):
                for i4 in range(8):
                    bank = PM.next()
                    for ii in range(4):
                        i = i4 * 4 + ii
                        k.op(pe, lambda i=i, ii=ii, tt=tt, bank=bank: nc.tensor.transpose(
                            bank.ap[:, ii * 128:(ii + 1) * 128], acc[:, i, tt * 128:(tt + 1) * 128], ident[:]),
                            reads=(acc, ident), writes=(bank,), signal=(ii == 3))
                    k.op(act, lambda i4=i4, bank=bank: nc.scalar.copy(out=ytok[:, i4 * 512:(i4 + 1) * 512], in_=bank.ap[:, :]),
                         reads=(bank,), writes=(ytok,))
                k.store(sp, ytok, y_own[o0 + tt * 128:o0 + (tt + 1) * 128, :], ytok[:])
        k.phase_barrier(allb)

    k.finish()


def _rope_tables(pos):
    rows = 4096 // 64
    n_freq = 16
    inv_freq = (10000.0 ** (-np.arange(n_freq, dtype=np.float32) / n_freq)).astype(np.float32)
    row = (pos // 64).astype(np.float32)
    col = (pos % 64).astype(np.float32)
    ang = np.concatenate([row[:, None] * inv_freq, col[:, None] * inv_freq], axis=-1)
    ang = np.concatenate([ang, ang], axis=-1)
    return np.cos(ang).astype(np.float32), np.sin(ang).astype(np.float32)


def core_inputs(inp, core):
    b, half = core // 2, core % 2
    xs = np.asarray(inp["x_sample"][b])
    pos = np.arange(4096)
    if half == 1:
        order = pos
    else:
        order = pos[::-1]
    xp = np.asarray(inp["x_prompt"][4 * core:4 * core + 4])
    if half == 0:
        xp = xp[:, ::-1]
    x_all = np.concatenate([xs[order], xp.reshape(NPR, D)], axis=0)
    cos, sin = _rope_tables(order)
    perm = np.zeros((64, 64), np.float32)
    for m in range(32):
        perm[m + 32, m] = -1.0
        perm[m, m + 32] = 1.0
    sq = lambda a: np.ascontiguousarray(np.asarray(a)[0])
    d = {
        "x_all": np.ascontiguousarray(x_all),
        "cvec": np.ascontiguousarray(np.stack([np.asarray(inp["c"][b]), np.asarray(inp["c_ctx"])])),
        "ckv_ctx": sq(inp["cache_ckv"][b]),
        "krope_ctx": sq(inp["cache_krope"][b]),
        "cosT": np.ascontiguousarray(cos.T), "sinT": np.ascontiguousarray(sin.T),
        "cident": np.eye(128, dtype=np.float32), "cperm": perm,
        "w_mod": sq(inp["w_mod"]), "b_mod": np.asarray(inp["b_mod"]).reshape(1, -1),
        "g_pre_mix": sq(inp["g_pre_mix"]), "w_in": sq(inp["w_in"]), "g_q_lat": sq(inp["g_q_lat"]),
        "g_kv_lat": sq(inp["g_kv_lat"]), "w_uq": sq(inp["w_uq"]), "w_ukv": sq(inp["w_ukv"]),
        "w_glu": sq(inp["w_glu"]), "w_out": sq(inp["w_out"]), "g_post_mix": sq(inp["g_post_mix"]),
        "g_pre_mlp": sq(inp["g_pre_mlp"]), "w_ff1": sq(inp["w_ff1"]), "w_ff2": sq(inp["w_ff2"]),
        "g_post_mlp": sq(inp["g_post_mlp"]), "s5_d": sq(inp["s5_d"]),
    }
    dsel = [0, 1] if half == 1 else [1, 0]
    for nm in ("s5_lam_re", "s5_lam_im", "s5_log_dt", "s5_b_re", "s5_b_im", "s5_c_re", "s5_c_im"):
        d[nm] = np.ascontiguousarray(np.asarray(inp[nm])[0][dsel])
    d["s5_h0"] = np.ascontiguousarray(np.asarray(inp["state_s5"])[b, 0][dsel])
    return d


def kernel(**inputs):
    nc = build()
    in_maps = [core_inputs(inputs, c) for c in range(8)]
    res = run_bass_kernel_spmd(nc, in_maps, core_ids=list(range(8)))
    R = res.results
    y_prompt = np.zeros((32, 256, D), np.float32)
    y_sample = np.zeros((4, 4096, D), np.float32)
    n_ckv = np.zeros((32, 1, 256, NKV), np.float32)
    n_kr = np.zeros((32, 1, 256, NR), np.float32)
    n_s5 = np.zeros((32, 1, 2, 2, 128, 64), np.float32)
    for c in range(8):
        b, half = c // 2, c % 2
        r = R[c]
        yo = np.asarray(r["y_own"])
        ys, yp = yo[:NOWN], yo[NOWN:].reshape(4, 256, D)
        ck = np.asarray(r["ckv_new"]).reshape(4, 256, NKV)
        kr = np.asarray(r["krope_new"]).reshape(4, 256, NR)
        s5 = np.asarray(r["s5_new"]).reshape(4, 2, 2, 128, 64)
        if half == 0:
            y_sample[b, :2048] = ys[::-1]
            yp, ck, kr = yp[:, ::-1], ck[:, ::-1], kr[:, ::-1]
            s5 = s5[:, ::-1]
        else:
            y_sample[b, 2048:] = ys
        y_prompt[4 * c:4 * c + 4] = yp
        n_ckv[4 * c:4 * c + 4, 0] = ck
        n_kr[4 * c:4 * c + 4, 0] = kr
        n_s5[4 * c:4 * c + 4, 0] = s5
    return (y_prompt, y_sample, n_ckv, n_kr, n_s5)


TC = 16


class _S5:
    def __init__(self, k, nc, env):
        self.k, self.nc, self.env = k, nc, env
        self.bufs = []

    def sb(self, shape, dt, name):
        b = self.env["psb"](shape, dt, name)
        self.bufs.append(b)
        return b

    def alloc_persist(self):
        self.Akr = self.sb([128, 2, 9, 64], F32, "Akr")
        self.Aki = self.sb([128, 2, 9, 64], F32, "Aki")
        self.Akn = self.sb([128, 2, 9, 64], F32, "Akn")
        self.h0 = self.sb([128, 2, 2, 64], F32, "h0")
        self.Hfin = self.sb([128, max(1, NPR // 256), 2, 2, 64], F32, "Hfin")

    def pre(self, P):
        k, nc = self.k, self.nc
        dve, act, pe, sp = k.dve, k.act, k.pe, k.sp
        nslow = dict(allow_slow_non_contiguous=True)
        ident, PS = self.env["ident"], self.env["PS"]
        PM = self.env["PM"]
        V = nc.vector
        with contextlib.ExitStack() as ph:
            tmpb = []

            def t(shape, dt, name):
                k.nbuf += 1
                tt = ph.enter_context(nc.sbuf_tensor("%s_%d" % (name, k.nbuf), list(shape), dt))
                b = Buf(k, tt, name)
                tmpb.append(b)
                return b

            def s64(name):
                return t([128, 64], F32, name)
            lr, li, ldt = s64("lr"), s64("li"), s64("ldt")
            dt_, er, th, kf, r_, x_, x2 = (s64(n) for n in ("dt", "er", "th", "kf", "r", "x", "x2"))
            ki = t([128, 64], mybir.dt.int32, "ki")
            sn, cs, cc, ss_, sc_, tm = (s64(n) for n in ("sn", "cs", "cc", "ss", "sc", "tm"))
            ar, ai, den, am1, wr, wi, t1, t2 = (s64(n) for n in ("ar", "ai", "den", "am1", "wr", "wi", "t1", "t2"))
            pr = t([128, 17, 64], F32, "pr")
            pi_ = t([128, 17, 64], F32, "pi")
            Bre, Bim, Cre, Cim = (t([128, 64, 16], F32, n) for n in ("Bre", "Bim", "Cre", "Cim"))
            Bbr, Bbi, Xr, Xi, T1, T2 = (t([128, 64, 16], F32, n) for n in ("Bbr", "Bbi", "Xr", "Xi", "T1", "T2"))
            XbdFr, XbdFi = t([128, 64, 32], F32, "XbdFr"), t([128, 64, 32], F32, "XbdFi")
            Xbdr, Xbdi = t([128, 64, 32], BF16, "Xbdr"), t([128, 64, 32], BF16, "Xbdi")
            Ybdr, Ybdi = t([128, 64, 32], BF16, "Ybdr"), t([128, 64, 32], BF16, "Ybdi")
            Cbdr, Cbdn = t([128, 64, 32], BF16, "Cbdr"), t([128, 64, 32], BF16, "Cbdn")
            kst = Ring([t([128, 128], BF16, "kst%d" % i) for i in range(3)])
            ctr = Ring([t([128, 128], F32, "ctile%d" % i) for i in range(2)])

            def tt_(out, a, b, op, rd, wrb):
                k.op(dve, lambda: V.tensor_tensor(out=out, in0=a, in1=b, op=op), reads=rd, writes=wrb)

            def ts_(out, a, s1, s2, op0, op1, rd, wrb):
                if op1 is None:
                    k.op(dve, lambda: V.tensor_scalar(out=out, in0=a, scalar1=s1, scalar2=None, op0=op0), reads=rd, writes=wrb)
                else:
                    k.op(dve, lambda: V.tensor_scalar(out=out, in0=a, scalar1=s1, scalar2=s2, op0=op0, op1=op1), reads=rd, writes=wrb)

            def cmul(orr, oi, ar_, ai_, br_, bi_, rd, wr_bufs, tA, tB):
                tt_(tA.ap[:], ar_, br_, ALU.mult, rd, (tA,))
                tt_(tB.ap[:], ai_, bi_, ALU.mult, rd, (tB,))
                tt_(orr, tA.ap[:], tB.ap[:], ALU.subtract, (tA, tB), wr_bufs[0:1])
                tt_(tA.ap[:], ar_, bi_, ALU.mult, rd, (tA,))
                tt_(tB.ap[:], ai_, br_, ALU.mult, rd, (tB,))
                tt_(oi, tA.ap[:], tB.ap[:], ALU.add, (tA, tB), wr_bufs[1:2])

            for d in range(2):
                for g2 in range(2):
                    ps_ = slice(g2 * 64, (g2 + 1) * 64)
                    ld = k.load if g2 == 0 else k.loadp
                    ld(sp, lr, lr.ap[ps_, :], P["lam_re"][d].rearrange("(q g) p -> g p q", g=2)[g2], **nslow)
                    ld(sp, li, li.ap[ps_, :], P["lam_im"][d].rearrange("(q g) p -> g p q", g=2)[g2], **nslow)
                    ld(sp, ldt, ldt.ap[ps_, :], P["log_dt"][d].rearrange("(q g) -> g q", g=2)[g2].partition_broadcast(64), **nslow)
                    ld(sp, Bre, Bre.ap[ps_, :, :], P["b_re"][d].rearrange("(q g) p j -> g p q j", g=2)[g2], **nslow)
                    ld(sp, Bim, Bim.ap[ps_, :, :], P["b_im"][d].rearrange("(q g) p j -> g p q j", g=2)[g2], **nslow)
                    for ri in range(2):
                        (k.load if (g2 == 0 and ri == 0 and d == 0) else k.loadp)(
                            sp, self.h0, self.h0.ap[ps_, d, ri, :], P["h0"][d, ri].rearrange("(q g) p -> g p q", g=2)[g2], **nslow)
                for (dstC, srcC) in ((Cre, P["c_re"]), (Cim, P["c_im"])):
                    for q8 in range(8):
                        ctile = ctr.next()
                        for g2 in range(2):
                            (k.load if g2 == 0 else k.loadp)(
                                sp, ctile, ctile.ap[:, g2 * 64:(g2 + 1) * 64],
                                srcC[d].rearrange("(q g) j p -> g q j p", g=2)[g2][q8 * 8:(q8 + 1) * 8])
                        bank = PM.next()
                        k.op(pe, lambda bank=bank, ctile=ctile: nc.tensor.transpose(bank.ap[:, 0:128], ctile.ap[:], ident[:]),
                             reads=(ctile, ident), writes=(bank,))
                        k.op(act, lambda bank=bank, dstC=dstC, q8=q8: nc.scalar.copy(
                            out=dstC.ap[:, q8 * 8:(q8 + 1) * 8, :], in_=bank.ap[:, 0:128].rearrange("p (a b) -> p a b", b=16)),
                            reads=(bank,), writes=(dstC,))
                yield
                k.op(act, lambda: nc.scalar.activation(out=dt_.ap[:], in_=ldt.ap[:], func=AF.Exp), reads=(ldt,), writes=(dt_,))
                tt_(t1.ap[:], lr.ap[:], dt_.ap[:], ALU.mult, (lr, dt_), (t1,))
                k.op(act, lambda: nc.scalar.activation(out=er.ap[:], in_=t1.ap[:], func=AF.Exp), reads=(t1,), writes=(er,))
                tt_(th.ap[:], li.ap[:], dt_.ap[:], ALU.mult, (li, dt_), (th,))
                ts_(kf.ap[:], th.ap[:], 1.0 / (2 * math.pi), None, ALU.mult, None, (th,), (kf,))
                k.op(dve, lambda: V.tensor_copy(out=ki.ap[:], in_=kf.ap[:]), reads=(kf,), writes=(ki,))
                k.op(dve, lambda: V.tensor_copy(out=kf.ap[:], in_=ki.ap[:]), reads=(ki,), writes=(kf,))
                k.op(dve, lambda: V.scalar_tensor_tensor(out=r_.ap[:], in0=kf.ap[:], scalar=-2 * math.pi, in1=th.ap[:],
                                                         op0=ALU.mult, op1=ALU.add), reads=(kf, th), writes=(r_,))
                ts_(x_.ap[:], r_.ap[:], 1.0 / 32, None, ALU.mult, None, (r_,), (x_,))
                tt_(x2.ap[:], x_.ap[:], x_.ap[:], ALU.mult, (x_,), (x2,))
                ts_(sn.ap[:], x2.ap[:], -1.0 / 42, 1.0, ALU.mult, ALU.add, (x2,), (sn,))
                for cdiv in (20.0, 6.0):
                    tt_(tm.ap[:], x2.ap[:], sn.ap[:], ALU.mult, (x2, sn), (tm,))
                    ts_(sn.ap[:], tm.ap[:], -1.0 / cdiv, 1.0, ALU.mult, ALU.add, (tm,), (sn,))
                tt_(sn.ap[:], sn.ap[:], x_.ap[:], ALU.mult, (sn, x_), (sn,))
                ts_(cs.ap[:], x2.ap[:], -1.0 / 56, 1.0, ALU.mult, ALU.add, (x2,), (cs,))
                for cdiv in (30.0, 12.0, 2.0):
                    tt_(tm.ap[:], x2.ap[:], cs.ap[:], ALU.mult, (x2, cs), (tm,))
                    ts_(cs.ap[:], tm.ap[:], -1.0 / cdiv, 1.0, ALU.mult, ALU.add, (tm,), (cs,))
                for _ in range(5):
                    tt_(cc.ap[:], cs.ap[:], cs.ap[:], ALU.mult, (cs,), (cc,))
                    tt_(ss_.ap[:], sn.ap[:], sn.ap[:], ALU.mult, (sn,), (ss_,))
                    tt_(sc_.ap[:], sn.ap[:], cs.ap[:], ALU.mult, (sn, cs), (sc_,))
                    tt_(cs.ap[:], cc.ap[:], ss_.ap[:], ALU.subtract, (cc, ss_), (cs,))
                    ts_(sn.ap[:], sc_.ap[:], 2.0, None, ALU.mult, None, (sc_,), (sn,))
                tt_(ar.ap[:], er.ap[:], cs.ap[:], ALU.mult, (er, cs), (ar,))
                tt_(ai.ap[:], er.ap[:], sn.ap[:], ALU.mult, (er, sn), (ai,))
                yield
                tt_(t1.ap[:], lr.ap[:], lr.ap[:], ALU.mult, (lr,), (t1,))
                tt_(t2.ap[:], li.ap[:], li.ap[:], ALU.mult, (li,), (t2,))
                tt_(den.ap[:], t1.ap[:], t2.ap[:], ALU.add, (t1, t2), (den,))
                k.op(dve, lambda: V.reciprocal(out=den.ap[:], in_=den.ap[:]), reads=(den,), writes=(den,))
                ts_(am1.ap[:], ar.ap[:], -1.0, None, ALU.add, None, (ar,), (am1,))
                tt_(t1.ap[:], am1.ap[:], lr.ap[:], ALU.mult, (am1, lr), (t1,))
                tt_(t2.ap[:], ai.ap[:], li.ap[:], ALU.mult, (ai, li), (t2,))
                tt_(wr.ap[:], t1.ap[:], t2.ap[:], ALU.add, (t1, t2), (wr,))
                tt_(wr.ap[:], wr.ap[:], den.ap[:], ALU.mult, (wr, den), (wr,))
                tt_(t1.ap[:], ai.ap[:], lr.ap[:], ALU.mult, (ai, lr), (t1,))
                tt_(t2.ap[:], am1.ap[:], li.ap[:], ALU.mult, (am1, li), (t2,))
                tt_(wi.ap[:], t1.ap[:], t2.ap[:], ALU.subtract, (t1, t2), (wi,))
                tt_(wi.ap[:], wi.ap[:], den.ap[:], ALU.mult, (wi, den), (wi,))

                def bc(b):
                    return b.ap[:, :].unsqueeze(2).to_broadcast([128, 64, 16])
                cmul(Bbr.ap[:], Bbi.ap[:], bc(wr), bc(wi), Bre.ap[:], Bim.ap[:], (wr, wi, Bre, Bim), (Bbr, Bbi), T1, T2)
                yield
                k.op(dve, lambda: V.memset(pr.ap[:, 0, :], 1.0), writes=(pr,))
                k.op(dve, lambda: V.memset(pi_.ap[:, 0, :], 0.0), writes=(pi_,))
                for n in range(1, 17):
                    cmul(pr.ap[:, n, :], pi_.ap[:, n, :], pr.ap[:, n - 1, :], pi_.ap[:, n - 1, :], ar.ap[:], ai.ap[:],
                         (pr, pi_, ar, ai), (pr, pi_), cc, ss_)
                yield
                Akr, Aki, Akn = self.Akr, self.Aki, self.Akn
                k.op(dve, lambda: V.tensor_copy(out=Akr.ap[:, d, 0, :], in_=pr.ap[:, 16, :]), reads=(pr,), writes=(Akr,))
                k.op(dve, lambda: V.tensor_copy(out=Aki.ap[:, d, 0, :], in_=pi_.ap[:, 16, :]), reads=(pi_,), writes=(Aki,))
                for kk in range(1, 9):
                    cmul(Akr.ap[:, d, kk, :], Aki.ap[:, d, kk, :], Akr.ap[:, d, kk - 1, :], Aki.ap[:, d, kk - 1, :],
                         Akr.ap[:, d, kk - 1, :], Aki.ap[:, d, kk - 1, :], (Akr, Aki), (Akr, Aki), cc, ss_)
                ts_(Akn.ap[:, d, :, :], Aki.ap[:, d, :, :], -1.0, None, ALU.mult, None, (Aki,), (Akn,))
                for (bd, src, sgn) in ((Cbdr, Cre, 1.0), (Cbdn, Cim, -1.0)):
                    k.op(dve, lambda bd=bd: V.memset(bd.ap[:], 0.0), writes=(bd,))
                    for g2 in range(2):
                        ps_ = slice(g2 * 64, (g2 + 1) * 64)
                        ts_(bd.ap[ps_, :, g2 * 16:(g2 + 1) * 16], src.ap[ps_, :, :], sgn, None, ALU.mult, None, (src,), (bd,))
                for bd in (XbdFr, XbdFi, Ybdr, Ybdi):
                    k.op(dve, lambda bd=bd: V.memset(bd.ap[:], 0.0), writes=(bd,))
                for n in range(17):
                    prn = pr.ap[:, n, :].unsqueeze(2).to_broadcast([128, 64, 16])
                    pin = pi_.ap[:, n, :].unsqueeze(2).to_broadcast([128, 64, 16])
                    if n <= 15:
                        cmul(Xr.ap[:], Xi.ap[:], prn, pin, Bbr.ap[:], Bbi.ap[:], (pr, pi_, Bbr, Bbi), (Xr, Xi), T1, T2)
                        for (bdF, bd, src) in ((XbdFr, Xbdr, Xr), (XbdFi, Xbdi, Xi)):
                            for g2 in range(2):
                                ps_ = slice(g2 * 64, (g2 + 1) * 64)
                                k.op(dve, lambda bdF=bdF, src=src, ps_=ps_, g2=g2: V.tensor_copy(
                                    out=bdF.ap[ps_, :, g2 * 16:(g2 + 1) * 16], in_=src.ap[ps_, :, :]), reads=(src,), writes=(bdF,))
                            k.op(dve, lambda bdF=bdF, bd=bd: V.tensor_copy(out=bd.ap[:], in_=bdF.ap[:]), reads=(bdF,), writes=(bd,))
                        for G in range(16):
                            bank = PM.next()
                            k.op(dve, lambda bank=bank: V.memset(bank.ap[:, 0:128], 0.0), writes=(bank,))
                            for q in range(4):
                                Q = 4 * G + q
                                k.op(pe, lambda bank=bank, q=q, Q=Q: nc.tensor.matmul(
                                    bank.ap[32 * q:32 * q + 32, 32 * q:32 * q + 32], Xbdr.ap[:, Q, :], Cbdr.ap[:, Q, :],
                                    start=True, stop=False, tile_position=(0, 32 * q)),
                                    reads=(Xbdr, Cbdr), writes=(bank,), signal=False)
                                k.op(pe, lambda bank=bank, q=q, Q=Q: nc.tensor.matmul(
                                    bank.ap[32 * q:32 * q + 32, 32 * q:32 * q + 32], Xbdi.ap[:, Q, :], Cbdn.ap[:, Q, :],
                                    start=False, stop=True, tile_position=(0, 32 * q)),
                                    reads=(Xbdi, Cbdn), writes=(bank,), signal=(q == 3))
                            st = kst.next()
                            k.op(act, lambda st=st, bank=bank: nc.scalar.copy(out=st.ap[:], in_=bank.ap[:, 0:128]), reads=(bank,), writes=(st,))
                            k.store(sp, st, P["Kst"][d, G, n], st.ap[:])
                            yield
                            for ri, bdF in enumerate((XbdFr, XbdFi)):
                                bank = PM.next()
                                k.op(pe, lambda bank=bank, bdF=bdF, G=G: nc.tensor.transpose(
                                    bank.ap[:, 0:128], bdF.ap[:, 4 * G:4 * G + 4, :].rearrange("p a b -> p (a b)"), ident[:]),
                                    reads=(bdF, ident), writes=(bank,))
                                st = kst.next()
                                k.op(act, lambda st=st, bank=bank: nc.scalar.copy(out=st.ap[:], in_=bank.ap[:, 0:128]), reads=(bank,), writes=(st,))
                                k.store(sp, st, P["M1"][d, G, n, ri], st.ap[:])
                    if n >= 1:
                        cmul(Xr.ap[:], Xi.ap[:], prn, pin, Cre.ap[:], Cim.ap[:], (pr, pi_, Cre, Cim), (Xr, Xi), T1, T2)
                        for g2 in range(2):
                            ps_ = slice(g2 * 64, (g2 + 1) * 64)
                            k.op(dve, lambda ps_=ps_, g2=g2: V.tensor_copy(out=Ybdr.ap[ps_, :, g2 * 16:(g2 + 1) * 16], in_=Xr.ap[ps_, :, :]),
                                 reads=(Xr,), writes=(Ybdr,))
                            ts_(Ybdi.ap[ps_, :, g2 * 16:(g2 + 1) * 16], Xi.ap[ps_, :, :], -1.0, None, ALU.mult, None, (Xi,), (Ybdi,))
                        k.store(sp, Ybdr, P["M2"][d, n, 0], Ybdr.ap[:].rearrange("p a b -> p (a b)"))
                        k.store(sp, Ybdi, P["M2"][d, n, 1], Ybdi.ap[:].rearrange("p a b -> p (a b)"))
                        yield
            k.phase_barrier(tmpb)
        yield

    def setup_main(self, P):
        k, nc = self.k, self.nc
        self.P = P
        S = NOTH + NOWN
        self.NPS = NPR // 256
        self.Kst_sb2 = [self.sb([128, 16, 128], BF16, "Kst_sb") for _ in range(2)]
        self.M1_sb = self.sb([128, 16, 2, 128], BF16, "M1_sb")
        self.M2_sb2 = [self.sb([128, 16, 2, 128], BF16, "M2_sb") for _ in range(2)]
        nsmax = S // TC + 1
        self.Zs = [[[self.sb([128, nsmax], F32, "Zs") for _ in range(2)] for _ in range(2)] for _ in range(4)]
        self.Zp = [[[self.sb([128, self.NPS, 17], F32, "Zp") for _ in range(2)] for _ in range(2)] for _ in range(4)]
        self.Hp2 = [[[self.sb([128, TO // TC], BF16, "Hp") for _ in range(2)] for _ in range(4)] for _ in range(2)]

    def stage(self, which, G, d, ub, ya):
        k, nc = self.k, self.nc
        dve, act, pe, sp, pool = k.dve, k.act, k.pe, k.sp, k.pool
        V = nc.vector
        P = self.P
        gbank, PM = self.env["gbank"], self.env["PM"]
        S = NOTH + NOWN
        NPS = self.NPS
        NT = TO // TC
        par = d
        Kst_sb, M1_sb, M2_sb = self.Kst_sb2[par], self.M1_sb, self.M2_sb2[par]
        Hp = self.Hp2[par]
        if True:
            fwd = (d == 0)
            tokA0 = 0 if fwd else NOTH
            nS = (S - tokA0) // TC
            NA = (T5 - tokA0) // TC
            if which == 1:
                k.load(sp, M1_sb, M1_sb.ap[:], P["M1"][d, G].rearrange("n r p c -> p n r c"))
                k.load(sp, Kst_sb, Kst_sb.ap[:], P["Kst"][d, G].rearrange("n p c -> p n c"))
                k.load(sp, M2_sb, M2_sb.ap[:], P["M2"][d, 1:17, :, :, G * 128:(G + 1) * 128].rearrange("n r p c -> p n r c"))
            uall = ub.ap[:, tokA0:T5].rearrange("p (c s) -> p c s", s=TC)
            uown = ub.ap[:, NOTH:T5].rearrange("p (c s) -> p c s", s=TC)
            for q in (range(4) if which == 1 else []):
                Q = 4 * G + q
                rows = slice(32 * q, 32 * q + 32)
                for ri in range(2):
                    bank = gbank.next()
                    for s in range(TC):
                        n = (TC - 1 - s) if fwd else s
                        k.op(pe, lambda s=s, n=n, bank=bank, ri=ri: nc.tensor.matmul(
                            bank.ap[:, 0:NA], M1_sb.ap[rows, n, ri, :], uall[rows, :, s],
                            start=(s == 0), stop=(s == TC - 1), tile_position=(32 * q, 0)),
                            reads=(M1_sb, ub), writes=(bank,), signal=(s == TC - 1))
                    zs, zp = self.Zs[q][0][ri], self.Zp[q][0][ri]
                    so = 1 if fwd else 0
                    k.op(act, lambda bank=bank, zs=zs, so=so: nc.scalar.copy(out=zs.ap[:, so:so + nS], in_=bank.ap[:, 0:nS]),
                         reads=(bank,), writes=(zs,))
                    hpos = 0 if fwd else nS
                    k.op(dve, lambda zs=zs, hpos=hpos, ri=ri, Q=Q: V.tensor_copy(out=zs.ap[:, hpos:hpos + 1], in_=self.h0.ap[:, d, ri, Q:Q + 1]),
                         reads=(self.h0,), writes=(zs,))
                    if NPS:
                        k.op(act, lambda bank=bank, zp=zp, so=so: nc.scalar.copy(
                            out=zp.ap[:, :, so:so + 16], in_=bank.ap[:, nS:nS + NPS * 16].rearrange("p (a b) -> p a b", b=16)),
                            reads=(bank,), writes=(zp,))
                        hp_ = 0 if fwd else 16
                        k.op(dve, lambda zp=zp, hp_=hp_: V.memset(zp.ap[:, :, hp_:hp_ + 1], 0.0), writes=(zp,))

                def scan(Z, L, nd3):
                    pp, kk, sh = 0, 0, 1
                    while sh < L:
                        src, dst = Z[pp], Z[1 - pp]
                        Ar, Ai, An = (self.Akr.ap[:, d, kk, Q:Q + 1], self.Aki.ap[:, d, kk, Q:Q + 1], self.Akn.ap[:, d, kk, Q:Q + 1])

                        def v(b, lo, hi):
                            return b.ap[:, :, lo:hi] if nd3 else b.ap[:, lo:hi]
                        if fwd:
                            o_lo, o_hi, i_lo, i_hi, c_lo, c_hi = sh, L, 0, L - sh, 0, sh
                        else:
                            o_lo, o_hi, i_lo, i_hi, c_lo, c_hi = 0, L - sh, sh, L, L - sh, L
                        rd = (src[0], src[1], self.Akr, self.Aki, self.Akn)
                        k.op(dve, lambda: V.scalar_tensor_tensor(out=v(dst[0], o_lo, o_hi), in0=v(src[0], i_lo, i_hi), scalar=Ar,
                                                                 in1=v(src[0], o_lo, o_hi), op0=ALU.mult, op1=ALU.add), reads=rd, writes=(dst[0],))
                        k.op(dve, lambda: V.scalar_tensor_tensor(out=v(dst[0], o_lo, o_hi), in0=v(src[1], i_lo, i_hi), scalar=An,
                                                                 in1=v(dst[0], o_lo, o_hi), op0=ALU.mult, op1=ALU.add), reads=rd + (dst[0],), writes=(dst[0],))
                        k.op(dve, lambda: V.scalar_tensor_tensor(out=v(dst[1], o_lo, o_hi), in0=v(src[1], i_lo, i_hi), scalar=Ar,
                                                                 in1=v(src[1], o_lo, o_hi), op0=ALU.mult, op1=ALU.add), reads=rd, writes=(dst[1],))
                        k.op(dve, lambda: V.scalar_tensor_tensor(out=v(dst[1], o_lo, o_hi), in0=v(src[0], i_lo, i_hi), scalar=Ai,
                                                                 in1=v(dst[1], o_lo, o_hi), op0=ALU.mult, op1=ALU.add), reads=rd + (dst[1],), writes=(dst[1],))
                        for ri in range(2):
                            k.op(act, lambda ri=ri: nc.scalar.copy(out=v(dst[ri], c_lo, c_hi), in_=v(src[ri], c_lo, c_hi)),
                                 reads=(src[ri],), writes=(dst[ri],))
                        pp, kk, sh = 1 - pp, kk + 1, sh * 2
                    return pp
                pS = scan(self.Zs[q], nS + 1, False)
                Rs = self.Zs[q][pS]
                if fwd:
                    s_lo = NOTH // TC
                else:
                    s_lo = 1
                for ri in range(2):
                    k.op(act, lambda ri=ri, Rs=Rs, s_lo=s_lo: nc.scalar.copy(out=Hp[q][ri].ap[:, 0:NOWN // TC],
                                                                             in_=Rs[ri].ap[:, s_lo:s_lo + NOWN // TC]),
                         reads=(Rs[ri],), writes=(Hp[q][ri],))
                if NPS:
                    pP = scan(self.Zp[q], 17, True)
                    Rp = self.Zp[q][pP]
                    p_lo = 0 if fwd else 1
                    fin = 16 if fwd else 0
                    for ri in range(2):
                        k.op(act, lambda ri=ri, Rp=Rp, p_lo=p_lo: nc.scalar.copy(
                            out=Hp[q][ri].ap[:, NOWN // TC:NT].rearrange("p (a b) -> p a b", b=16), in_=Rp[ri].ap[:, :, p_lo:p_lo + 16]),
                            reads=(Rp[ri],), writes=(Hp[q][ri],))
                        k.op(dve, lambda ri=ri, Rp=Rp, fin=fin, Q=Q: V.tensor_copy(out=self.Hfin.ap[:, :, d, ri, Q:Q + 1], in_=Rp[ri].ap[:, :, fin:fin + 1]),
                             reads=(Rp[ri],), writes=(self.Hfin,))
            if which == 1:
                return
            yav = ya.ap[:, :].rearrange("p (c s) -> p c s", s=TC)
            for so_ in range(TC):
                bank = gbank.next()
                srcs = list(range(0, so_ + 1)) if fwd else list(range(so_, TC))
                for ii, sp_ in enumerate(srcs):
                    lag = abs(so_ - sp_)
                    k.op(pe, lambda ii=ii, sp_=sp_, lag=lag, bank=bank: nc.tensor.matmul(
                        bank.ap[:, 0:NT], Kst_sb.ap[:, lag, :], uown[:, :, sp_], start=(ii == 0), stop=False),
                        reads=(Kst_sb, ub), writes=(bank,), signal=False)
                n = (so_ + 1) if fwd else (TC - so_)
                for q in range(4):
                    for ri in range(2):
                        last = (q == 3 and ri == 1)
                        k.op(pe, lambda q=q, ri=ri, n=n, bank=bank, last=last: nc.tensor.matmul(
                            bank.ap[32 * q:32 * q + 32, 0:NT], M2_sb.ap[:, n - 1, ri, 32 * q:32 * q + 32], Hp[q][ri].ap[:, :],
                            start=False, stop=(ri == 1), tile_position=(0, 32 * q)),
                            reads=(M2_sb, Hp[q][ri]), writes=(bank,), signal=last)
                k.op(dve, lambda so_=so_, bank=bank: V.tensor_tensor(out=yav[:, :, so_], in0=yav[:, :, so_], in1=bank.ap[:, 0:NT], op=ALU.add),
                     reads=(bank, ya), writes=(ya,))

    def finish(self):
        k = self.k
        if not self.NPS:
            return
        out = self.P["s5_new"]
        for g2 in range(2):
            for sq_ in range(self.NPS):
                for d in range(2):
                    for ri in range(2):
                        k.store(k.sp, self.Hfin, out[sq_, d, ri].rearrange("(q g) p -> g p q", g=2)[g2],
                                self.Hfin.ap[g2 * 64:(g2 + 1) * 64, sq_, d, ri, :], allow_slow_non_contiguous=True)


def _s5_tile(k, nc, env):
    s5 = k.s5
    s5.env["psb"] = env["psb"]
    s5.setup_main(env["s5P"])
    return s5
```

```python
import contextlib
import math
import os
import numpy as np
import concourse.bass as bass
import concourse.mybir as mybir
from concourse.bass_utils import run_bass_kernel_spmd

F32 = mybir.dt.float32
BF16 = mybir.dt.bfloat16
AF = mybir.ActivationFunctionType
ALU = mybir.AluOpType

D = 4096
NQ, NKV, NR, NU = 1024, 512, 64, 2048
H = 32
DFF = 16384
EPS = 1e-6
TB = 512
NOTH, NOWN, NPR = [int(v) for v in os.environ.get('MK_SIZES', '2048,2048,1024').split(',')]
T5 = NOTH + NOWN + NPR
TO = NOWN + NPR
NCTX = 512
NKEY = NCTX + NOTH + NOWN + NPR
DEBUG = bool(int(os.environ.get("MK_DEBUG", "0")))
STOP_AFTER = int(os.environ.get("MK_STOP", "99"))


class Tok:
    __slots__ = ("key", "sem", "val")

    def __init__(self, key, sem, val):
        self.key, self.sem, self.val = key, sem, val


class Buf:
    def __init__(self, k, ap, name):
        self.k, self.ap, self.name = k, ap, name
        self.w, self.r = {}, {}
        self.sem = None
        self.cnt = 0

    def __getitem__(self, key):
        return self.ap[key]


class Eng:
    def __init__(self, k, name, e, own_wait):
        self.k, self.name, self.e = k, name, e
        self.key, self.sem = k.new_sem("e_" + name)
        self.n = 0
        self.seen = {}
        self.own_wait = own_wait
        self.pending = []

    def wait(self, tok):
        if tok.key == self.key and not self.own_wait:
            return
        if self.seen.get(tok.key, 0) >= tok.val:
            return
        self.e.wait_ge(tok.sem, tok.val)
        self.seen[tok.key] = tok.val


class K:
    def __init__(self, nc, es):
        self.nc, self.es = nc, es
        self.nsem = 0
        self.pe = Eng(self, "pe", nc.tensor, False)
        self.act = Eng(self, "act", nc.scalar, True)
        self.dve = Eng(self, "dve", nc.vector, True)
        self.pool = Eng(self, "pool", nc.gpsimd, True)
        self.sp = Eng(self, "sp", nc.sync, False)
        self.store_toks = {}
        self.nbuf = 0

    def new_sem(self, name):
        self.nsem += 1
        s = self.es.enter_context(self.nc.semaphore("s%d_%s" % (self.nsem, name)))
        return self.nsem, s

    def sb(self, shape, dt, name):
        self.nbuf += 1
        t = self.es.enter_context(self.nc.sbuf_tensor("%s_%d" % (name, self.nbuf), list(shape), dt))
        return Buf(self, t, name)

    def ring(self, n, shape, dt, name):
        return Ring([self.sb(shape, dt, name + str(i)) for i in range(n)])

    def _deps(self, eng, reads, writes):
        for b in reads:
            for t in b.w.values():
                eng.wait(t)
        for b in writes:
            for t in b.w.values():
                eng.wait(t)
            for t in b.r.values():
                eng.wait(t)

    def op(self, eng, fn, reads=(), writes=(), signal=True, **kw):
        self._deps(eng, reads, writes)
        inst = fn(**kw)
        if not signal:
            eng.pending.append((reads, writes))
            return None
        eng.n += 1
        inst.then_inc(eng.sem, 1)
        tok = Tok(eng.key, eng.sem, eng.n)
        for (rs, ws) in eng.pending + [(reads, writes)]:
            for b in rs:
                b.r[tok.key] = tok
            for b in ws:
                b.w = {tok.key: tok}
                b.r = {}
        eng.pending = []
        return tok

    def dma(self, q, out, in_, reads=(), writes=(), sembuf=None, store=False, **kw):
        self._deps(q, reads, writes)
        sbf = sembuf
        if sbf.sem is None:
            sbf.key, sbf.sem = self.new_sem("b_" + sbf.name)
        inst = q.e.dma_start(out=out, in_=in_, **kw)
        sbf.cnt += 16
        inst.then_inc(sbf.sem, 16)
        tok = Tok(sbf.key, sbf.sem, sbf.cnt)
        for b in reads:
            b.r[tok.key] = tok
        for b in writes:
            b.w = {tok.key: tok}
            b.r = {}
        if store:
            self.store_toks[tok.key] = tok
        return tok

    def load(self, q, buf, out, in_, **kw):
        return self.dma(q, out, in_, reads=(), writes=(buf,), sembuf=buf, **kw)

    def loadp(self, q, buf, out, in_, **kw):
        self._deps(q, (), ())
        tok = self.dma(q, out, in_, reads=(), writes=(), sembuf=buf, **kw)
        buf.w = {tok.key: tok}
        return tok

    def store(self, q, buf, out, in_, **kw):
        return self.dma(q, out, in_, reads=(buf,), writes=(), sembuf=buf, store=True, **kw)

    def dram_barrier(self):
        for q in (self.sp, self.pool):
            for t in self.store_toks.values():
                q.wait(t)

    def phase_barrier(self, bufs):
        nc = self.nc
        tok = self.op(self.dve, lambda: nc.vector.memset(self.scr[:, 0:1], 0.0), reads=(), writes=tuple(bufs) + (self.scr,))
        for e in (self.pe, self.act, self.pool, self.sp):
            e.wait(tok)

    def finish(self):
        for t in self.store_toks.values():
            self.sp.wait(t)


class Ring:
    def __init__(self, bufs):
        self.bufs = bufs
        self.i = 0

    def next(self):
        b = self.bufs[self.i % len(self.bufs)]
        self.i += 1
        return b


def _chunks(tiles, maxw):
    out, cur, w = [], [], 0
    for t in tiles:
        if cur and (w + t[1] > maxw or cur[-1][0] + cur[-1][1] != t[0]):
            out.append(cur)
            cur, w = [], 0
        cur.append(t)
        w += t[1]
    if cur:
        out.append(cur)
    return out


def build():
    nc = bass.Bass("TRN2", target_bir_lowering=False)
    es = contextlib.ExitStack()
    with es:
        _build(nc, es)
    return nc


def _build(nc, es):
    k = K(nc, es)
    pe, act, dve, pool, sp = k.pe, k.act, k.dve, k.pool, k.sp

    def din(name, shape, dt=F32):
        return nc.dram_tensor(name, list(shape), dt, kind="ExternalInput").ap()

    def dout(name, shape, dt=F32):
        return nc.dram_tensor(name, list(shape), dt, kind="ExternalOutput").ap()

    def dscr(name, shape, dt):
        return nc.dram_tensor(name, list(shape), dt, kind=("ExternalOutput" if DEBUG else "Internal")).ap()

    x_all = din("x_all", [T5, D])
    cvec = din("cvec", [2, D])
    ckv_ctx = din("ckv_ctx", [NCTX, NKV])
    krope_ctx = din("krope_ctx", [NCTX, NR])
    cosT = din("cosT", [NR, NOTH + NOWN])
    sinT = din("sinT", [NR, NOTH + NOWN])
    cident = din("cident", [128, 128])
    cperm = din("cperm", [64, 64])
    w_mod = din("w_mod", [D, 6 * D])
    b_mod = din("b_mod", [1, 6 * D])
    g_pre_mix = din("g_pre_mix", [D])
    w_in = din("w_in", [D, 11840])
    g_q_lat = din("g_q_lat", [NQ])
    g_kv_lat = din("g_kv_lat", [NKV])
    w_uq = din("w_uq", [NQ, H * 192])
    w_ukv = din("w_ukv", [NKV, H * 256])
    w_glu = din("w_glu", [NU, 2 * D])
    w_out = din("w_out", [D, D])
    g_post_mix = din("g_post_mix", [D])
    g_pre_mlp = din("g_pre_mlp", [D])
    w_ff1 = din("w_ff1", [D, DFF])
    w_ff2 = din("w_ff2", [DFF, D])
    g_post_mlp = din("g_post_mlp", [D])
    s5_d = din("s5_d", [NU])
    s5P = {
        "lam_re": din("s5_lam_re", [2, 128, 64]), "lam_im": din("s5_lam_im", [2, 128, 64]), "log_dt": din("s5_log_dt", [2, 128]),
        "b_re": din("s5_b_re", [2, 128, 64, 16]), "b_im": din("s5_b_im", [2, 128, 64, 16]),
        "c_re": din("s5_c_re", [2, 128, 16, 64]), "c_im": din("s5_c_im", [2, 128, 16, 64]),
        "h0": din("s5_h0", [2, 2, 128, 64]),
    }

    y_own = dout("y_own", [TO, D])
    ckv_new = dout("ckv_new", [NPR, NKV])
    krope_new = dout("krope_new", [NPR, NR])
    s5_new = dout("s5_new", [max(1, NPR // 256), 2, 2, 128, 64])

    modv = dscr("modv", [2, 6 * D], F32)
    xT = dscr("xT", [D, TO], F32)
    qT = dscr("qT", [H * 192, TO], BF16)
    ckvT = dscr("ckvT", [NKV, NKEY], BF16)
    kropeT = dscr("kropeT", [NR, NKEY], BF16)
    uT = dscr("uT", [NU, T5], BF16)
    sgaT = dscr("sgaT", [D, TO], BF16)
    sgbT = dscr("sgbT", [D, TO], BF16)
    maT = dscr("maT", [D, TO], BF16)
    yT = dscr("yT", [NU, TO], BF16)
    mT = dscr("mT", [D, TO], BF16)
    x1T = dscr("x1T", [D, TO], F32)
    a1T = dscr("a1T", [DFF, TO], BF16)
    h2T = dscr("h2T", [D, TO], BF16)
    s5P["s5_new"] = s5_new
    s5P["Kst"] = dscr("s5Kst", [2, 16, 16, 128, 128], BF16)
    s5P["M1"] = dscr("s5M1", [2, 16, 16, 2, 128, 128], BF16)
    s5P["M2"] = dscr("s5M2", [2, 17, 2, 128, 2048], BF16)

    ident = k.sb([128, 128], F32, "ident")
    perm = k.sb([64, 64], F32, "perm")
    ones_f = k.sb([128, 128], F32, "ones_f")
    ones_b = k.sb([128, 128], BF16, "ones_b")
    epsb = k.sb([128, 1], F32, "epsb")
    k.scr = k.sb([128, 4], F32, "scr")
    k.load(sp, ident, ident[:], cident)
    k.load(sp, perm, perm[:], cperm)
    k.op(dve, lambda: nc.vector.memset(ones_f[:], 1.0), writes=(ones_f,))
    k.op(dve, lambda: nc.vector.memset(ones_b[:], 1.0), writes=(ones_b,))
    k.op(dve, lambda: nc.vector.memset(epsb[:], EPS), writes=(epsb,))
    sc_a_mix = k.sb([128, 2, 32], F32, "a_mix")
    sc_sh_mix = k.sb([128, 2, 32], F32, "sh_mix")
    sc_gg_mix = k.sb([128, 2, 32], F32, "gg_mix")
    sc_a_mlp = k.sb([128, 2, 32], F32, "a_mlp")
    sc_sh_mlp = k.sb([128, 2, 32], F32, "sh_mlp")
    sc_gg_mlp = k.sb([128, 2, 32], F32, "gg_mlp")
    gq = k.sb([128, 8], F32, "gq")
    gkv = k.sb([128, 4], F32, "gkv")

    psum = es.enter_context(nc.psum_tensor("psum", [128, 8, 512], F32))
    PS = [Buf(k, psum[:, i, :], "ps%d" % i) for i in range(8)]
    gbank = Ring(PS[0:4])
    PR = PS[4]
    PM = Ring(PS[5:8])

    WR = {}
    st_bf = k.ring(4, [128, TB], BF16, "stbf")

    nslow = dict(allow_slow_non_contiguous=True)

    def gemm(W, KT, tiles, xview, xbufs, cb, ntok=TB, maxw=None):
        maxw = maxw or (8192 // KT)
        Wv = W.rearrange("(t p) n -> p t n", p=128)
        chunks = _chunks(tiles, maxw)

        def issue_load(ch):
            slot = WR['ring'].next()
            cw = sum(t[1] for t in ch)
            view = slot.ap[:, 0:KT * cw].rearrange("p (t c) -> p t c", c=cw)
            c0 = ch[0][0]
            k.load(pool, slot, view, Wv[:, :, c0:c0 + cw])
            return slot, view

        pend = [issue_load(chunks[0])]
        for ci, ch in enumerate(chunks):
            if ci + 1 < len(chunks):
                pend.append(issue_load(chunks[ci + 1]))
            slot, view = pend.pop(0)
            off = 0
            for (c0, M, tag) in ch:
                if ntok <= 512:
                    bank = gbank.next()
                    for kt in range(KT):
                        k.op(pe, lambda kt=kt, off=off, M=M, bank=bank, view=view: nc.tensor.matmul(
                            bank.ap[0:M, 0:ntok], view[:, kt, off:off + M], xview(kt),
                            start=(kt == 0), stop=(kt == KT - 1)),
                            reads=(slot,) + tuple(xbufs), writes=(bank,), signal=(kt == KT - 1))
                    cb(bank, M, tag)
                else:
                    for hf in range(ntok // 512):
                        bank = gbank.next()
                        for kt in range(KT):
                            k.op(pe, lambda kt=kt, off=off, M=M, bank=bank, view=view, hf=hf: nc.tensor.matmul(
                                bank.ap[0:M, 0:512], view[:, kt, off:off + M], xview(kt, hf),
                                start=(kt == 0), stop=(kt == KT - 1)),
                                reads=(slot,) + tuple(xbufs), writes=(bank,), signal=(kt == KT - 1))
                        cb(bank, M, tag, hf)
                off += M

    def rstd_from_bank(bank, F, out_buf, ntok=TB):
        k.op(act, lambda: nc.scalar.activation(out=out_buf.ap[:, 0:ntok], in_=bank.ap[:, 0:ntok], func=AF.Sqrt,
                                               bias=epsb[:, 0:1], scale=1.0 / F),
             reads=(bank, epsb), writes=(out_buf,))
        k.op(dve, lambda: nc.vector.reciprocal(out=out_buf.ap[:, 0:ntok], in_=out_buf.ap[:, 0:ntok]),
             reads=(out_buf,), writes=(out_buf,))

    def sumsq_tile(src_ap, src_buf, first, last, scratch, ntok=TB):
        k.op(act, lambda: nc.scalar.activation(out=scratch.ap[:, 0:ntok], in_=src_ap, func=AF.Square),
             reads=(src_buf,), writes=(scratch,))
        k.op(pe, lambda: nc.tensor.matmul(PR.ap[:, 0:ntok], ones_f[:], scratch.ap[:, 0:ntok], start=first, stop=last),
             reads=(ones_f, scratch), writes=(PR,), signal=True)

    k.s5 = None
    if STOP_AFTER >= 4 and not int(os.environ.get("MK_NOS5", "0")):
        k.s5 = _S5(k, nc, dict(psb=k.sb, ident=ident, PS=PS, PM=PM, gbank=gbank, s5P=s5P))
        k.s5.alloc_persist()
    with contextlib.ExitStack() as ph:
        def psb(shape, dt, name):
            k.nbuf += 1
            t = ph.enter_context(nc.sbuf_tensor("%s_%d" % (name, k.nbuf), list(shape), dt))
            return Buf(k, t, name)
        cT = psb([128, 2, 32], F32, "cT")
        scT = psb([128, 2, 32], BF16, "scT")
        WR['ring'] = Ring([psb([128, 8192], BF16, "wslot%d" % i) for i in range(3)])
        msr = Ring([psb([2, 256], F32, "msr%d" % i) for i in range(3)])
        s5gen = k.s5.pre(s5P) if k.s5 is not None else None
        bmr = Ring([psb([2, 512], F32, "bmr%d" % i) for i in range(3)])
        for j in range(2):
            (k.load if j == 0 else k.loadp)(sp, cT, cT[:, j, :], cvec[j].rearrange("(t p) -> p t", p=128), **nslow)
        k.op(act, lambda: nc.scalar.activation(out=scT[:], in_=cT[:], func=AF.Silu), reads=(cT,), writes=(scT,))
        wmv = w_mod.rearrange("(t p) n -> p t n", p=128)
        CWM = 256
        NCH = 6 * D // CWM

        def ldwm(ci):
            slot = WR['ring'].next()
            view = slot.ap[:, 0:32 * CWM].rearrange("p (t c) -> p t c", c=CWM)
            k.load(pool, slot, view, wmv[:, :, ci * CWM:(ci + 1) * CWM])
            return slot, view
        pend = [ldwm(0), ldwm(1)]
        for ci in range(NCH):
            if ci + 2 < NCH:
                pend.append(ldwm(ci + 2))
            wb, wview = pend.pop(0)
            bank = gbank.next()
            for kt in range(32):
                k.op(pe, lambda kt=kt, wview=wview, bank=bank: nc.tensor.matmul(
                    bank.ap[0:2, 0:CWM], scT[:, :, kt], wview[:, kt, :], start=(kt == 0), stop=(kt == 31)),
                    reads=(wb, scT), writes=(bank,), signal=(kt == 31))
            bm = bmr.next()
            k.load(sp, bm, bm[:, 0:CWM], b_mod[0, ci * CWM:(ci + 1) * CWM].partition_broadcast(2))
            ms = msr.next()
            k.op(dve, lambda bank=bank, ci=ci, bm=bm, ms=ms: nc.vector.tensor_tensor(
                out=ms[:, 0:CWM], in0=bank.ap[0:2, 0:CWM], in1=bm[:, 0:CWM],
                op=ALU.add), reads=(bank, bm), writes=(ms,))
            k.store(sp, ms, modv[:, ci * CWM:(ci + 1) * CWM], ms[:, 0:CWM])
            if s5gen is not None:
                for _ in range(6):
                    next(s5gen, None)
        if s5gen is not None:
            for _ in s5gen:
                pass
        k.dram_barrier()
        modT = psb([128, 2, 6, 32], F32, "modT")
        gT = psb([128, 4, 32], F32, "gT")
        for j in range(2):
            for i in range(6):
                (k.load if (j == 0 and i == 0) else k.loadp)(sp, modT, modT[:, j, i, :],
                                                             modv[j, i * D:(i + 1) * D].rearrange("(t p) -> p t", p=128), **nslow)
        for gi, g in enumerate((g_pre_mix, g_post_mix, g_pre_mlp, g_post_mlp)):
            (k.load if gi == 0 else k.loadp)(sp, gT, gT[:, gi, :], g.rearrange("(t p) -> p t", p=128), **nslow)
        k.load(sp, gq, gq[:], g_q_lat.rearrange("(t p) -> p t", p=128), **nslow)
        k.load(sp, gkv, gkv[:], g_kv_lat.rearrange("(t p) -> p t", p=128), **nslow)
        for j in range(2):
            for (dst, gi, isc) in ((sc_a_mix, 0, 1), (sc_a_mlp, 2, 4)):
                k.op(dve, lambda dst=dst, gi=gi, isc=isc, j=j: nc.vector.scalar_tensor_tensor(
                    out=dst[:, j, :], in0=modT[:, j, isc, :], scalar=1.0, in1=gT[:, gi, :], op0=ALU.add, op1=ALU.mult),
                    reads=(modT, gT), writes=(dst,))
            for (dst, ish) in ((sc_sh_mix, 0), (sc_sh_mlp, 3)):
                k.op(dve, lambda dst=dst, ish=ish, j=j: nc.vector.tensor_copy(out=dst[:, j, :], in_=modT[:, j, ish, :]),
                     reads=(modT,), writes=(dst,))
            for (dst, gi, ig) in ((sc_gg_mix, 1, 2), (sc_gg_mlp, 3, 5)):
                k.op(dve, lambda dst=dst, gi=gi, ig=ig, j=j: nc.vector.tensor_tensor(
                    out=dst[:, j, :], in0=modT[:, j, ig, :], in1=gT[:, gi, :], op=ALU.mult),
                    reads=(modT, gT), writes=(dst,))
        k.phase_barrier((modT, gT, cT, scT) + tuple(bmr.bufs) + tuple(WR['ring'].bufs) + tuple(msr.bufs))

    if STOP_AFTER <= 0:
        k.finish()
        return

    with contextlib.ExitStack() as ph:
        def psb(shape, dt, name):
            k.nbuf += 1
            t = ph.enter_context(nc.sbuf_tensor("%s_%d" % (name, k.nbuf), list(shape), dt))
            return Buf(k, t, name)
        WR['ring'] = Ring([psb([128, 8192], BF16, "wslot%d" % i) for i in range(3)])
        xtok = Ring([psb([128, D], F32, "xtok%d" % i) for i in range(1)])
        xTst = Ring([psb([128, 32, 128], F32, "xTst%d" % i) for i in range(1)])
        hT = psb([128, 32, TB], BF16, "hT")
        Rsb = psb([128, TB], F32, "Rsb")
        R2 = psb([128, TB], F32, "R2")
        tmpf = Ring([psb([128, 128], F32, "tmpf%d" % i) for i in range(2)])
        qlat = psb([128, 8, TB], F32, "qlat")
        qn = psb([128, 8, TB], BF16, "qn")
        ckvf = psb([128, 4, TB], F32, "ckvf")
        krf = psb([64, TB], F32, "krf")
        cs_t = psb([64, 2, TB], F32, "cs_t")
        sq = Ring([psb([128, TB], F32, "sq%d" % i) for i in range(2)])
        ropef = Ring([psb([64, TB], F32, "ropef%d" % i) for i in range(2)])
        t1r = Ring([psb([64, TB], F32, "t1r%d" % i) for i in range(2)])
        t2r = Ring([psb([64, TB], F32, "t2r%d" % i) for i in range(2)])
        tokst = Ring([psb([128, NKV], F32, "tokst%d" % i) for i in range(2)])

        def rope(src, dst_st):
            bank = PM.next()
            k.op(pe, lambda: nc.tensor.matmul(bank.ap[0:64, :], perm[:], src.ap[0:64, :], start=True, stop=True),
                 reads=(perm, src), writes=(bank,))
            t1, t2 = t1r.next(), t2r.next()
            k.op(dve, lambda: nc.vector.tensor_tensor(out=t1[:], in0=src.ap[0:64, :], in1=cs_t[:, 0, :], op=ALU.mult),
                 reads=(src, cs_t), writes=(t1,))
            k.op(dve, lambda: nc.vector.tensor_tensor(out=t2[:], in0=bank.ap[0:64, :], in1=cs_t[:, 1, :], op=ALU.mult),
                 reads=(bank, cs_t), writes=(t2,))
            k.op(dve, lambda: nc.vector.tensor_tensor(out=dst_st.ap[0:64, :], in0=t1[:], in1=t2[:], op=ALU.add),
                 reads=(t1, t2), writes=(dst_st,))

        for blk in range(T5 // TB):
            tok0 = blk * TB
            own = tok0 >= NOTH
            prompt = tok0 >= NOTH + NOWN
            j = 1 if prompt else 0
            o0 = tok0 - NOTH
            kv0 = NCTX + tok0
            if not prompt:
                k.load(sp, cs_t, cs_t[:, 0, :], cosT[:, tok0:tok0 + TB])
                k.loadp(sp, cs_t, cs_t[:, 1, :], sinT[:, tok0:tok0 + TB])
            for tt in range(4):
                xt = xtok.next()
                k.load(sp, xt, xt[:], x_all[tok0 + tt * 128: tok0 + (tt + 1) * 128, :])
                xs = xTst.next()
                for t in range(32):
                    bank = PM.next()
                    k.op(pe, lambda xt=xt, t=t, bank=bank: nc.tensor.transpose(
                        bank.ap[:, 0:128], xt[:, t * 128:(t + 1) * 128], ident[:]),
                        reads=(xt, ident), writes=(bank,))
                    k.op(act, lambda xs=xs, t=t, bank=bank: nc.scalar.copy(out=xs[:, t, :], in_=bank.ap[:, 0:128]),
                         reads=(bank,), writes=(xs,) if t == 0 else ())
                    s = sq.next()
                    k.op(act, lambda s=s, bank=bank: nc.scalar.activation(out=s.ap[:, 0:128], in_=bank.ap[:, 0:128], func=AF.Square),
                         reads=(bank,), writes=(s,))
                    k.op(pe, lambda s=s, t=t: nc.tensor.matmul(PR.ap[:, 0:128], ones_f[:], s.ap[:, 0:128], start=(t == 0), stop=(t == 31)),
                         reads=(ones_f, s), writes=(PR,), signal=True)
                xs.w = {act.key: Tok(act.key, act.sem, act.n)}
                rstd_from_bank(PR, D, Rsb, ntok=128)
                for t in range(32):
                    tf = tmpf.next()
                    k.op(dve, lambda tf=tf, xs=xs, t=t: nc.vector.tensor_tensor(
                        out=tf[:], in0=xs[:, t, :], in1=Rsb[:, 0:128], op=ALU.mult),
                        reads=(xs, Rsb), writes=(tf,))
                    k.op(act, lambda tf=tf, t=t, tt=tt, j=j: nc.scalar.activation(
                        out=hT[:, t, tt * 128:(tt + 1) * 128], in_=tf[:], func=AF.Identity,
                        bias=sc_sh_mix[:, j, t:t + 1], scale=sc_a_mix[:, j, t:t + 1]),
                        reads=(tf, sc_sh_mix, sc_a_mix), writes=(hT,) if (t == 0 and tt == 0) else ())
                if own:
                    k.store(sp, xs, xT.rearrange("(t p) n -> p t n", p=128)[:, :, o0 + tt * 128:o0 + (tt + 1) * 128], xs[:])
            hT.w = {act.key: Tok(act.key, act.sem, act.n)}
            hT.r = {}

            tiles = []
            if own:
                tiles += [(i * 128, 128, ("ql", i)) for i in range(8)]
            tiles += [(NQ + i * 128, 128, ("ckv", i)) for i in range(4)]
            tiles += [(NQ + NKV, 64, ("kr", 0))]
            tiles += [(NQ + NKV + NR + i * 128, 128, ("u", i)) for i in range(16)]
            if own:
                tiles += [(NQ + NKV + NR + NU + i * 128, 128, ("ga", i)) for i in range(32)]
                tiles += [(NQ + NKV + NR + NU + D + i * 128, 128, ("gb", i)) for i in range(32)]

            def cb(bank, M, tag, tok0=tok0, o0=o0):
                kind, i = tag
                if kind == "ql":
                    k.op(act, lambda: nc.scalar.copy(out=qlat[:, i, :], in_=bank.ap[:, :]), reads=(bank,),
                         writes=(qlat,) if i == 0 else ())
                elif kind == "ckv":
                    k.op(act, lambda: nc.scalar.copy(out=ckvf[:, i, :], in_=bank.ap[:, :]), reads=(bank,),
                         writes=(ckvf,) if i == 0 else ())
                elif kind == "kr":
                    k.op(act, lambda: nc.scalar.copy(out=krf[:, :], in_=bank.ap[0:64, :]), reads=(bank,), writes=(krf,))
                elif kind == "u":
                    st = st_bf.next()
                    k.op(act, lambda: nc.scalar.copy(out=st[:], in_=bank.ap[:, :]), reads=(bank,), writes=(st,))
                    k.store(sp, st, uT[i * 128:(i + 1) * 128, tok0:tok0 + TB], st[:])
                else:
                    st = st_bf.next()
                    k.op(act, lambda: nc.scalar.activation(out=st[:], in_=bank.ap[:, :], func=AF.Sigmoid),
                         reads=(bank,), writes=(st,))
                    dstT = sgaT if kind == "ga" else sgbT
                    k.store(sp, st, dstT[i * 128:(i + 1) * 128, o0:o0 + TB], st[:])

            gemm(w_in, 32, tiles, lambda kt: hT[:, kt, :], (hT,), cb)
            if own:
                qlat.w = {act.key: Tok(act.key, act.sem, act.n)}
            ckvf.w = {act.key: Tok(act.key, act.sem, act.n)}

            for i in range(4):
                s = sq.next()
                sumsq_tile(ckvf[:, i, :], ckvf, i == 0, i == 3, s)
            rstd_from_bank(PR, NKV, R2)
            for i in range(4):
                k.op(dve, lambda i=i: nc.vector.scalar_tensor_tensor(
                    out=ckvf[:, i, :], in0=ckvf[:, i, :], scalar=gkv[:, i:i + 1], in1=R2[:], op0=ALU.mult, op1=ALU.mult),
                    reads=(ckvf, gkv, R2), writes=(ckvf,))
                st = st_bf.next()
                k.op(act, lambda i=i, st=st: nc.scalar.copy(out=st[:], in_=ckvf[:, i, :]), reads=(ckvf,), writes=(st,))
                k.store(sp, st, ckvT[i * 128:(i + 1) * 128, kv0:kv0 + TB], st[:])
            if prompt:
                p0 = tok0 - NOTH - NOWN
                for tt in range(4):
                    bank = PM.next()
                    for i in range(4):
                        k.op(pe, lambda i=i, tt=tt, bank=bank: nc.tensor.transpose(
                            bank.ap[:, i * 128:(i + 1) * 128], ckvf[:, i, tt * 128:(tt + 1) * 128], ident[:]),
                            reads=(ckvf, ident), writes=(bank,), signal=(i == 3))
                    ts = tokst.next()
                    k.op(act, lambda ts=ts, bank=bank: nc.scalar.copy(out=ts[:], in_=bank.ap[:, :]), reads=(bank,), writes=(ts,))
                    k.store(sp, ts, ckv_new[p0 + tt * 128:p0 + (tt + 1) * 128, :], ts[:])
            st = st_bf.next()
            if prompt:
                k.op(act, lambda st=st: nc.scalar.copy(out=st.ap[0:64, :], in_=krf[:, :]), reads=(krf,), writes=(st,))
                for tt in range(4):
                    bank = PM.next()
                    k.op(pe, lambda tt=tt, bank=bank: nc.tensor.transpose(
                        bank.ap[:, 0:64], krf[:, tt * 128:(tt + 1) * 128], ident[0:64, 0:64]),
                        reads=(krf, ident), writes=(bank,))
                    ts = tokst.next()
                    k.op(act, lambda ts=ts, bank=bank: nc.scalar.copy(out=ts.ap[:, 0:64], in_=bank.ap[:, 0:64]),
                         reads=(bank,), writes=(ts,))
                    k.store(sp, ts, krope_new[p0 + tt * 128:p0 + (tt + 1) * 128, :], ts.ap[:, 0:64])
            else:
                rope(krf, st)
            k.store(sp, st, kropeT[:, kv0:kv0 + TB], st.ap[0:64, :])

            if own:
                for i in range(8):
                    s = sq.next()
                    sumsq_tile(qlat[:, i, :], qlat, i == 0, i == 7, s)
                rstd_from_bank(PR, NQ, R2)
                for i in range(8):
                    k.op(dve, lambda i=i: nc.vector.scalar_tensor_tensor(
                        out=qn[:, i, :], in0=qlat[:, i, :], scalar=gq[:, i:i + 1], in1=R2[:], op0=ALU.mult, op1=ALU.mult),
                        reads=(qlat, gq, R2), writes=(qn,) if i == 0 else ())
                qn.w = {dve.key: Tok(dve.key, dve.sem, dve.n)}
                qtiles = []
                for h in range(H):
                    qtiles += [(h * 192, 128, ("qn", h)), (h * 192 + 128, 64, ("qr", h))]

                def cbq(bank, M, tag, o0=o0):
                    kind, h = tag
                    st = st_bf.next()
                    if kind == "qn":
                        k.op(act, lambda: nc.scalar.copy(out=st[:], in_=bank.ap[:, :]), reads=(bank,), writes=(st,))
                        k.store(sp, st, qT[h * 192:h * 192 + 128, o0:o0 + TB], st[:])
                    else:
                        if prompt:
                            k.op(act, lambda: nc.scalar.copy(out=st.ap[0:64, :], in_=bank.ap[0:64, :]), reads=(bank,), writes=(st,))
                        else:
                            rf = ropef.next()
                            k.op(act, lambda: nc.scalar.copy(out=rf[:], in_=bank.ap[0:64, :]), reads=(bank,), writes=(rf,))
                            rope(rf, st)
                        k.store(sp, st, qT[h * 192 + 128:h * 192 + 192, o0:o0 + TB], st.ap[0:64, :])

                gemm(w_uq, 8, qtiles, lambda kt: qn[:, kt, :], (qn,), cbq, maxw=768)
        k.phase_barrier(tuple(xtok.bufs) + (hT, Rsb, R2, qlat, qn, ckvf, krf, cs_t) + tuple(xTst.bufs)
                        + tuple(tmpf.bufs) + tuple(sq.bufs) + tuple(ropef.bufs) + tuple(t1r.bufs) + tuple(t2r.bufs) + tuple(tokst.bufs)
                        + tuple(WR['ring'].bufs))


    if STOP_AFTER <= 1:
        k.finish()
        return
    k.dram_barrier()
    NPS = NPR // 256
    SKV = NCTX + NOTH + NOWN

    with contextlib.ExitStack() as ph:
        def psb(shape, dt, name):
            k.nbuf += 1
            t = ph.enter_context(nc.sbuf_tensor("%s_%d" % (name, k.nbuf), list(shape), dt))
            return Buf(k, t, name)
        allb = []

        def prg(n, shape, dt, name):
            r = Ring([psb(shape, dt, name + str(i)) for i in range(n)])
            allb.extend(r.bufs)
            return r
        ctok = prg(2, [128, NKV], F32, "ctok")
        krtok = prg(2, [128, NR], F32, "krtok")
        ckv_sb = prg(1, [128, 4, SKV], BF16, "ckv_sb").bufs[0]
        kr_sb = prg(1, [64, SKV], BF16, "kr_sb").bufs[0]
        KhTr = prg(2, [128, SKV], BF16, "KhT")
        Vhr = prg(2, [128, SKV // 128, 128], BF16, "Vh")
        wkvr = prg(2, [128, 4, 256], BF16, "wkv")
        qnr = prg(2, [128, NOWN], BF16, "qnh")
        qrr = prg(2, [64, NOWN], BF16, "qrh")
        PTr = prg(3, [128, TB], BF16, "PT")
        recr = prg(2, [128, TB], F32, "rec")
        ofr = prg(2, [128, TB], F32, "of")
        sgar = prg(2, [128, TB], BF16, "sga")
        PO = Ring([PS[5], PS[6]])
        PSUMS = Ring([PS[7], PS[4]])
        for t4 in range(NCTX // 128):
            ct = ctok.next()
            k.load(sp, ct, ct[:], ckv_ctx[t4 * 128:(t4 + 1) * 128, :])
            bank = gbank.next()
            for i in range(4):
                k.op(pe, lambda i=i, ct=ct, bank=bank: nc.tensor.transpose(bank.ap[:, i * 128:(i + 1) * 128], ct[:, i * 128:(i + 1) * 128], ident[:]),
                     reads=(ct, ident), writes=(bank,), signal=(i == 3))
            st = st_bf.next()
            k.op(act, lambda st=st, bank=bank: nc.scalar.copy(out=st[:], in_=bank.ap[:, :]), reads=(bank,), writes=(st,))
            k.store(sp, st, ckvT.rearrange("(i p) n -> p i n", p=128)[:, :, t4 * 128:(t4 + 1) * 128],
                    st.ap[:, :].rearrange("p (i n) -> p i n", n=128))
            kt_ = krtok.next()
            k.load(sp, kt_, kt_[:], krope_ctx[t4 * 128:(t4 + 1) * 128, :])
            bank = gbank.next()
            k.op(pe, lambda kt_=kt_, bank=bank: nc.tensor.transpose(bank.ap[0:64, 0:128], kt_[:, :], ident[:]),
                 reads=(kt_, ident), writes=(bank,))
            st = st_bf.next()
            k.op(act, lambda st=st, bank=bank: nc.scalar.copy(out=st.ap[0:64, 0:128], in_=bank.ap[0:64, 0:128]), reads=(bank,), writes=(st,))
            k.store(sp, st, kropeT[:, t4 * 128:(t4 + 1) * 128], st.ap[0:64, 0:128])
        k.dram_barrier()
        SCALE = 192.0 ** -0.5
        wkv_v = w_ukv.rearrange("(t p) n -> p t n", p=128)

        def attn_seq(q0, Lq, kv0, Lk):
            nkt = Lk // 128
            k.load(sp, ckv_sb, ckv_sb[:, :, 0:Lk], ckvT.rearrange("(i p) n -> p i n", p=128)[:, :, kv0:kv0 + Lk])
            k.load(sp, kr_sb, kr_sb[:, 0:Lk], kropeT[:, kv0:kv0 + Lk])
            for h in range(H):
                wkv = wkvr.next()
                k.load(pool, wkv, wkv[:], wkv_v[:, :, h * 256:(h + 1) * 256])
                qn_, qr_ = qnr.next(), qrr.next()
                k.load(sp, qn_, qn_[:, 0:Lq], qT[h * 192:h * 192 + 128, q0:q0 + Lq])
                k.load(sp, qr_, qr_[:, 0:Lq], qT[h * 192 + 128:h * 192 + 192, q0:q0 + Lq])
                KhT, Vh = KhTr.next(), Vhr.next()
                for kb in range(0, Lk, 512):
                    w = min(512, Lk - kb)
                    bank = gbank.next()
                    for kt in range(4):
                        k.op(pe, lambda kt=kt, kb=kb, w=w, bank=bank, wkv=wkv: nc.tensor.matmul(
                            bank.ap[:, 0:w], wkv[:, kt, 0:128], ckv_sb[:, kt, kb:kb + w], start=(kt == 0), stop=(kt == 3)),
                            reads=(wkv, ckv_sb), writes=(bank,), signal=(kt == 3))
                    k.op(act, lambda kb=kb, w=w, bank=bank, KhT=KhT: nc.scalar.copy(out=KhT[:, kb:kb + w], in_=bank.ap[:, 0:w]),
                         reads=(bank,), writes=(KhT,) if kb == 0 else ())
                KhT.w = {act.key: Tok(act.key, act.sem, act.n)}
                for g4 in range(0, nkt, 4):
                    n4 = min(4, nkt - g4)
                    bank = gbank.next()
                    for i in range(n4):
                        for kt in range(4):
                            k.op(pe, lambda kt=kt, i=i, g4=g4, bank=bank, wkv=wkv: nc.tensor.matmul(
                                bank.ap[:, i * 128:(i + 1) * 128], ckv_sb[:, kt, (g4 + i) * 128:(g4 + i + 1) * 128], wkv[:, kt, 128:256],
                                start=(kt == 0), stop=(kt == 3)),
                                reads=(wkv, ckv_sb), writes=(bank,), signal=(kt == 3 and i == n4 - 1))
                    k.op(dve, lambda g4=g4, n4=n4, bank=bank, Vh=Vh: nc.vector.tensor_copy(
                        out=Vh[:, g4:g4 + n4, :], in_=bank.ap[:, 0:n4 * 128].rearrange("p (i n) -> p i n", n=128)),
                        reads=(bank,), writes=(Vh,) if g4 == 0 else ())
                Vh.w = {dve.key: Tok(dve.key, dve.sem, dve.n)}
                for qb in range(0, Lq, TB):
                    nq = min(TB, Lq - qb)
                    po, psm = PO.next(), PSUMS.next()

                    def s_mm(kt):
                        bank = gbank.next()
                        k.op(pe, lambda: nc.tensor.matmul(bank.ap[:, 0:nq], KhT[:, kt * 128:(kt + 1) * 128], qn_[:, qb:qb + nq],
                                                          start=True, stop=False),
                             reads=(KhT, qn_), writes=(bank,), signal=False)
                        k.op(pe, lambda: nc.tensor.matmul(bank.ap[:, 0:nq], kr_sb[:, kt * 128:(kt + 1) * 128], qr_[:, qb:qb + nq],
                                                          start=False, stop=True),
                             reads=(kr_sb, qr_), writes=(bank,), signal=True)
                        pt = PTr.next()
                        k.op(act, lambda: nc.scalar.activation(out=pt[:, 0:nq], in_=bank.ap[:, 0:nq], func=AF.Exp, scale=SCALE),
                             reads=(bank,), writes=(pt,))
                        return pt
                    pts = [s_mm(0)]
                    for kt in range(nkt):
                        if kt + 1 < nkt:
                            pts.append(s_mm(kt + 1))
                        pt = pts.pop(0)
                        k.op(pe, lambda kt=kt, pt=pt: nc.tensor.matmul(po.ap[:, 0:nq], Vh[:, kt, :], pt[:, 0:nq],
                                                                     start=(kt == 0), stop=(kt == nkt - 1)),
                             reads=(Vh, pt), writes=(po,), signal=False)
                        k.op(pe, lambda kt=kt, pt=pt: nc.tensor.matmul(psm.ap[:, 0:nq], ones_b[:], pt[:, 0:nq],
                                                                     start=(kt == 0), stop=(kt == nkt - 1)),
                             reads=(ones_b, pt), writes=(psm,), signal=True)
                    rec, of, sga = recr.next(), ofr.next(), sgar.next()
                    k.load(sp, sga, sga[:, 0:nq], sgaT[h * 128:(h + 1) * 128, q0 + qb:q0 + qb + nq])
                    k.op(dve, lambda: nc.vector.reciprocal(out=rec[:, 0:nq], in_=psm.ap[:, 0:nq]), reads=(psm,), writes=(rec,))
                    k.op(dve, lambda: nc.vector.tensor_tensor(out=of[:, 0:nq], in0=po.ap[:, 0:nq], in1=rec[:, 0:nq], op=ALU.mult),
                         reads=(po, rec), writes=(of,))
                    st = st_bf.next()
                    k.op(dve, lambda: nc.vector.tensor_tensor(out=st[:, 0:nq], in0=of[:, 0:nq], in1=sga[:, 0:nq], op=ALU.mult),
                         reads=(of, sga), writes=(st,))
                    k.store(sp, st, maT[h * 128:(h + 1) * 128, q0 + qb:q0 + qb + nq], st[:, 0:nq])

        attn_seq(0, NOWN, 0, SKV)
        for i in range(NPS):
            attn_seq(NOWN + 256 * i, 256, SKV + 256 * i, 256)
        k.phase_barrier(allb)

    if STOP_AFTER <= 3:
        k.finish()
        return
    k.dram_barrier()

    with contextlib.ExitStack() as ph:
        def psb(shape, dt, name):
            k.nbuf += 1
            t = ph.enter_context(nc.sbuf_tensor("%s_%d" % (name, k.nbuf), list(shape), dt))
            return Buf(k, t, name)
        allb = []

        def prg(n, shape, dt, name):
            r = Ring([psb(shape, dt, name + str(i)) for i in range(n)])
            allb.extend(r.bufs)
            return r
        s5_ctx = _s5_tile(k, nc, locals()) if k.s5 is not None else None
        Dv = prg(1, [128, 16], F32, "Dv").bufs[0]
        k.load(sp, Dv, Dv[:], s5_d.rearrange("(t p) -> p t", p=128), **nslow)
        u_sb = prg(2, [128, T5], BF16, "u_sb")
        yacc = prg(2, [128, TO], F32, "yacc")
        gt1 = prg(1, [128, TO], F32, "gt1").bufs[0]
        gt2 = prg(1, [128, TO], F32, "gt2").bufs[0]
        yst = prg(1, [128, TO], BF16, "yst")
        def gelu_store(G, ya):
            k.op(act, lambda ya=ya: nc.scalar.activation(out=gt1[:], in_=ya[:], func=AF.Square), reads=(ya,), writes=(gt1,))
            k.op(dve, lambda: nc.vector.tensor_scalar(out=gt1[:], in0=gt1[:], scalar1=0.044715, scalar2=1.0, op0=ALU.mult, op1=ALU.add),
                 reads=(gt1,), writes=(gt1,))
            k.op(dve, lambda ya=ya: nc.vector.tensor_tensor(out=gt2[:], in0=gt1[:], in1=ya[:], op=ALU.mult), reads=(gt1, ya), writes=(gt2,))
            k.op(act, lambda: nc.scalar.activation(out=gt2[:], in_=gt2[:], func=AF.Sigmoid, scale=1.5957691216), reads=(gt2,), writes=(gt2,))
            ys = yst.next()
            k.op(dve, lambda ya=ya, ys=ys: nc.vector.tensor_tensor(out=ys[:], in0=gt2[:], in1=ya[:], op=ALU.mult), reads=(gt2, ya), writes=(ys,))
            k.store(sp, ys, yT[G * 128:(G + 1) * 128, :], ys[:])

        prev = None
        for G in range(16):
            ub = u_sb.next()
            k.load(sp, ub, ub[:], uT[G * 128:(G + 1) * 128, :])
            ya = yacc.next()
            k.op(dve, lambda ub=ub, ya=ya, G=G: nc.vector.tensor_scalar(out=ya[:], in0=ub[:, NOTH:T5], scalar1=Dv[:, G:G + 1],
                                                                      scalar2=None, op0=ALU.mult),
                 reads=(ub, Dv), writes=(ya,))
            for d in range(2 if s5_ctx is not None else 0):
                s5_ctx.stage(1, G, d, ub, ya)
                if prev is not None:
                    s5_ctx.stage(2, *prev)
                    if prev[1] == 1:
                        gelu_store(prev[0], prev[3])
                prev = (G, d, ub, ya)
            if s5_ctx is None:
                gelu_store(G, ya)
        if prev is not None:
            s5_ctx.stage(2, *prev)
            gelu_store(prev[0], prev[3])
        if s5_ctx is not None:
            s5_ctx.finish()
        k.phase_barrier(allb + (s5_ctx.bufs if s5_ctx is not None else []))

    if STOP_AFTER <= 4:
        k.finish()
        return
    k.dram_barrier()

    with contextlib.ExitStack() as ph:
        def psb(shape, dt, name):
            k.nbuf += 1
            t = ph.enter_context(nc.sbuf_tensor("%s_%d" % (name, k.nbuf), list(shape), dt))
            return Buf(k, t, name)
        allb = []

        def prg(n, shape, dt, name):
            r = Ring([psb(shape, dt, name + str(i)) for i in range(n)])
            allb.extend(r.bufs)
            return r
        WR['ring'] = prg(3, [128, 8192], BF16, "wslot")
        yb = prg(1, [128, 16, TB], BF16, "yb").bufs[0]
        zar = prg(4, [128, TB], F32, "za")
        sigr = prg(2, [128, TB], F32, "sig")
        sgbr = prg(2, [128, TB], BF16, "sgb")
        mar = prg(2, [128, TB], BF16, "ma")
        for blk in range(TO // TB):
            o0 = blk * TB
            k.load(sp, yb, yb[:], yT.rearrange("(t p) n -> p t n", p=128)[:, :, o0:o0 + TB])
            tiles = []
            for i in range(0, 32, 2):
                tiles += [(i * 128, 128, ("za", i)), ((i + 1) * 128, 128, ("za", i + 1)),
                          (D + i * 128, 128, ("zb", i)), (D + (i + 1) * 128, 128, ("zb", i + 1))]
            zas = {}

            def cb(bank, M, tag, o0=o0):
                kind, i = tag
                if kind == "za":
                    z = zar.next()
                    k.op(act, lambda: nc.scalar.copy(out=z[:], in_=bank.ap[:, :]), reads=(bank,), writes=(z,))
                    zas[i] = z
                else:
                    z = zas.pop(i)
                    sg = sigr.next()
                    k.op(act, lambda: nc.scalar.activation(out=sg[:], in_=bank.ap[:, :], func=AF.Sigmoid), reads=(bank,), writes=(sg,))
                    sgb, ma = sgbr.next(), mar.next()
                    k.load(sp, sgb, sgb[:], sgbT[i * 128:(i + 1) * 128, o0:o0 + TB])
                    k.load(sp, ma, ma[:], maT[i * 128:(i + 1) * 128, o0:o0 + TB])
                    k.op(dve, lambda: nc.vector.tensor_tensor(out=sg[:], in0=sg[:], in1=z[:], op=ALU.mult), reads=(sg, z), writes=(sg,))
                    k.op(dve, lambda: nc.vector.tensor_tensor(out=sg[:], in0=sg[:], in1=sgb[:], op=ALU.mult), reads=(sg, sgb), writes=(sg,))
                    st = st_bf.next()
                    k.op(dve, lambda: nc.vector.tensor_tensor(out=st[:], in0=sg[:], in1=ma[:], op=ALU.add), reads=(sg, ma), writes=(st,))
                    k.store(sp, st, mT[i * 128:(i + 1) * 128, o0:o0 + TB], st[:])

            gemm(w_glu, 16, tiles, lambda kt: yb[:, kt, :], (yb,), cb, maxw=256)
        k.phase_barrier(allb)

    if STOP_AFTER <= 5:
        k.finish()
        return
    k.dram_barrier()

    with contextlib.ExitStack() as ph:
        def psb(shape, dt, name):
            k.nbuf += 1
            t = ph.enter_context(nc.sbuf_tensor("%s_%d" % (name, k.nbuf), list(shape), dt))
            return Buf(k, t, name)
        allb = []

        def prg(n, shape, dt, name):
            r = Ring([psb(shape, dt, name + str(i)) for i in range(n)])
            allb.extend(r.bufs)
            return r
        WR['ring'] = prg(3, [128, 8192], BF16, "wslot")
        actb = prg(1, [128, 32, TB], BF16, "actb").bufs[0]
        acc = prg(1, [128, 32, TB], F32, "acc").bufs[0]
        Rsb = prg(1, [128, TB], F32, "Rsb5").bufs[0]
        sq = prg(2, [128, TB], F32, "sq5")
        xtr = prg(2, [128, TB], F32, "xt5")
        tfr = prg(2, [128, TB], F32, "tf5")
        ytok = prg(1, [128, D], F32, "ytok").bufs[0]

        def acc_sumsq(i):
            s_ = sq.next()
            sumsq_tile(acc[:, i, :], acc, i == 0, i == 31, s_)

        def resid(j, ggv, srcT, store_to):
            for i in range(32):
                xt = xtr.next()
                k.load(sp, xt, xt[:], srcT[i * 128:(i + 1) * 128, o0:o0 + TB])
                k.op(dve, lambda i=i: nc.vector.scalar_tensor_tensor(out=acc[:, i, :], in0=acc[:, i, :], scalar=ggv[:, j, i:i + 1],
                                                                   in1=Rsb[:], op0=ALU.mult, op1=ALU.mult),
                     reads=(acc, ggv, Rsb), writes=(acc,))
                k.op(dve, lambda i=i, xt=xt: nc.vector.tensor_tensor(out=acc[:, i, :], in0=acc[:, i, :], in1=xt[:], op=ALU.add),
                     reads=(acc, xt), writes=(acc,))
                if store_to is not None:
                    k.store(sp, acc, store_to[i * 128:(i + 1) * 128, o0:o0 + TB], acc[:, i, :])

        for blk in range(TO // TB):
            o0 = blk * TB
            j = 0 if o0 < NOWN else 1
            k.load(sp, actb, actb[:], mT.rearrange("(t p) n -> p t n", p=128)[:, :, o0:o0 + TB])

            def cb_out(bank, M, tag):
                i = tag
                k.op(act, lambda: nc.scalar.copy(out=acc[:, i, :], in_=bank.ap[:, :]), reads=(bank,), writes=(acc,) if i == 0 else ())
            gemm(w_out, 32, [(i * 128, 128, i) for i in range(32)], lambda kt: actb[:, kt, :], (actb,), cb_out)
            acc.w = {act.key: Tok(act.key, act.sem, act.n)}
            for i in range(32):
                acc_sumsq(i)
            rstd_from_bank(PR, D, Rsb)
            resid(j, sc_gg_mix, xT, x1T)
            for i in range(32):
                acc_sumsq(i)
            rstd_from_bank(PR, D, Rsb)
            for i in range(32):
                tf = tfr.next()
                k.op(dve, lambda i=i, tf=tf: nc.vector.tensor_tensor(out=tf[:], in0=acc[:, i, :], in1=Rsb[:], op=ALU.mult),
                     reads=(acc, Rsb), writes=(tf,))
                st = st_bf.next()
                k.op(act, lambda i=i, tf=tf, j=j, st=st: nc.scalar.activation(out=st[:], in_=tf[:], func=AF.Identity,
                                                                             bias=sc_sh_mlp[:, j, i:i + 1], scale=sc_a_mlp[:, j, i:i + 1]),
                     reads=(tf, sc_sh_mlp, sc_a_mlp), writes=(st,))
                k.store(sp, st, h2T[i * 128:(i + 1) * 128, o0:o0 + TB], st[:])

        k.dram_barrier()
        k.phase_barrier(allb)
    with contextlib.ExitStack() as ph:
        def psb(shape, dt, name):
            k.nbuf += 1
            t = ph.enter_context(nc.sbuf_tensor("%s_%d" % (name, k.nbuf), list(shape), dt))
            return Buf(k, t, name)
        allb = []

        def prg(n, shape, dt, name):
            r = Ring([psb(shape, dt, name + str(i)) for i in range(n)])
            allb.extend(r.bufs)
            return r
        WR['ring'] = prg(3, [128, 8192], BF16, "wslot")
        TB1 = 1024 if TO % 1024 == 0 else 512
        h2b = prg(1, [128, 32, TB1], BF16, "h2b").bufs[0]
        tfr = prg(3, [128, TB], F32, "tf6")
        for blk in range(TO // TB1):
            o0 = blk * TB1
            k.load(sp, h2b, h2b[:], h2T.rearrange("(t p) n -> p t n", p=128)[:, :, o0:o0 + TB1])

            def cb_ff1(bank, M, tag, hf=0, o0=o0):
                i = tag
                tf = tfr.next()
                k.op(act, lambda: nc.scalar.activation(out=tf[:], in_=bank.ap[:, :], func=AF.Relu), reads=(bank,), writes=(tf,))
                st = st_bf.next()
                k.op(dve, lambda: nc.vector.tensor_tensor(out=st[:], in0=tf[:], in1=tf[:], op=ALU.mult), reads=(tf,), writes=(st,))
                k.store(sp, st, a1T[i * 128:(i + 1) * 128, o0 + hf * 512:o0 + hf * 512 + 512], st[:])
            if TB1 == 1024:
                gemm(w_ff1, 32, [(i * 128, 128, i) for i in range(DFF // 128)], lambda kt, hf: h2b[:, kt, hf * 512:(hf + 1) * 512], (h2b,), cb_ff1, ntok=1024)
            else:
                gemm(w_ff1, 32, [(i * 128, 128, i) for i in range(DFF // 128)], lambda kt: h2b[:, kt, :], (h2b,), cb_ff1)
        k.dram_barrier()
        k.phase_barrier(allb)
    with contextlib.ExitStack() as ph:
        def psb(shape, dt, name):
            k.nbuf += 1
            t = ph.enter_context(nc.sbuf_tensor("%s_%d" % (name, k.nbuf), list(shape), dt))
            return Buf(k, t, name)
        allb = []

        def prg(n, shape, dt, name):
            r = Ring([psb(shape, dt, name + str(i)) for i in range(n)])
            allb.extend(r.bufs)
            return r
        WR['ring'] = prg(3, [128, 8192], BF16, "wslot")
        actb = prg(1, [128, 32, TB], BF16, "actb").bufs[0]
        acc = prg(1, [128, 32, TB], F32, "acc").bufs[0]
        Rsb = prg(1, [128, TB], F32, "Rsb5").bufs[0]
        sq = prg(2, [128, TB], F32, "sq5")
        xtr = prg(2, [128, TB], F32, "xt5")
        ytok = prg(1, [128, D], F32, "ytok").bufs[0]

        def acc_sumsq(i):
            s_ = sq.next()
            sumsq_tile(acc[:, i, :], acc, i == 0, i == 31, s_)

        def resid(j, ggv, srcT, store_to):
            for i in range(32):
                xt = xtr.next()
                k.load(sp, xt, xt[:], srcT[i * 128:(i + 1) * 128, o0:o0 + TB])
                k.op(dve, lambda i=i: nc.vector.scalar_tensor_tensor(out=acc[:, i, :], in0=acc[:, i, :], scalar=ggv[:, j, i:i + 1],
                                                                   in1=Rsb[:], op0=ALU.mult, op1=ALU.mult),
                     reads=(acc, ggv, Rsb), writes=(acc,))
                k.op(dve, lambda i=i, xt=xt: nc.vector.tensor_tensor(out=acc[:, i, :], in0=acc[:, i, :], in1=xt[:], op=ALU.add),
                     reads=(acc, xt), writes=(acc,))

        for blk in range(TO // TB):
            o0 = blk * TB
            j = 0 if o0 < NOWN else 1
            for kc in range(4):
                k.load(sp, actb, actb[:], a1T.rearrange("(t p) n -> p t n", p=128)[:, kc * 32:(kc + 1) * 32, o0:o0 + TB])

                def cb_ff2(bank, M, tag, kc=kc):
                    i = tag
                    if kc == 0:
                        k.op(act, lambda: nc.scalar.copy(out=acc[:, i, :], in_=bank.ap[:, :]), reads=(bank,), writes=(acc,))
                    else:
                        k.op(dve, lambda: nc.vector.tensor_tensor(out=acc[:, i, :], in0=acc[:, i, :], in1=bank.ap[:, :], op=ALU.add),
                             reads=(acc, bank), writes=(acc,))
                gemm(w_ff2[kc * D:(kc + 1) * D, :], 32, [(i * 128, 128, i) for i in range(32)], lambda kt: actb[:, kt, :], (actb,), cb_ff2)
            for i in range(32):
                acc_sumsq(i)
            rstd_from_bank(PR, D, Rsb)
            resid(j, sc_gg_mlp, x1T, None)
            for tt in range(4):
                for i4 in range(8):
                    bank = PM.next()
                    for ii in range(4):
                        i = i4 * 4 + ii
                        k.op(pe, lambda i=i, ii=ii, tt=tt, bank=bank: nc.tensor.transpose(
                            bank.ap[:, ii * 128:(ii + 1) * 128], acc[:, i, tt * 128:(tt + 1) * 128], ident[:]),
                            reads=(acc, ident), writes=(bank,), signal=(ii == 3))
                    k.op(act, lambda i4=i4, bank=bank: nc.scalar.copy(out=ytok[:, i4 * 512:(i4 + 1) * 512], in_=bank.ap[:, :]),
                         reads=(bank,), writes=(ytok,))
                k.store(sp, ytok, y_own[o0 + tt * 128:o0 + (tt + 1) * 128, :], ytok[:])
        k.phase_barrier(allb)

    k.finish()


def _rope_tables(pos):
    rows = 4096 // 64
    n_freq = 16
    inv_freq = (10000.0 ** (-np.arange(n_freq, dtype=np.float32) / n_freq)).astype(np.float32)
    row = (pos // 64).astype(np.float32)
    col = (pos % 64).astype(np.float32)
    ang = np.concatenate([row[:, None] * inv_freq, col[:, None] * inv_freq], axis=-1)
    ang = np.concatenate([ang, ang], axis=-1)
    return np.cos(ang).astype(np.float32), np.sin(ang).astype(np.float32)


def core_inputs(inp, core):
    b, half = core // 2, core % 2
    xs = np.asarray(inp["x_sample"][b])
    pos = np.arange(4096)
    if half == 1:
        order = pos
    else:
        order = pos[::-1]
    xp = np.asarray(inp["x_prompt"][4 * core:4 * core + 4])
    if half == 0:
        xp = xp[:, ::-1]
    x_all = np.concatenate([xs[order], xp.reshape(NPR, D)], axis=0)
    cos, sin = _rope_tables(order)
    perm = np.zeros((64, 64), np.float32)
    for m in range(32):
        perm[m + 32, m] = -1.0
        perm[m, m + 32] = 1.0
    sq = lambda a: np.ascontiguousarray(np.asarray(a)[0])
    d = {
        "x_all": np.ascontiguousarray(x_all),
        "cvec": np.ascontiguousarray(np.stack([np.asarray(inp["c"][b]), np.asarray(inp["c_ctx"])])),
        "ckv_ctx": sq(inp["cache_ckv"][b]),
        "krope_ctx": sq(inp["cache_krope"][b]),
        "cosT": np.ascontiguousarray(cos.T), "sinT": np.ascontiguousarray(sin.T),
        "cident": np.eye(128, dtype=np.float32), "cperm": perm,
        "w_mod": sq(inp["w_mod"]), "b_mod": np.asarray(inp["b_mod"]).reshape(1, -1),
        "g_pre_mix": sq(inp["g_pre_mix"]), "w_in": sq(inp["w_in"]), "g_q_lat": sq(inp["g_q_lat"]),
        "g_kv_lat": sq(inp["g_kv_lat"]), "w_uq": sq(inp["w_uq"]), "w_ukv": sq(inp["w_ukv"]),
        "w_glu": sq(inp["w_glu"]), "w_out": sq(inp["w_out"]), "g_post_mix": sq(inp["g_post_mix"]),
        "g_pre_mlp": sq(inp["g_pre_mlp"]), "w_ff1": sq(inp["w_ff1"]), "w_ff2": sq(inp["w_ff2"]),
        "g_post_mlp": sq(inp["g_post_mlp"]), "s5_d": sq(inp["s5_d"]),
    }
    dsel = [0, 1] if half == 1 else [1, 0]
    for nm in ("s5_lam_re", "s5_lam_im", "s5_log_dt", "s5_b_re", "s5_b_im", "s5_c_re", "s5_c_im"):
        d[nm] = np.ascontiguousarray(np.asarray(inp[nm])[0][dsel])
    d["s5_h0"] = np.ascontiguousarray(np.asarray(inp["state_s5"])[b, 0][dsel])
    return d


def kernel(**inputs):
    nc = build()
    in_maps = [core_inputs(inputs, c) for c in range(8)]
    res = run_bass_kernel_spmd(nc, in_maps, core_ids=list(range(8)))
    R = res.results
    y_prompt = np.zeros((32, 256, D), np.float32)
    y_sample = np.zeros((4, 4096, D), np.float32)
    n_ckv = np.zeros((32, 1, 256, NKV), np.float32)
    n_kr = np.zeros((32, 1, 256, NR), np.float32)
    n_s5 = np.zeros((32, 1, 2, 2, 128, 64), np.float32)
    for c in range(8):
        b, half = c // 2, c % 2
        r = R[c]
        yo = np.asarray(r["y_own"])
        ys, yp = yo[:NOWN], yo[NOWN:].reshape(4, 256, D)
        ck = np.asarray(r["ckv_new"]).reshape(4, 256, NKV)
        kr = np.asarray(r["krope_new"]).reshape(4, 256, NR)
        s5 = np.asarray(r["s5_new"]).reshape(4, 2, 2, 128, 64)
        if half == 0:
            y_sample[b, :2048] = ys[::-1]
            yp, ck, kr = yp[:, ::-1], ck[:, ::-1], kr[:, ::-1]
            s5 = s5[:, ::-1]
        else:
            y_sample[b, 2048:] = ys
        y_prompt[4 * c:4 * c + 4] = yp
        n_ckv[4 * c:4 * c + 4, 0] = ck
        n_kr[4 * c:4 * c + 4, 0] = kr
        n_s5[4 * c:4 * c + 4, 0] = s5
    return (y_prompt, y_sample, n_ckv, n_kr, n_s5)


TC = 16


class _S5:
    def __init__(self, k, nc, env):
        self.k, self.nc, self.env = k, nc, env
        self.bufs = []

    def sb(self, shape, dt, name):
        b = self.env["psb"](shape, dt, name)
        self.bufs.append(b)
        return b

    def alloc_persist(self):
        self.Akr = self.sb([128, 2, 9, 64], F32, "Akr")
        self.Aki = self.sb([128, 2, 9, 64], F32, "Aki")
        self.Akn = self.sb([128, 2, 9, 64], F32, "Akn")
        self.h0 = self.sb([128, 2, 2, 64], F32, "h0")
        self.Hfin = self.sb([128, max(1, NPR // 256), 2, 2, 64], F32, "Hfin")

    def pre(self, P):
        k, nc = self.k, self.nc
        dve, act, pe, sp = k.dve, k.act, k.pe, k.sp
        nslow = dict(allow_slow_non_contiguous=True)
        ident, PS = self.env["ident"], self.env["PS"]
        PM = self.env["PM"]
        V = nc.vector
        with contextlib.ExitStack() as ph:
            tmpb = []

            def t(shape, dt, name):
                k.nbuf += 1
                tt = ph.enter_context(nc.sbuf_tensor("%s_%d" % (name, k.nbuf), list(shape), dt))
                b = Buf(k, tt, name)
                tmpb.append(b)
                return b

            def s64(name):
                return t([128, 64], F32, name)
            lr, li, ldt = s64("lr"), s64("li"), s64("ldt")
            dt_, er, th, kf, r_, x_, x2 = (s64(n) for n in ("dt", "er", "th", "kf", "r", "x", "x2"))
            ki = t([128, 64], mybir.dt.int32, "ki")
            sn, cs, cc, ss_, sc_, tm = (s64(n) for n in ("sn", "cs", "cc", "ss", "sc", "tm"))
            ar, ai, den, am1, wr, wi, t1, t2 = (s64(n) for n in ("ar", "ai", "den", "am1", "wr", "wi", "t1", "t2"))
            pr = t([128, 17, 64], F32, "pr")
            pi_ = t([128, 17, 64], F32, "pi")
            Bre, Bim, Cre, Cim = (t([128, 64, 16], F32, n) for n in ("Bre", "Bim", "Cre", "Cim"))
            Bbr, Bbi, Xr, Xi, T1, T2 = (t([128, 64, 16], F32, n) for n in ("Bbr", "Bbi", "Xr", "Xi", "T1", "T2"))
            XbdFr, XbdFi = t([128, 64, 32], F32, "XbdFr"), t([128, 64, 32], F32, "XbdFi")
            Xbdr, Xbdi = t([128, 64, 32], BF16, "Xbdr"), t([128, 64, 32], BF16, "Xbdi")
            Ybdr, Ybdi = t([128, 64, 32], BF16, "Ybdr"), t([128, 64, 32], BF16, "Ybdi")
            Cbdr, Cbdn = t([128, 64, 32], BF16, "Cbdr"), t([128, 64, 32], BF16, "Cbdn")
            kst = Ring([t([128, 128], BF16, "kst%d" % i) for i in range(3)])
            ctr = Ring([t([128, 128], F32, "ctile%d" % i) for i in range(2)])

            def tt_(out, a, b, op, rd, wrb):
                k.op(dve, lambda: V.tensor_tensor(out=out, in0=a, in1=b, op=op), reads=rd, writes=wrb)

            def ts_(out, a, s1, s2, op0, op1, rd, wrb):
                if op1 is None:
                    k.op(dve, lambda: V.tensor_scalar(out=out, in0=a, scalar1=s1, scalar2=None, op0=op0), reads=rd, writes=wrb)
                else:
                    k.op(dve, lambda: V.tensor_scalar(out=out, in0=a, scalar1=s1, scalar2=s2, op0=op0, op1=op1), reads=rd, writes=wrb)

            def cmul(orr, oi, ar_, ai_, br_, bi_, rd, wr_bufs, tA, tB):
                tt_(tA.ap[:], ar_, br_, ALU.mult, rd, (tA,))
                tt_(tB.ap[:], ai_, bi_, ALU.mult, rd, (tB,))
                tt_(orr, tA.ap[:], tB.ap[:], ALU.subtract, (tA, tB), wr_bufs[0:1])
                tt_(tA.ap[:], ar_, bi_, ALU.mult, rd, (tA,))
                tt_(tB.ap[:], ai_, br_, ALU.mult, rd, (tB,))
                tt_(oi, tA.ap[:], tB.ap[:], ALU.add, (tA, tB), wr_bufs[1:2])

            for d in range(2):
                for g2 in range(2):
                    ps_ = slice(g2 * 64, (g2 + 1) * 64)
                    ld = k.load if g2 == 0 else k.loadp
                    ld(sp, lr, lr.ap[ps_, :], P["lam_re"][d].rearrange("(q g) p -> g p q", g=2)[g2], **nslow)
                    ld(sp, li, li.ap[ps_, :], P["lam_im"][d].rearrange("(q g) p -> g p q", g=2)[g2], **nslow)
                    ld(sp, ldt, ldt.ap[ps_, :], P["log_dt"][d].rearrange("(q g) -> g q", g=2)[g2].partition_broadcast(64), **nslow)
                    ld(sp, Bre, Bre.ap[ps_, :, :], P["b_re"][d].rearrange("(q g) p j -> g p q j", g=2)[g2], **nslow)
                    ld(sp, Bim, Bim.ap[ps_, :, :], P["b_im"][d].rearrange("(q g) p j -> g p q j", g=2)[g2], **nslow)
                    for ri in range(2):
                        (k.load if (g2 == 0 and ri == 0 and d == 0) else k.loadp)(
                            sp, self.h0, self.h0.ap[ps_, d, ri, :], P["h0"][d, ri].rearrange("(q g) p -> g p q", g=2)[g2], **nslow)
                for (dstC, srcC) in ((Cre, P["c_re"]), (Cim, P["c_im"])):
                    for q8 in range(8):
                        ctile = ctr.next()
                        for g2 in range(2):
                            (k.load if g2 == 0 else k.loadp)(
                                sp, ctile, ctile.ap[:, g2 * 64:(g2 + 1) * 64],
                                srcC[d].rearrange("(q g) j p -> g q j p", g=2)[g2][q8 * 8:(q8 + 1) * 8])
                        bank = PM.next()
                        k.op(pe, lambda bank=bank, ctile=ctile: nc.tensor.transpose(bank.ap[:, 0:128], ctile.ap[:], ident[:]),
                             reads=(ctile, ident), writes=(bank,))
                        k.op(act, lambda bank=bank, dstC=dstC, q8=q8: nc.scalar.copy(
                            out=dstC.ap[:, q8 * 8:(q8 + 1) * 8, :], in_=bank.ap[:, 0:128].rearrange("p (a b) -> p a b", b=16)),
                            reads=(bank,), writes=(dstC,))
                yield
                k.op(act, lambda: nc.scalar.activation(out=dt_.ap[:], in_=ldt.ap[:], func=AF.Exp), reads=(ldt,), writes=(dt_,))
                tt_(t1.ap[:], lr.ap[:], dt_.ap[:], ALU.mult, (lr, dt_), (t1,))
                k.op(act, lambda: nc.scalar.activation(out=er.ap[:], in_=t1.ap[:], func=AF.Exp), reads=(t1,), writes=(er,))
                tt_(th.ap[:], li.ap[:], dt_.ap[:], ALU.mult, (li, dt_), (th,))
                ts_(kf.ap[:], th.ap[:], 1.0 / (2 * math.pi), None, ALU.mult, None, (th,), (kf,))
                k.op(dve, lambda: V.tensor_copy(out=ki.ap[:], in_=kf.ap[:]), reads=(kf,), writes=(ki,))
                k.op(dve, lambda: V.tensor_copy(out=kf.ap[:], in_=ki.ap[:]), reads=(ki,), writes=(kf,))
                k.op(dve, lambda: V.scalar_tensor_tensor(out=r_.ap[:], in0=kf.ap[:], scalar=-2 * math.pi, in1=th.ap[:],
                                                         op0=ALU.mult, op1=ALU.add), reads=(kf, th), writes=(r_,))
                ts_(x_.ap[:], r_.ap[:], 1.0 / 32, None, ALU.mult, None, (r_,), (x_,))
                tt_(x2.ap[:], x_.ap[:], x_.ap[:], ALU.mult, (x_,), (x2,))
                ts_(sn.ap[:], x2.ap[:], -1.0 / 42, 1.0, ALU.mult, ALU.add, (x2,), (sn,))
                for cdiv in (20.0, 6.0):
                    tt_(tm.ap[:], x2.ap[:], sn.ap[:], ALU.mult, (x2, sn), (tm,))
                    ts_(sn.ap[:], tm.ap[:], -1.0 / cdiv, 1.0, ALU.mult, ALU.add, (tm,), (sn,))
                tt_(sn.ap[:], sn.ap[:], x_.ap[:], ALU.mult, (sn, x_), (sn,))
                ts_(cs.ap[:], x2.ap[:], -1.0 / 56, 1.0, ALU.mult, ALU.add, (x2,), (cs,))
                for cdiv in (30.0, 12.0, 2.0):
                    tt_(tm.ap[:], x2.ap[:], cs.ap[:], ALU.mult, (x2, cs), (tm,))
                    ts_(cs.ap[:], tm.ap[:], -1.0 / cdiv, 1.0, ALU.mult, ALU.add, (tm,), (cs,))
                for _ in range(5):
                    tt_(cc.ap[:], cs.ap[:], cs.ap[:], ALU.mult, (cs,), (cc,))
                    tt_(ss_.ap[:], sn.ap[:], sn.ap[:], ALU.mult, (sn,), (ss_,))
                    tt_(sc_.ap[:], sn.ap[:], cs.ap[:], ALU.mult, (sn, cs), (sc_,))
                    tt_(cs.ap[:], cc.ap[:], ss_.ap[:], ALU.subtract, (cc, ss_), (cs,))
                    ts_(sn.ap[:], sc_.ap[:], 2.0, None, ALU.mult, None, (sc_,), (sn,))
                tt_(ar.ap[:], er.ap[:], cs.ap[:], ALU.mult, (er, cs), (ar,))
                tt_(ai.ap[:], er.ap[:], sn.ap[:], ALU.mult, (er, sn), (ai,))
                yield
                tt_(t1.ap[:], lr.ap[:], lr.ap[:], ALU.mult, (lr,), (t1,))
                tt_(t2.ap[:], li.ap[:], li.ap[:], ALU.mult, (li,), (t2,))
                tt_(den.ap[:], t1.ap[:], t2.ap[:], ALU.add, (t1, t2), (den,))
                k.op(dve, lambda: V.reciprocal(out=den.ap[:], in_=den.ap[:]), reads=(den,), writes=(den,))
                ts_(am1.ap[:], ar.ap[:], -1.0, None, ALU.add, None, (ar,), (am1,))
                tt_(t1.ap[:], am1.ap[:], lr.ap[:], ALU.mult, (am1, lr), (t1,))
                tt_(t2.ap[:], ai.ap[:], li.ap[:], ALU.mult, (ai, li), (t2,))
                tt_(wr.ap[:], t1.ap[:], t2.ap[:], ALU.add, (t1, t2), (wr,))
                tt_(wr.ap[:], wr.ap[:], den.ap[:], ALU.mult, (wr, den), (wr,))
                tt_(t1.ap[:], ai.ap[:], lr.ap[:], ALU.mult, (ai, lr), (t1,))
                tt_(t2.ap[:], am1.ap[:], li.ap[:], ALU.mult, (am1, li), (t2,))
                tt_(wi.ap[:], t1.ap[:], t2.ap[:], ALU.subtract, (t1, t2), (wi,))
                tt_(wi.ap[:], wi.ap[:], den.ap[:], ALU.mult, (wi, den), (wi,))

                def bc(b):
                    return b.ap[:, :].unsqueeze(2).to_broadcast([128, 64, 16])
                cmul(Bbr.ap[:], Bbi.ap[:], bc(wr), bc(wi), Bre.ap[:], Bim.ap[:], (wr, wi, Bre, Bim), (Bbr, Bbi), T1, T2)
                yield
                k.op(dve, lambda: V.memset(pr.ap[:, 0, :], 1.0), writes=(pr,))
                k.op(dve, lambda: V.memset(pi_.ap[:, 0, :], 0.0), writes=(pi_,))
                for n in range(1, 17):
                    cmul(pr.ap[:, n, :], pi_.ap[:, n, :], pr.ap[:, n - 1, :], pi_.ap[:, n - 1, :], ar.ap[:], ai.ap[:],
                         (pr, pi_, ar, ai), (pr, pi_), cc, ss_)
                yield
                Akr, Aki, Akn = self.Akr, self.Aki, self.Akn
                k.op(dve, lambda: V.tensor_copy(out=Akr.ap[:, d, 0, :], in_=pr.ap[:, 16, :]), reads=(pr,), writes=(Akr,))
                k.op(dve, lambda: V.tensor_copy(out=Aki.ap[:, d, 0, :], in_=pi_.ap[:, 16, :]), reads=(pi_,), writes=(Aki,))
                for kk in range(1, 9):
                    cmul(Akr.ap[:, d, kk, :], Aki.ap[:, d, kk, :], Akr.ap[:, d, kk - 1, :], Aki.ap[:, d, kk - 1, :],
                         Akr.ap[:, d, kk - 1, :], Aki.ap[:, d, kk - 1, :], (Akr, Aki), (Akr, Aki), cc, ss_)
                ts_(Akn.ap[:, d, :, :], Aki.ap[:, d, :, :], -1.0, None, ALU.mult, None, (Aki,), (Akn,))
                for (bd, src, sgn) in ((Cbdr, Cre, 1.0), (Cbdn, Cim, -1.0)):
                    k.op(dve, lambda bd=bd: V.memset(bd.ap[:], 0.0), writes=(bd,))
                    for g2 in range(2):
                        ps_ = slice(g2 * 64, (g2 + 1) * 64)
                        ts_(bd.ap[ps_, :, g2 * 16:(g2 + 1) * 16], src.ap[ps_, :, :], sgn, None, ALU.mult, None, (src,), (bd,))
                for bd in (XbdFr, XbdFi, Ybdr, Ybdi):
                    k.op(dve, lambda bd=bd: V.memset(bd.ap[:], 0.0), writes=(bd,))
                for n in range(17):
                    prn = pr.ap[:, n, :].unsqueeze(2).to_broadcast([128, 64, 16])
                    pin = pi_.ap[:, n, :].unsqueeze(2).to_broadcast([128, 64, 16])
                    if n <= 15:
                        cmul(Xr.ap[:], Xi.ap[:], prn, pin, Bbr.ap[:], Bbi.ap[:], (pr, pi_, Bbr, Bbi), (Xr, Xi), T1, T2)
                        for (bdF, bd, src) in ((XbdFr, Xbdr, Xr), (XbdFi, Xbdi, Xi)):
                            for g2 in range(2):
                                ps_ = slice(g2 * 64, (g2 + 1) * 64)
                                k.op(dve, lambda bdF=bdF, src=src, ps_=ps_, g2=g2: V.tensor_copy(
                                    out=bdF.ap[ps_, :, g2 * 16:(g2 + 1) * 16], in_=src.ap[ps_, :, :]), reads=(src,), writes=(bdF,))
                            k.op(dve, lambda bdF=bdF, bd=bd: V.tensor_copy(out=bd.ap[:], in_=bdF.ap[:]), reads=(bdF,), writes=(bd,))
                        for G in range(16):
                            bank = PM.next()
                            k.op(dve, lambda bank=bank: V.memset(bank.ap[:, 0:128], 0.0), writes=(bank,))
                            for q in range(4):
                                Q = 4 * G + q
                                k.op(pe, lambda bank=bank, q=q, Q=Q: nc.tensor.matmul(
                                    bank.ap[32 * q:32 * q + 32, 32 * q:32 * q + 32], Xbdr.ap[:, Q, :], Cbdr.ap[:, Q, :],
                                    start=True, stop=False, tile_position=(0, 32 * q)),
                                    reads=(Xbdr, Cbdr), writes=(bank,), signal=False)
                                k.op(pe, lambda bank=bank, q=q, Q=Q: nc.tensor.matmul(
                                    bank.ap[32 * q:32 * q + 32, 32 * q:32 * q + 32], Xbdi.ap[:, Q, :], Cbdn.ap[:, Q, :],
                                    start=False, stop=True, tile_position=(0, 32 * q)),
                                    reads=(Xbdi, Cbdn), writes=(bank,), signal=(q == 3))
                            st = kst.next()
                            k.op(act, lambda st=st, bank=bank: nc.scalar.copy(out=st.ap[:], in_=bank.ap[:, 0:128]), reads=(bank,), writes=(st,))
                            k.store(sp, st, P["Kst"][d, G, n], st.ap[:])
                            yield
                            for ri, bdF in enumerate((XbdFr, XbdFi)):
                                bank = PM.next()
                                k.op(pe, lambda bank=bank, bdF=bdF, G=G: nc.tensor.transpose(
                                    bank.ap[:, 0:128], bdF.ap[:, 4 * G:4 * G + 4, :].rearrange("p a b -> p (a b)"), ident[:]),
                                    reads=(bdF, ident), writes=(bank,))
                                st = kst.next()
                                k.op(act, lambda st=st, bank=bank: nc.scalar.copy(out=st.ap[:], in_=bank.ap[:, 0:128]), reads=(bank,), writes=(st,))
                                k.store(sp, st, P["M1"][d, G, n, ri], st.ap[:])
                    if n >= 1:
                        cmul(Xr.ap[:], Xi.ap[:], prn, pin, Cre.ap[:], Cim.ap[:], (pr, pi_, Cre, Cim), (Xr, Xi), T1, T2)
                        for g2 in range(2):
                            ps_ = slice(g2 * 64, (g2 + 1) * 64)
                            k.op(dve, lambda ps_=ps_, g2=g2: V.tensor_copy(out=Ybdr.ap[ps_, :, g2 * 16:(g2 + 1) * 16], in_=Xr.ap[ps_, :, :]),
                                 reads=(Xr,), writes=(Ybdr,))
                            ts_(Ybdi.ap[ps_, :, g2 * 16:(g2 + 1) * 16], Xi.ap[ps_, :, :], -1.0, None, ALU.mult, None, (Xi,), (Ybdi,))
                        k.store(sp, Ybdr, P["M2"][d, n, 0], Ybdr.ap[:].rearrange("p a b -> p (a b)"))
                        k.store(sp, Ybdi, P["M2"][d, n, 1], Ybdi.ap[:].rearrange("p a b -> p (a b)"))
                        yield
            k.phase_barrier(tmpb)
        yield

    def setup_main(self, P):
        k, nc = self.k, self.nc
        self.P = P
        S = NOTH + NOWN
        self.NPS = NPR // 256
        self.Kst_sb2 = [self.sb([128, 16, 128], BF16, "Kst_sb") for _ in range(2)]
        self.M1_sb = self.sb([128, 16, 2, 128], BF16, "M1_sb")
        self.M2_sb2 = [self.sb([128, 16, 2, 128], BF16, "M2_sb") for _ in range(2)]
        nsmax = S // TC + 1
        self.Zs = [[[self.sb([128, nsmax], F32, "Zs") for _ in range(2)] for _ in range(2)] for _ in range(4)]
        self.Zp = [[[self.sb([128, self.NPS, 17], F32, "Zp") for _ in range(2)] for _ in range(2)] for _ in range(4)]
        self.Hp2 = [[[self.sb([128, TO // TC], BF16, "Hp") for _ in range(2)] for _ in range(4)] for _ in range(2)]

    def stage(self, which, G, d, ub, ya):
        k, nc = self.k, self.nc
        dve, act, pe, sp, pool = k.dve, k.act, k.pe, k.sp, k.pool
        V = nc.vector
        P = self.P
        gbank, PM = self.env["gbank"], self.env["PM"]
        S = NOTH + NOWN
        NPS = self.NPS
        NT = TO // TC
        par = d
        Kst_sb, M1_sb, M2_sb = self.Kst_sb2[par], self.M1_sb, self.M2_sb2[par]
        Hp = self.Hp2[par]
        if True:
            fwd = (d == 0)
            tokA0 = 0 if fwd else NOTH
            nS = (S - tokA0) // TC
            NA = (T5 - tokA0) // TC
            if which == 1:
                k.load(sp, M1_sb, M1_sb.ap[:], P["M1"][d, G].rearrange("n r p c -> p n r c"))
                k.load(sp, Kst_sb, Kst_sb.ap[:], P["Kst"][d, G].rearrange("n p c -> p n c"))
                k.load(sp, M2_sb, M2_sb.ap[:], P["M2"][d, 1:17, :, :, G * 128:(G + 1) * 128].rearrange("n r p c -> p n r c"))
            uall = ub.ap[:, tokA0:T5].rearrange("p (c s) -> p c s", s=TC)
            uown = ub.ap[:, NOTH:T5].rearrange("p (c s) -> p c s", s=TC)
            for q in (range(4) if which == 1 else []):
                Q = 4 * G + q
                rows = slice(32 * q, 32 * q + 32)
                for ri in range(2):
                    bank = gbank.next()
                    for s in range(TC):
                        n = (TC - 1 - s) if fwd else s
                        k.op(pe, lambda s=s, n=n, bank=bank, ri=ri: nc.tensor.matmul(
                            bank.ap[:, 0:NA], M1_sb.ap[rows, n, ri, :], uall[rows, :, s],
                            start=(s == 0), stop=(s == TC - 1), tile_position=(32 * q, 0)),
                            reads=(M1_sb, ub), writes=(bank,), signal=(s == TC - 1))
                    zs, zp = self.Zs[q][0][ri], self.Zp[q][0][ri]
                    so = 1 if fwd else 0
                    k.op(act, lambda bank=bank, zs=zs, so=so: nc.scalar.copy(out=zs.ap[:, so:so + nS], in_=bank.ap[:, 0:nS]),
                         reads=(bank,), writes=(zs,))
                    hpos = 0 if fwd else nS
                    k.op(dve, lambda zs=zs, hpos=hpos, ri=ri, Q=Q: V.tensor_copy(out=zs.ap[:, hpos:hpos + 1], in_=self.h0.ap[:, d, ri, Q:Q + 1]),
                         reads=(self.h0,), writes=(zs,))
                    if NPS:
                        k.op(act, lambda bank=bank, zp=zp, so=so: nc.scalar.copy(
                            out=zp.ap[:, :, so:so + 16], in_=bank.ap[:, nS:nS + NPS * 16].rearrange("p (a b) -> p a b", b=16)),
                            reads=(bank,), writes=(zp,))
                        hp_ = 0 if fwd else 16
                        k.op(dve, lambda zp=zp, hp_=hp_: V.memset(zp.ap[:, :, hp_:hp_ + 1], 0.0), writes=(zp,))

                def scan(Z, L, nd3):
                    pp, kk, sh = 0, 0, 1
                    while sh < L:
                        src, dst = Z[pp], Z[1 - pp]
                        Ar, Ai, An = (self.Akr.ap[:, d, kk, Q:Q + 1], self.Aki.ap[:, d, kk, Q:Q + 1], self.Akn.ap[:, d, kk, Q:Q + 1])

                        def v(b, lo, hi):
                            return b.ap[:, :, lo:hi] if nd3 else b.ap[:, lo:hi]
                        if fwd:
                            o_lo, o_hi, i_lo, i_hi, c_lo, c_hi = sh, L, 0, L - sh, 0, sh
                        else:
                            o_lo, o_hi, i_lo, i_hi, c_lo, c_hi = 0, L - sh, sh, L, L - sh, L
                        rd = (src[0], src[1], self.Akr, self.Aki, self.Akn)
                        k.op(dve, lambda: V.scalar_tensor_tensor(out=v(dst[0], o_lo, o_hi), in0=v(src[0], i_lo, i_hi), scalar=Ar,
                                                                 in1=v(src[0], o_lo, o_hi), op0=ALU.mult, op1=ALU.add), reads=rd, writes=(dst[0],))
                        k.op(dve, lambda: V.scalar_tensor_tensor(out=v(dst[0], o_lo, o_hi), in0=v(src[1], i_lo, i_hi), scalar=An,
                                                                 in1=v(dst[0], o_lo, o_hi), op0=ALU.mult, op1=ALU.add), reads=rd + (dst[0],), writes=(dst[0],))
                        k.op(dve, lambda: V.scalar_tensor_tensor(out=v(dst[1], o_lo, o_hi), in0=v(src[1], i_lo, i_hi), scalar=Ar,
                                                                 in1=v(src[1], o_lo, o_hi), op0=ALU.mult, op1=ALU.add), reads=rd, writes=(dst[1],))
                        k.op(dve, lambda: V.scalar_tensor_tensor(out=v(dst[1], o_lo, o_hi), in0=v(src[0], i_lo, i_hi), scalar=Ai,
                                                                 in1=v(dst[1], o_lo, o_hi), op0=ALU.mult, op1=ALU.add), reads=rd + (dst[1],), writes=(dst[1],))
                        for ri in range(2):
                            k.op(act, lambda ri=ri: nc.scalar.copy(out=v(dst[ri], c_lo, c_hi), in_=v(src[ri], c_lo, c_hi)),
                                 reads=(src[ri],), writes=(dst[ri],))
                        pp, kk, sh = 1 - pp, kk + 1, sh * 2
                    return pp
                pS = scan(self.Zs[q], nS + 1, False)
                Rs = self.Zs[q][pS]
                if fwd:
                    s_lo = NOTH // TC
                else:
                    s_lo = 1
                for ri in range(2):
                    k.op(act, lambda ri=ri, Rs=Rs, s_lo=s_lo: nc.scalar.copy(out=Hp[q][ri].ap[:, 0:NOWN // TC],
                                                                             in_=Rs[ri].ap[:, s_lo:s_lo + NOWN // TC]),
                         reads=(Rs[ri],), writes=(Hp[q][ri],))
                if NPS:
                    pP = scan(self.Zp[q], 17, True)
                    Rp = self.Zp[q][pP]
                    p_lo = 0 if fwd else 1
                    fin = 16 if fwd else 0
                    for ri in range(2):
                        k.op(act, lambda ri=ri, Rp=Rp, p_lo=p_lo: nc.scalar.copy(
                            out=Hp[q][ri].ap[:, NOWN // TC:NT].rearrange("p (a b) -> p a b", b=16), in_=Rp[ri].ap[:, :, p_lo:p_lo + 16]),
                            reads=(Rp[ri],), writes=(Hp[q][ri],))
                        k.op(dve, lambda ri=ri, Rp=Rp, fin=fin, Q=Q: V.tensor_copy(out=self.Hfin.ap[:, :, d, ri, Q:Q + 1], in_=Rp[ri].ap[:, :, fin:fin + 1]),
                             reads=(Rp[ri],), writes=(self.Hfin,))
            if which == 1:
                return
            yav = ya.ap[:, :].rearrange("p (c s) -> p c s", s=TC)
            for so_ in range(TC):
                bank = gbank.next()
                srcs = list(range(0, so_ + 1)) if fwd else list(range(so_, TC))
                for ii, sp_ in enumerate(srcs):
                    lag = abs(so_ - sp_)
                    k.op(pe, lambda ii=ii, sp_=sp_, lag=lag, bank=bank: nc.tensor.matmul(
                        bank.ap[:, 0:NT], Kst_sb.ap[:, lag, :], uown[:, :, sp_], start=(ii == 0), stop=False),
                        reads=(Kst_sb, ub), writes=(bank,), signal=False)
                n = (so_ + 1) if fwd else (TC - so_)
                for q in range(4):
                    for ri in range(2):
                        last = (q == 3 and ri == 1)
                        k.op(pe, lambda q=q, ri=ri, n=n, bank=bank, last=last: nc.tensor.matmul(
                            bank.ap[32 * q:32 * q + 32, 0:NT], M2_sb.ap[:, n - 1, ri, 32 * q:32 * q + 32], Hp[q][ri].ap[:, :],
                            start=False, stop=(ri == 1), tile_position=(0, 32 * q)),
                            reads=(M2_sb, Hp[q][ri]), writes=(bank,), signal=last)
                k.op(dve, lambda so_=so_, bank=bank: V.tensor_tensor(out=yav[:, :, so_], in0=yav[:, :, so_], in1=bank.ap[:, 0:NT], op=ALU.add),
                     reads=(bank, ya), writes=(ya,))

    def finish(self):
        k = self.k
        if not self.NPS:
            return
        out = self.P["s5_new"]
        for g2 in range(2):
            for sq_ in range(self.NPS):
                for d in range(2):
                    for ri in range(2):
                        k.store(k.sp, self.Hfin, out[sq_, d, ri].rearrange("(q g) p -> g p q", g=2)[g2],
                                self.Hfin.ap[g2 * 64:(g2 + 1) * 64, sq_, d, ri, :], allow_slow_non_contiguous=True)


def _s5_tile(k, nc, env):
    s5 = k.s5
    s5.env["psb"] = env["psb"]
    s5.setup_main(env["s5P"])
    return s5
```

```python
import contextlib
import math
import os
import numpy as np
import concourse.bass as bass
import concourse.mybir as mybir
from concourse.bass_utils import run_bass_kernel_spmd

F32 = mybir.dt.float32
BF16 = mybir.dt.bfloat16
AF = mybir.ActivationFunctionType
ALU = mybir.AluOpType

D = 4096
NQ, NKV, NR, NU = 1024, 512, 64, 2048
H = 32
DFF = 16384
EPS = 1e-6
TB = 512
NOTH, NOWN, NPR = [int(v) for v in os.environ.get('MK_SIZES', '2048,2048,1024').split(',')]
T5 = NOTH + NOWN + NPR
TO = NOWN + NPR
NCTX = 512
NKEY = NCTX + NOTH + NOWN + NPR
DEBUG = bool(int(os.environ.get("MK_DEBUG", "0")))
STOP_AFTER = int(os.environ.get("MK_STOP", "99"))


class Tok:
    __slots__ = ("key", "sem", "val")

    def __init__(self, key, sem, val):
        self.key, self.sem, self.val = key, sem, val


class Buf:
    def __init__(self, k, ap, name):
        self.k, self.ap, self.name = k, ap, name
        self.w, self.r = {}, {}
        self.sem = None
        self.cnt = 0

    def __getitem__(self, key):
        return self.ap[key]


class Eng:
    def __init__(self, k, name, e, own_wait):
        self.k, self.name, self.e = k, name, e
        self.key, self.sem = k.new_sem("e_" + name)
        self.n = 0
        self.seen = {}
        self.own_wait = own_wait
        self.pending = []

    def wait(self, tok):
        if tok.key == self.key and not self.own_wait:
            return
        if self.seen.get(tok.key, 0) >= tok.val:
            return
        self.e.wait_ge(tok.sem, tok.val)
        self.seen[tok.key] = tok.val


class K:
    def __init__(self, nc, es):
        self.nc, self.es = nc, es
        self.nsem = 0
        self.pe = Eng(self, "pe", nc.tensor, False)
        self.act = Eng(self, "act", nc.scalar, True)
        self.dve = Eng(self, "dve", nc.vector, True)
        self.pool = Eng(self, "pool", nc.gpsimd, True)
        self.sp = Eng(self, "sp", nc.sync, False)
        self.store_toks = {}
        self.nbuf = 0

    def new_sem(self, name):
        self.nsem += 1
        s = self.es.enter_context(self.nc.semaphore("s%d_%s" % (self.nsem, name)))
        return self.nsem, s

    def sb(self, shape, dt, name):
        self.nbuf += 1
        t = self.es.enter_context(self.nc.sbuf_tensor("%s_%d" % (name, self.nbuf), list(shape), dt))
        return Buf(self, t, name)

    def ring(self, n, shape, dt, name):
        return Ring([self.sb(shape, dt, name + str(i)) for i in range(n)])

    def _deps(self, eng, reads, writes):
        for b in reads:
            for t in b.w.values():
                eng.wait(t)
        for b in writes:
            for t in b.w.values():
                eng.wait(t)
            for t in b.r.values():
                eng.wait(t)

    def op(self, eng, fn, reads=(), writes=(), signal=True, **kw):
        self._deps(eng, reads, writes)
        inst = fn(**kw)
        if not signal:
            eng.pending.append((reads, writes))
            return None
        eng.n += 1
        inst.then_inc(eng.sem, 1)
        tok = Tok(eng.key, eng.sem, eng.n)
        for (rs, ws) in eng.pending + [(reads, writes)]:
            for b in rs:
                b.r[tok.key] = tok
            for b in ws:
                b.w = {tok.key: tok}
                b.r = {}
        eng.pending = []
        return tok

    def dma(self, q, out, in_, reads=(), writes=(), sembuf=None, store=False, **kw):
        self._deps(q, reads, writes)
        sbf = sembuf
        if sbf.sem is None:
            sbf.key, sbf.sem = self.new_sem("b_" + sbf.name)
        inst = q.e.dma_start(out=out, in_=in_, **kw)
        sbf.cnt += 16
        inst.then_inc(sbf.sem, 16)
        tok = Tok(sbf.key, sbf.sem, sbf.cnt)
        for b in reads:
            b.r[tok.key] = tok
        for b in writes:
            b.w = {tok.key: tok}
            b.r = {}
        if store:
            self.store_toks[tok.key] = tok
        return tok

    def load(self, q, buf, out, in_, **kw):
        return self.dma(q, out, in_, reads=(), writes=(buf,), sembuf=buf, **kw)

    def loadp(self, q, buf, out, in_, **kw):
        self._deps(q, (), ())
        tok = self.dma(q, out, in_, reads=(), writes=(), sembuf=buf, **kw)
        buf.w = {tok.key: tok}
        return tok

    def store(self, q, buf, out, in_, **kw):
        return self.dma(q, out, in_, reads=(buf,), writes=(), sembuf=buf, store=True, **kw)

    def dram_barrier(self):
        for q in (self.sp, self.pool):
            for t in self.store_toks.values():
                q.wait(t)

    def phase_barrier(self, bufs):
        nc = self.nc
        tok = self.op(self.dve, lambda: nc.vector.memset(self.scr[:, 0:1], 0.0), reads=(), writes=tuple(bufs) + (self.scr,))
        for e in (self.pe, self.act, self.pool, self.sp):
            e.wait(tok)

    def finish(self):
        for t in self.store_toks.values():
            self.sp.wait(t)


class Ring:
    def __init__(self, bufs):
        self.bufs = bufs
        self.i = 0

    def next(self):
        b = self.bufs[self.i % len(self.bufs)]
        self.i += 1
        return b


def _chunks(tiles, maxw):
    out, cur, w = [], [], 0
    for t in tiles:
        if cur and (w + t[1] > maxw or cur[-1][0] + cur[-1][1] != t[0]):
            out.append(cur)
            cur, w = [], 0
        cur.append(t)
        w += t[1]
    if cur:
        out.append(cur)
    return out


def build():
    nc = bass.Bass("TRN2", target_bir_lowering=False)
    es = contextlib.ExitStack()
    with es:
        _build(nc, es)
    return nc


def _build(nc, es):
    k = K(nc, es)
    pe, act, dve, pool, sp = k.pe, k.act, k.dve, k.pool, k.sp

    def din(name, shape, dt=F32):
        return nc.dram_tensor(name, list(shape), dt, kind="ExternalInput").ap()

    def dout(name, shape, dt=F32):
        return nc.dram_tensor(name, list(shape), dt, kind="ExternalOutput").ap()

    def dscr(name, shape, dt):
        return nc.dram_tensor(name, list(shape), dt, kind=("ExternalOutput" if DEBUG else "Internal")).ap()

    x_all = din("x_all", [T5, D])
    cvec = din("cvec", [2, D])
    ckv_ctx = din("ckv_ctx", [NCTX, NKV])
    krope_ctx = din("krope_ctx", [NCTX, NR])
    cosT = din("cosT", [NR, NOTH + NOWN])
    sinT = din("sinT", [NR, NOTH + NOWN])
    cident = din("cident", [128, 128])
    cperm = din("cperm", [64, 64])
    w_mod = din("w_mod", [D, 6 * D])
    b_mod = din("b_mod", [1, 6 * D])
    g_pre_mix = din("g_pre_mix", [D])
    w_in = din("w_in", [D, 11840])
    g_q_lat = din("g_q_lat", [NQ])
    g_kv_lat = din("g_kv_lat", [NKV])
    w_uq = din("w_uq", [NQ, H * 192])
    w_ukv = din("w_ukv", [NKV, H * 256])
    w_glu = din("w_glu", [NU, 2 * D])
    w_out = din("w_out", [D, D])
    g_post_mix = din("g_post_mix", [D])
    g_pre_mlp = din("g_pre_mlp", [D])
    w_ff1 = din("w_ff1", [D, DFF])
    w_ff2 = din("w_ff2", [DFF, D])
    g_post_mlp = din("g_post_mlp", [D])
    s5_d = din("s5_d", [NU])
    s5P = {
        "lam_re": din("s5_lam_re", [2, 128, 64]), "lam_im": din("s5_lam_im", [2, 128, 64]), "log_dt": din("s5_log_dt", [2, 128]),
        "b_re": din("s5_b_re", [2, 128, 64, 16]), "b_im": din("s5_b_im", [2, 128, 64, 16]),
        "c_re": din("s5_c_re", [2, 128, 16, 64]), "c_im": din("s5_c_im", [2, 128, 16, 64]),
        "h0": din("s5_h0", [2, 2, 128, 64]),
    }

    y_own = dout("y_own", [TO, D])
    ckv_new = dout("ckv_new", [NPR, NKV])
    krope_new = dout("krope_new", [NPR, NR])
    s5_new = dout("s5_new", [max(1, NPR // 256), 2, 2, 128, 64])

    modv = dscr("modv", [2, 6 * D], F32)
    xT = dscr("xT", [D, TO], F32)
    qT = dscr("qT", [H * 192, TO], BF16)
    ckvT = dscr("ckvT", [NKV, NKEY], BF16)
    kropeT = dscr("kropeT", [NR, NKEY], BF16)
    uT = dscr("uT", [NU, T5], BF16)
    sgaT = dscr("sgaT", [D, TO], BF16)
    sgbT = dscr("sgbT", [D, TO], BF16)
    maT = dscr("maT", [D, TO], BF16)
    yT = dscr("yT", [NU, TO], BF16)
    mT = dscr("mT", [D, TO], BF16)
    x1T = dscr("x1T", [D, TO], F32)
    a1T = dscr("a1T", [DFF, TO], BF16)
    h2T = dscr("h2T", [D, TO], BF16)
    s5P["s5_new"] = s5_new
    s5P["Kst"] = dscr("s5Kst", [2, 16, 16, 128, 128], BF16)
    s5P["M1"] = dscr("s5M1", [2, 16, 16, 2, 128, 128], BF16)
    s5P["M2"] = dscr("s5M2", [2, 17, 2, 128, 2048], BF16)

    ident = k.sb([128, 128], F32, "ident")
    perm = k.sb([64, 64], F32, "perm")
    ones_f = k.sb([128, 128], F32, "ones_f")
    ones_b = k.sb([128, 128], BF16, "ones_b")
    epsb = k.sb([128, 1], F32, "epsb")
    k.scr = k.sb([128, 4], F32, "scr")
    k.load(sp, ident, ident[:], cident)
    k.load(sp, perm, perm[:], cperm)
    k.op(dve, lambda: nc.vector.memset(ones_f[:], 1.0), writes=(ones_f,))
    k.op(dve, lambda: nc.vector.memset(ones_b[:], 1.0), writes=(ones_b,))
    k.op(dve, lambda: nc.vector.memset(epsb[:], EPS), writes=(epsb,))
    sc_a_mix = k.sb([128, 2, 32], F32, "a_mix")
    sc_sh_mix = k.sb([128, 2, 32], F32, "sh_mix")
    sc_gg_mix = k.sb([128, 2, 32], F32, "gg_mix")
    sc_a_mlp = k.sb([128, 2, 32], F32, "a_mlp")
    sc_sh_mlp = k.sb([128, 2, 32], F32, "sh_mlp")
    sc_gg_mlp = k.sb([128, 2, 32], F32, "gg_mlp")
    gq = k.sb([128, 8], F32, "gq")
    gkv = k.sb([128, 4], F32, "gkv")

    psum = es.enter_context(nc.psum_tensor("psum", [128, 8, 512], F32))
    PS = [Buf(k, psum[:, i, :], "ps%d" % i) for i in range(8)]
    gbank = Ring(PS[0:4])
    PR = PS[4]
    PM = Ring(PS[5:8])

    WR = {}
    st_bf = k.ring(4, [128, TB], BF16, "stbf")

    nslow = dict(allow_slow_non_contiguous=True)

    def gemm(W, KT, tiles, xview, xbufs, cb, ntok=TB, maxw=None):
        maxw = maxw or (8192 // KT)
        Wv = W.rearrange("(t p) n -> p t n", p=128)
        chunks = _chunks(tiles, maxw)

        def issue_load(ch):
            slot = WR['ring'].next()
            cw = sum(t[1] for t in ch)
            view = slot.ap[:, 0:KT * cw].rearrange("p (t c) -> p t c", c=cw)
            c0 = ch[0][0]
            k.load(pool, slot, view, Wv[:, :, c0:c0 + cw])
            return slot, view

        pend = [issue_load(chunks[0])]
        for ci, ch in enumerate(chunks):
            if ci + 1 < len(chunks):
                pend.append(issue_load(chunks[ci + 1]))
            slot, view = pend.pop(0)
            off = 0
            for (c0, M, tag) in ch:
                if ntok <= 512:
                    bank = gbank.next()
                    for kt in range(KT):
                        k.op(pe, lambda kt=kt, off=off, M=M, bank=bank, view=view: nc.tensor.matmul(
                            bank.ap[0:M, 0:ntok], view[:, kt, off:off + M], xview(kt),
                            start=(kt == 0), stop=(kt == KT - 1)),
                            reads=(slot,) + tuple(xbufs), writes=(bank,), signal=(kt == KT - 1))
                    cb(bank, M, tag)
                else:
                    for hf in range(ntok // 512):
                        bank = gbank.next()
                        for kt in range(KT):
                            k.op(pe, lambda kt=kt, off=off, M=M, bank=bank, view=view, hf=hf: nc.tensor.matmul(
                                bank.ap[0:M, 0:512], view[:, kt, off:off + M], xview(kt, hf),
                                start=(kt == 0), stop=(kt == KT - 1)),
                                reads=(slot,) + tuple(xbufs), writes=(bank,), signal=(kt == KT - 1))
                        cb(bank, M, tag, hf)
                off += M

    def rstd_from_bank(bank, F, out_buf, ntok=TB):
        k.op(act, lambda: nc.scalar.activation(out=out_buf.ap[:, 0:ntok], in_=bank.ap[:, 0:ntok], func=AF.Sqrt,
                                               bias=epsb[:, 0:1], scale=1.0 / F),
             reads=(bank, epsb), writes=(out_buf,))
        k.op(dve, lambda: nc.vector.reciprocal(out=out_buf.ap[:, 0:ntok], in_=out_buf.ap[:, 0:ntok]),
             reads=(out_buf,), writes=(out_buf,))

    def sumsq_tile(src_ap, src_buf, first, last, scratch, ntok=TB):
        k.op(act, lambda: nc.scalar.activation(out=scratch.ap[:, 0:ntok], in_=src_ap, func=AF.Square),
             reads=(src_buf,), writes=(scratch,))
        k.op(pe, lambda: nc.tensor.matmul(PR.ap[:, 0:ntok], ones_f[:], scratch.ap[:, 0:ntok], start=first, stop=last),
             reads=(ones_f, scratch), writes=(PR,), signal=True)

    k.s5 = None
    if STOP_AFTER >= 4 and not int(os.environ.get("MK_NOS5", "0")):
        k.s5 = _S5(k, nc, dict(psb=k.sb, ident=ident, PS=PS, PM=PM, gbank=gbank, s5P=s5P))
        k.s5.alloc_persist()
    with contextlib.ExitStack() as ph:
        def psb(shape, dt, name):
            k.nbuf += 1
            t = ph.enter_context(nc.sbuf_tensor("%s_%d" % (name, k.nbuf), list(shape), dt))
            return Buf(k, t, name)
        cT = psb([128, 2, 32], F32, "cT")
        scT = psb([128, 2, 32], BF16, "scT")
        WR['ring'] = Ring([psb([128, 8192], BF16, "wslot%d" % i) for i in range(3)])
        msr = Ring([psb([2, 256], F32, "msr%d" % i) for i in range(3)])
        s5gen = k.s5.pre(s5P) if k.s5 is not None else None
        bmr = Ring([psb([2, 512], F32, "bmr%d" % i) for i in range(3)])
        for j in range(2):
            (k.load if j == 0 else k.loadp)(sp, cT, cT[:, j, :], cvec[j].rearrange("(t p) -> p t", p=128), **nslow)
        k.op(act, lambda: nc.scalar.activation(out=scT[:], in_=cT[:], func=AF.Silu), reads=(cT,), writes=(scT,))
        wmv = w_mod.rearrange("(t p) n -> p t n", p=128)
        CWM = 256
        NCH = 6 * D // CWM

        def ldwm(ci):
            slot = WR['ring'].next()
            view = slot.ap[:, 0:32 * CWM].rearrange("p (t c) -> p t c", c=CWM)
            k.load(pool, slot, view, wmv[:, :, ci * CWM:(ci + 1) * CWM])
            return slot, view
        pend = [ldwm(0), ldwm(1)]
        for ci in range(NCH):
            if ci + 2 < NCH:
                pend.append(ldwm(ci + 2))
            wb, wview = pend.pop(0)
            bank = gbank.next()
            for kt in range(32):
                k.op(pe, lambda kt=kt, wview=wview, bank=bank: nc.tensor.matmul(
                    bank.ap[0:2, 0:CWM], scT[:, :, kt], wview[:, kt, :], start=(kt == 0), stop=(kt == 31)),
                    reads=(wb, scT), writes=(bank,), signal=(kt == 31))
            bm = bmr.next()
            k.load(sp, bm, bm[:, 0:CWM], b_mod[0, ci * CWM:(ci + 1) * CWM].partition_broadcast(2))
            ms = msr.next()
            k.op(dve, lambda bank=bank, ci=ci, bm=bm, ms=ms: nc.vector.tensor_tensor(
                out=ms[:, 0:CWM], in0=bank.ap[0:2, 0:CWM], in1=bm[:, 0:CWM],
                op=ALU.add), reads=(bank, bm), writes=(ms,))
            k.store(sp, ms, modv[:, ci * CWM:(ci + 1) * CWM], ms[:, 0:CWM])
            if s5gen is not None:
                for _ in range(6):
                    next(s5gen, None)
        if s5gen is not None:
            for _ in s5gen:
                pass
        k.dram_barrier()
        modT = psb([128, 2, 6, 32], F32, "modT")
        gT = psb([128, 4, 32], F32, "gT")
        for j in range(2):
            for i in range(6):
                (k.load if (j == 0 and i == 0) else k.loadp)(sp, modT, modT[:, j, i, :],
                                                             modv[j, i * D:(i + 1) * D].rearrange("(t p) -> p t", p=128), **nslow)
        for gi, g in enumerate((g_pre_mix, g_post_mix, g_pre_mlp, g_post_mlp)):
            (k.load if gi == 0 else k.loadp)(sp, gT, gT[:, gi, :], g.rearrange("(t p) -> p t", p=128), **nslow)
        k.load(sp, gq, gq[:], g_q_lat.rearrange("(t p) -> p t", p=128), **nslow)
        k.load(sp, gkv, gkv[:], g_kv_lat.rearrange("(t p) -> p t", p=128), **nslow)
        for j in range(2):
            for (dst, gi, isc) in ((sc_a_mix, 0, 1), (sc_a_mlp, 2, 4)):
                k.op(dve, lambda dst=dst, gi=gi, isc=isc, j=j: nc.vector.scalar_tensor_tensor(
                    out=dst[:, j, :], in0=modT[:, j, isc, :], scalar=1.0, in1=gT[:, gi, :], op0=ALU.add, op1=ALU.mult),
                    reads=(modT, gT), writes=(dst,))
            for (dst, ish) in ((sc_sh_mix, 0), (sc_sh_mlp, 3)):
                k.op(dve, lambda dst=dst, ish=ish, j=j: nc.vector.tensor_copy(out=dst[:, j, :], in_=modT[:, j, ish, :]),
                     reads=(modT,), writes=(dst,))
            for (dst, gi, ig) in ((sc_gg_mix, 1, 2), (sc_gg_mlp, 3, 5)):
                k.op(dve, lambda dst=dst, gi=gi, ig=ig, j=j: nc.vector.tensor_tensor(
                    out=dst[:, j, :], in0=modT[:, j, ig, :], in1=gT[:, gi, :], op=ALU.mult),
                    reads=(modT, gT), writes=(dst,))
        k.phase_barrier((modT, gT, cT, scT) + tuple(bmr.bufs) + tuple(WR['ring'].bufs) + tuple(msr.bufs))

    if STOP_AFTER <= 0:
        k.finish()
        return

    with contextlib.ExitStack() as ph:
        def psb(shape, dt, name):
            k.nbuf += 1
            t = ph.enter_context(nc.sbuf_tensor("%s_%d" % (name, k.nbuf), list(shape), dt))
            return Buf(k, t, name)
        WR['ring'] = Ring([psb([128, 8192], BF16, "wslot%d" % i) for i in range(3)])
        xtok = Ring([psb([128, D], F32, "xtok%d" % i) for i in range(1)])
        xTst = Ring([psb([128, 32, 128], F32, "xTst%d" % i) for i in range(1)])
        hT = psb([128, 32, TB], BF16, "hT")
        Rsb = psb([128, TB], F32, "Rsb")
        R2 = psb([128, TB], F32, "R2")
        tmpf = Ring([psb([128, 128], F32, "tmpf%d" % i) for i in range(2)])
        qlat = psb([128, 8, TB], F32, "qlat")
        qn = psb([128, 8, TB], BF16, "qn")
        ckvf = psb([128, 4, TB], F32, "ckvf")
        krf = psb([64, TB], F32, "krf")
        cs_t = psb([64, 2, TB], F32, "cs_t")
        sq = Ring([psb([128, TB], F32, "sq%d" % i) for i in range(2)])
        ropef = Ring([psb([64, TB], F32, "ropef%d" % i) for i in range(2)])
        t1r = Ring([psb([64, TB], F32, "t1r%d" % i) for i in range(2)])
        t2r = Ring([psb([64, TB], F32, "t2r%d" % i) for i in range(2)])
        tokst = Ring([psb([128, NKV], F32, "tokst%d" % i) for i in range(2)])

        def rope(src, dst_st):
            bank = PM.next()
            k.op(pe, lambda: nc.tensor.matmul(bank.ap[0:64, :], perm[:], src.ap[0:64, :], start=True, stop=True),
                 reads=(perm, src), writes=(bank,))
            t1, t2 = t1r.next(), t2r.next()
            k.op(dve, lambda: nc.vector.tensor_tensor(out=t1[:], in0=src.ap[0:64, :], in1=cs_t[:, 0, :], op=ALU.mult),
                 reads=(src, cs_t), writes=(t1,))
            k.op(dve, lambda: nc.vector.tensor_tensor(out=t2[:], in0=bank.ap[0:64, :], in1=cs_t[:, 1, :], op=ALU.mult),
                 reads=(bank, cs_t), writes=(t2,))
            k.op(dve, lambda: nc.vector.tensor_tensor(out=dst_st.ap[0:64, :], in0=t1[:], in1=t2[:], op=ALU.add),
                 reads=(t1, t2), writes=(dst_st,))

        for blk in range(T5 // TB):
            tok0 = blk * TB
            own = tok0 >= NOTH
            prompt = tok0 >= NOTH + NOWN
            j = 1 if prompt else 0
            o0 = tok0 - NOTH
            kv0 = NCTX + tok0
            if not prompt:
                k.load(sp, cs_t, cs_t[:, 0, :], cosT[:, tok0:tok0 + TB])
                k.loadp(sp, cs_t, cs_t[:, 1, :], sinT[:, tok0:tok0 + TB])
            for tt in range(4):
                xt = xtok.next()
                k.load(sp, xt, xt[:], x_all[tok0 + tt * 128: tok0 + (tt + 1) * 128, :])
                xs = xTst.next()
                for t in range(32):
                    bank = PM.next()
                    k.op(pe, lambda xt=xt, t=t, bank=bank: nc.tensor.transpose(
                        bank.ap[:, 0:128], xt[:, t * 128:(t + 1) * 128], ident[:]),
                        reads=(xt, ident), writes=(bank,))
                    k.op(act, lambda xs=xs, t=t, bank=bank: nc.scalar.copy(out=xs[:, t, :], in_=bank.ap[:, 0:128]),
                         reads=(bank,), writes=(xs,) if t == 0 else ())
                    s = sq.next()
                    k.op(act, lambda s=s, bank=bank: nc.scalar.activation(out=s.ap[:, 0:128], in_=bank.ap[:, 0:128], func=AF.Square),
                         reads=(bank,), writes=(s,))
                    k.op(pe, lambda s=s, t=t: nc.tensor.matmul(PR.ap[:, 0:128], ones_f[:], s.ap[:, 0:128], start=(t == 0), stop=(t == 31)),
                         reads=(ones_f, s), writes=(PR,), signal=True)
                xs.w = {act.key: Tok(act.key, act.sem, act.n)}
                rstd_from_bank(PR, D, Rsb, ntok=128)
                for t in range(32):
                    tf = tmpf.next()
                    k.op(dve, lambda tf=tf, xs=xs, t=t: nc.vector.tensor_tensor(
                        out=tf[:], in0=xs[:, t, :], in1=Rsb[:, 0:128], op=ALU.mult),
                        reads=(xs, Rsb), writes=(tf,))
                    k.op(act, lambda tf=tf, t=t, tt=tt, j=j: nc.scalar.activation(
                        out=hT[:, t, tt * 128:(tt + 1) * 128], in_=tf[:], func=AF.Identity,
                        bias=sc_sh_mix[:, j, t:t + 1], scale=sc_a_mix[:, j, t:t + 1]),
                        reads=(tf, sc_sh_mix, sc_a_mix), writes=(hT,) if (t == 0 and tt == 0) else ())
                if own:
                    k.store(sp, xs, xT.rearrange("(t p) n -> p t n", p=128)[:, :, o0 + tt * 128:o0 + (tt + 1) * 128], xs[:])
            hT.w = {act.key: Tok(act.key, act.sem, act.n)}
            hT.r = {}

            tiles = []
            if own:
                tiles += [(i * 128, 128, ("ql", i)) for i in range(8)]
            tiles += [(NQ + i * 128, 128, ("ckv", i)) for i in range(4)]
            tiles += [(NQ + NKV, 64, ("kr", 0))]
            tiles += [(NQ + NKV + NR + i * 128, 128, ("u", i)) for i in range(16)]
            if own:
                tiles += [(NQ + NKV + NR + NU + i * 128, 128, ("ga", i)) for i in range(32)]
                tiles += [(NQ + NKV + NR + NU + D + i * 128, 128, ("gb", i)) for i in range(32)]

            def cb(bank, M, tag, tok0=tok0, o0=o0):
                kind, i = tag
                if kind == "ql":
                    k.op(act, lambda: nc.scalar.copy(out=qlat[:, i, :], in_=bank.ap[:, :]), reads=(bank,),
                         writes=(qlat,) if i == 0 else ())
                elif kind == "ckv":
                    k.op(act, lambda: nc.scalar.copy(out=ckvf[:, i, :], in_=bank.ap[:, :]), reads=(bank,),
                         writes=(ckvf,) if i == 0 else ())
                elif kind == "kr":
                    k.op(act, lambda: nc.scalar.copy(out=krf[:, :], in_=bank.ap[0:64, :]), reads=(bank,), writes=(krf,))
                elif kind == "u":
                    st = st_bf.next()
                    k.op(act, lambda: nc.scalar.copy(out=st[:], in_=bank.ap[:, :]), reads=(bank,), writes=(st,))
                    k.store(sp, st, uT[i * 128:(i + 1) * 128, tok0:tok0 + TB], st[:])
                else:
                    st = st_bf.next()
                    k.op(act, lambda: nc.scalar.activation(out=st[:], in_=bank.ap[:, :], func=AF.Sigmoid),
                         reads=(bank,), writes=(st,))
                    dstT = sgaT if kind == "ga" else sgbT
                    k.store(sp, st, dstT[i * 128:(i + 1) * 128, o0:o0 + TB], st[:])

            gemm(w_in, 32, tiles, lambda kt: hT[:, kt, :], (hT,), cb)
            if own:
                qlat.w = {act.key: Tok(act.key, act.sem, act.n)}
            ckvf.w = {act.key: Tok(act.key, act.sem, act.n)}

            for i in range(4):
                s = sq.next()
                sumsq_tile(ckvf[:, i, :], ckvf, i == 0, i == 3, s)
            rstd_from_bank(PR, NKV, R2)
            for i in range(4):
                k.op(dve, lambda i=i: nc.vector.scalar_tensor_tensor(
                    out=ckvf[:, i, :], in0=ckvf[:, i, :], scalar=gkv[:, i:i + 1], in1=R2[:], op0=ALU.mult, op1=ALU.mult),
                    reads=(ckvf, gkv, R2), writes=(ckvf,))
                st = st_bf.next()
                k.op(act, lambda i=i, st=st: nc.scalar.copy(out=st[:], in_=ckvf[:, i, :]), reads=(ckvf,), writes=(st,))
                k.store(sp, st, ckvT[i * 128:(i + 1) * 128, kv0:kv0 + TB], st[:])
            if prompt:
                p0 = tok0 - NOTH - NOWN
                for tt in range(4):
                    bank = PM.next()
                    for i in range(4):
                        k.op(pe, lambda i=i, tt=tt, bank=bank: nc.tensor.transpose(
                            bank.ap[:, i * 128:(i + 1) * 128], ckvf[:, i, tt * 128:(tt + 1) * 128], ident[:]),
                            reads=(ckvf, ident), writes=(bank,), signal=(i == 3))
                    ts = tokst.next()
                    k.op(act, lambda ts=ts, bank=bank: nc.scalar.copy(out=ts[:], in_=bank.ap[:, :]), reads=(bank,), writes=(ts,))
                    k.store(sp, ts, ckv_new[p0 + tt * 128:p0 + (tt + 1) * 128, :], ts[:])
            st = st_bf.next()
            if prompt:
                k.op(act, lambda st=st: nc.scalar.copy(out=st.ap[0:64, :], in_=krf[:, :]), reads=(krf,), writes=(st,))
                for tt in range(4):
                    bank = PM.next()
                    k.op(pe, lambda tt=tt, bank=bank: nc.tensor.transpose(
                        bank.ap[:, 0:64], krf[:, tt * 128:(tt + 1) * 128], ident[0:64, 0:64]),
                        reads=(krf, ident), writes=(bank,))
                    ts = tokst.next()
                    k.op(act, lambda ts=ts, bank=bank: nc.scalar.copy(out=ts.ap[:, 0:64], in_=bank.ap[:, 0:64]),
                         reads=(bank,), writes=(ts,))
                    k.store(sp, ts, krope_new[p0 + tt * 128:p0 + (tt + 1) * 128, :], ts.ap[:, 0:64])
            else:
                rope(krf, st)
            k.store(sp, st, kropeT[:, kv0:kv0 + TB], st.ap[0:64, :])

            if own:
                for i in range(8):
                    s = sq.next()
                    sumsq_tile(qlat[:, i, :], qlat, i == 0, i == 7, s)
                rstd_from_bank(PR, NQ, R2)
                for i in range(8):
                    k.op(dve, lambda i=i: nc.vector.scalar_tensor_tensor(
                        out=qn[:, i, :], in0=qlat[:, i, :], scalar=gq[:, i:i + 1], in1=R2[:], op0=ALU.mult, op1=ALU.mult),
                        reads=(qlat, gq, R2), writes=(qn,) if i == 0 else ())
                qn.w = {dve.key: Tok(dve.key, dve.sem, dve.n)}
                qtiles = []
                for h in range(H):
                    qtiles += [(h * 192, 128, ("qn", h)), (h * 192 + 128, 64, ("qr", h))]

                def cbq(bank, M, tag, o0=o0):
                    kind, h = tag
                    st = st_bf.next()
                    if kind == "qn":
                        k.op(act, lambda: nc.scalar.copy(out=st[:], in_=bank.ap[:, :]), reads=(bank,), writes=(st,))
                        k.store(sp, st, qT[h * 192:h * 192 + 128, o0:o0 + TB], st[:])
                    else:
                        if prompt:
                            k.op(act, lambda: nc.scalar.copy(out=st.ap[0:64, :], in_=bank.ap[0:64, :]), reads=(bank,), writes=(st,))
                        else:
                            rf = ropef.next()
                            k.op(act, lambda: nc.scalar.copy(out=rf[:], in_=bank.ap[0:64, :]), reads=(bank,), writes=(rf,))
                            rope(rf, st)
                        k.store(sp, st, qT[h * 192 + 128:h * 192 + 192, o0:o0 + TB], st.ap[0:64, :])

                gemm(w_uq, 8, qtiles, lambda kt: qn[:, kt, :], (qn,), cbq, maxw=768)
        k.phase_barrier(tuple(xtok.bufs) + (hT, Rsb, R2, qlat, qn, ckvf, krf, cs_t) + tuple(xTst.bufs)
                        + tuple(tmpf.bufs) + tuple(sq.bufs) + tuple(ropef.bufs) + tuple(t1r.bufs) + tuple(t2r.bufs) + tuple(tokst.bufs)
                        + tuple(WR['ring'].bufs))


    if STOP_AFTER <= 1:
        k.finish()
        return
    k.dram_barrier()
    NPS = NPR // 256
    SKV = NCTX + NOTH + NOWN

    with contextlib.ExitStack() as ph:
        def psb(shape, dt, name):
            k.nbuf += 1
            t = ph.enter_context(nc.sbuf_tensor("%s_%d" % (name, k.nbuf), list(shape), dt))
            return Buf(k, t, name)
        allb = []

        def prg(n, shape, dt, name):
            r = Ring([psb(shape, dt, name + str(i)) for i in range(n)])
            allb.extend(r.bufs)
            return r
        ctok = prg(2, [128, NKV], F32, "ctok")
        krtok = prg(2, [128, NR], F32, "krtok")
        ckv_sb = prg(1, [128, 4, SKV], BF16, "ckv_sb").bufs[0]
        kr_sb = prg(1, [64, SKV], BF16, "kr_sb").bufs[0]
        KhTr = prg(2, [128, SKV], BF16, "KhT")
        Vhr = prg(2, [128, SKV // 128, 128], BF16, "Vh")
        wkvr = prg(2, [128, 4, 256], BF16, "wkv")
        qnr = prg(2, [128, NOWN], BF16, "qnh")
        qrr = prg(2, [64, NOWN], BF16, "qrh")
        PTr = prg(3, [128, TB], BF16, "PT")
        recr = prg(2, [128, TB], F32, "rec")
        ofr = prg(2, [128, TB], F32, "of")
        sgar = prg(2, [128, TB], BF16, "sga")
        sacc = prg(2, [128, TB], F32, "sacc")
        PO = Ring([PS[5], PS[6]])
        PSUMS = Ring([PS[7], PS[4]])
        for t4 in range(NCTX // 128):
            ct = ctok.next()
            k.load(sp, ct, ct[:], ckv_ctx[t4 * 128:(t4 + 1) * 128, :])
            bank = gbank.next()
            for i in range(4):
                k.op(pe, lambda i=i, ct=ct, bank=bank: nc.tensor.transpose(bank.ap[:, i * 128:(i + 1) * 128], ct[:, i * 128:(i + 1) * 128], ident[:]),
                     reads=(ct, ident), writes=(bank,), signal=(i == 3))
            st = st_bf.next()
            k.op(act, lambda st=st, bank=bank: nc.scalar.copy(out=st[:], in_=bank.ap[:, :]), reads=(bank,), writes=(st,))
            k.store(sp, st, ckvT.rearrange("(i p) n -> p i n", p=128)[:, :, t4 * 128:(t4 + 1) * 128],
                    st.ap[:, :].rearrange("p (i n) -> p i n", n=128))
            kt_ = krtok.next()
            k.load(sp, kt_, kt_[:], krope_ctx[t4 * 128:(t4 + 1) * 128, :])
            bank = gbank.next()
            k.op(pe, lambda kt_=kt_, bank=bank: nc.tensor.transpose(bank.ap[0:64, 0:128], kt_[:, :], ident[:]),
                 reads=(kt_, ident), writes=(bank,))
            st = st_bf.next()
            k.op(act, lambda st=st, bank=bank: nc.scalar.copy(out=st.ap[0:64, 0:128], in_=bank.ap[0:64, 0:128]), reads=(bank,), writes=(st,))
            k.store(sp, st, kropeT[:, t4 * 128:(t4 + 1) * 128], st.ap[0:64, 0:128])
        k.dram_barrier()
        SCALE = 192.0 ** -0.5
        wkv_v = w_ukv.rearrange("(t p) n -> p t n", p=128)

        def attn_seq(q0, Lq, kv0, Lk):
            nkt = Lk // 128
            k.load(sp, ckv_sb, ckv_sb[:, :, 0:Lk], ckvT.rearrange("(i p) n -> p i n", p=128)[:, :, kv0:kv0 + Lk])
            k.load(sp, kr_sb, kr_sb[:, 0:Lk], kropeT[:, kv0:kv0 + Lk])
            for h in range(H):
                wkv = wkvr.next()
                k.load(pool, wkv, wkv[:], wkv_v[:, :, h * 256:(h + 1) * 256])
                qn_, qr_ = qnr.next(), qrr.next()
                k.load(sp, qn_, qn_[:, 0:Lq], qT[h * 192:h * 192 + 128, q0:q0 + Lq])
                k.load(sp, qr_, qr_[:, 0:Lq], qT[h * 192 + 128:h * 192 + 192, q0:q0 + Lq])
                KhT, Vh = KhTr.next(), Vhr.next()
                for kb in range(0, Lk, 512):
                    w = min(512, Lk - kb)
                    bank = gbank.next()
                    for kt in range(4):
                        k.op(pe, lambda kt=kt, kb=kb, w=w, bank=bank, wkv=wkv: nc.tensor.matmul(
                            bank.ap[:, 0:w], wkv[:, kt, 0:128], ckv_sb[:, kt, kb:kb + w], start=(kt == 0), stop=(kt == 3)),
                            reads=(wkv, ckv_sb), writes=(bank,), signal=(kt == 3))
                    k.op(act, lambda kb=kb, w=w, bank=bank, KhT=KhT: nc.scalar.copy(out=KhT[:, kb:kb + w], in_=bank.ap[:, 0:w]),
                         reads=(bank,), writes=(KhT,) if kb == 0 else ())
                KhT.w = {act.key: Tok(act.key, act.sem, act.n)}
                for g4 in range(0, nkt, 4):
                    n4 = min(4, nkt - g4)
                    bank = gbank.next()
                    for i in range(n4):
                        for kt in range(4):
                            k.op(pe, lambda kt=kt, i=i, g4=g4, bank=bank, wkv=wkv: nc.tensor.matmul(
                                bank.ap[:, i * 128:(i + 1) * 128], ckv_sb[:, kt, (g4 + i) * 128:(g4 + i + 1) * 128], wkv[:, kt, 128:256],
                                start=(kt == 0), stop=(kt == 3)),
                                reads=(wkv, ckv_sb), writes=(bank,), signal=(kt == 3 and i == n4 - 1))
                    k.op(dve, lambda g4=g4, n4=n4, bank=bank, Vh=Vh: nc.vector.tensor_copy(
                        out=Vh[:, g4:g4 + n4, :], in_=bank.ap[:, 0:n4 * 128].rearrange("p (i n) -> p i n", n=128)),
                        reads=(bank,), writes=(Vh,) if g4 == 0 else ())
                Vh.w = {dve.key: Tok(dve.key, dve.sem, dve.n)}
                for qb in range(0, Lq, TB):
                    nq = min(TB, Lq - qb)
                    po, psm = PO.next(), PSUMS.next()
                    sa = sacc.next()

                    def s_mm(kt):
                        bank = gbank.next()
                        k.op(pe, lambda: nc.tensor.matmul(bank.ap[:, 0:nq], KhT[:, kt * 128:(kt + 1) * 128], qn_[:, qb:qb + nq],
                                                          start=True, stop=False),
                             reads=(KhT, qn_), writes=(bank,), signal=False)
                        k.op(pe, lambda: nc.tensor.matmul(bank.ap[:, 0:nq], kr_sb[:, kt * 128:(kt + 1) * 128], qr_[:, qb:qb + nq],
                                                          start=False, stop=True),
                             reads=(kr_sb, qr_), writes=(bank,), signal=True)
                        pt = PTr.next()
                        k.op(act, lambda: nc.scalar.activation(out=pt[:, 0:nq], in_=bank.ap[:, 0:nq], func=AF.Exp, scale=SCALE),
                             reads=(bank,), writes=(pt,))
                        return pt
                    pts = [s_mm(0)]
                    for kt in range(nkt):
                        if kt + 1 < nkt:
                            pts.append(s_mm(kt + 1))
                        pt = pts.pop(0)
                        k.op(pe, lambda kt=kt, pt=pt: nc.tensor.matmul(po.ap[:, 0:nq], Vh[:, kt, :], pt[:, 0:nq],
                                                                     start=(kt == 0), stop=(kt == nkt - 1)),
                             reads=(Vh, pt), writes=(po,), signal=True)
                        if kt == 0:
                            k.op(dve, lambda pt=pt: nc.vector.tensor_copy(out=sa[:, 0:nq], in_=pt[:, 0:nq]), reads=(pt,), writes=(sa,))
                        else:
                            k.op(dve, lambda pt=pt: nc.vector.tensor_tensor(out=sa[:, 0:nq], in0=sa[:, 0:nq], in1=pt[:, 0:nq], op=ALU.add),
                                 reads=(pt, sa), writes=(sa,))
                    k.op(pe, lambda: nc.tensor.matmul(psm.ap[:, 0:nq], ones_f[:], sa[:, 0:nq], start=True, stop=True),
                         reads=(ones_f, sa), writes=(psm,), signal=True)
                    rec, of, sga = recr.next(), ofr.next(), sgar.next()
                    k.load(sp, sga, sga[:, 0:nq], sgaT[h * 128:(h + 1) * 128, q0 + qb:q0 + qb + nq])
                    k.op(dve, lambda: nc.vector.reciprocal(out=rec[:, 0:nq], in_=psm.ap[:, 0:nq]), reads=(psm,), writes=(rec,))
                    k.op(dve, lambda: nc.vector.tensor_tensor(out=of[:, 0:nq], in0=po.ap[:, 0:nq], in1=rec[:, 0:nq], op=ALU.mult),
                         reads=(po, rec), writes=(of,))
                    st = st_bf.next()
                    k.op(dve, lambda: nc.vector.tensor_tensor(out=st[:, 0:nq], in0=of[:, 0:nq], in1=sga[:, 0:nq], op=ALU.mult),
                         reads=(of, sga), writes=(st,))
                    k.store(sp, st, maT[h * 128:(h + 1) * 128, q0 + qb:q0 + qb + nq], st[:, 0:nq])

        attn_seq(0, NOWN, 0, SKV)
        for i in range(NPS):
            attn_seq(NOWN + 256 * i, 256, SKV + 256 * i, 256)
        k.phase_barrier(allb)

    if STOP_AFTER <= 3:
        k.finish()
        return
    k.dram_barrier()

    with contextlib.ExitStack() as ph:
        def psb(shape, dt, name):
            k.nbuf += 1
            t = ph.enter_context(nc.sbuf_tensor("%s_%d" % (name, k.nbuf), list(shape), dt))
            return Buf(k, t, name)
        allb = []

        def prg(n, shape, dt, name):
            r = Ring([psb(shape, dt, name + str(i)) for i in range(n)])
            allb.extend(r.bufs)
            return r
        s5_ctx = _s5_tile(k, nc, locals()) if k.s5 is not None else None
        Dv = prg(1, [128, 16], F32, "Dv").bufs[0]
        k.load(sp, Dv, Dv[:], s5_d.rearrange("(t p) -> p t", p=128), **nslow)
        u_sb = prg(2, [128, T5], BF16, "u_sb")
        yacc = prg(2, [128, TO], F32, "yacc")
        gt1 = prg(1, [128, TO], F32, "gt1").bufs[0]
        gt2 = prg(1, [128, TO], F32, "gt2").bufs[0]
        yst = prg(1, [128, TO], BF16, "yst")
        def gelu_store(G, ya):
            k.op(act, lambda ya=ya: nc.scalar.activation(out=gt1[:], in_=ya[:], func=AF.Square), reads=(ya,), writes=(gt1,))
            k.op(dve, lambda: nc.vector.tensor_scalar(out=gt1[:], in0=gt1[:], scalar1=0.044715, scalar2=1.0, op0=ALU.mult, op1=ALU.add),
                 reads=(gt1,), writes=(gt1,))
            k.op(dve, lambda ya=ya: nc.vector.tensor_tensor(out=gt2[:], in0=gt1[:], in1=ya[:], op=ALU.mult), reads=(gt1, ya), writes=(gt2,))
            k.op(act, lambda: nc.scalar.activation(out=gt2[:], in_=gt2[:], func=AF.Sigmoid, scale=1.5957691216), reads=(gt2,), writes=(gt2,))
            ys = yst.next()
            k.op(dve, lambda ya=ya, ys=ys: nc.vector.tensor_tensor(out=ys[:], in0=gt2[:], in1=ya[:], op=ALU.mult), reads=(gt2, ya), writes=(ys,))
            k.store(sp, ys, yT[G * 128:(G + 1) * 128, :], ys[:])

        prev = None
        for G in range(16):
            ub = u_sb.next()
            k.load(sp, ub, ub[:], uT[G * 128:(G + 1) * 128, :])
            ya = yacc.next()
            k.op(dve, lambda ub=ub, ya=ya, G=G: nc.vector.tensor_scalar(out=ya[:], in0=ub[:, NOTH:T5], scalar1=Dv[:, G:G + 1],
                                                                      scalar2=None, op0=ALU.mult),
                 reads=(ub, Dv), writes=(ya,))
            for d in range(2 if s5_ctx is not None else 0):
                s5_ctx.stage(1, G, d, ub, ya)
                if prev is not None:
                    s5_ctx.stage(2, *prev)
                    if prev[1] == 1:
                        gelu_store(prev[0], prev[3])
                prev = (G, d, ub, ya)
            if s5_ctx is None:
                gelu_store(G, ya)
        if prev is not None:
            s5_ctx.stage(2, *prev)
            gelu_store(prev[0], prev[3])
        if s5_ctx is not None:
            s5_ctx.finish()
        k.phase_barrier(allb + (s5_ctx.bufs if s5_ctx is not None else []))

    if STOP_AFTER <= 4:
        k.finish()
        return
    k.dram_barrier()

    with contextlib.ExitStack() as ph:
        def psb(shape, dt, name):
            k.nbuf += 1
            t = ph.enter_context(nc.sbuf_tensor("%s_%d" % (name, k.nbuf), list(shape), dt))
            return Buf(k, t, name)
        allb = []

        def prg(n, shape, dt, name):
            r = Ring([psb(shape, dt, name + str(i)) for i in range(n)])
            allb.extend(r.bufs)
            return r
        WR['ring'] = prg(3, [128, 8192], BF16, "wslot")
        yb = prg(1, [128, 16, TB], BF16, "yb").bufs[0]
        zar = prg(4, [128, TB], F32, "za")
        sigr = prg(2, [128, TB], F32, "sig")
        sgbr = prg(2, [128, TB], BF16, "sgb")
        mar = prg(2, [128, TB], BF16, "ma")
        for blk in range(TO // TB):
            o0 = blk * TB
            k.load(sp, yb, yb[:], yT.rearrange("(t p) n -> p t n", p=128)[:, :, o0:o0 + TB])
            tiles = []
            for i in range(0, 32, 2):
                tiles += [(i * 128, 128, ("za", i)), ((i + 1) * 128, 128, ("za", i + 1)),
                          (D + i * 128, 128, ("zb", i)), (D + (i + 1) * 128, 128, ("zb", i + 1))]
            zas = {}

            def cb(bank, M, tag, o0=o0):
                kind, i = tag
                if kind == "za":
                    z = zar.next()
                    k.op(act, lambda: nc.scalar.copy(out=z[:], in_=bank.ap[:, :]), reads=(bank,), writes=(z,))
                    zas[i] = z
                else:
                    z = zas.pop(i)
                    sg = sigr.next()
                    k.op(act, lambda: nc.scalar.activation(out=sg[:], in_=bank.ap[:, :], func=AF.Sigmoid), reads=(bank,), writes=(sg,))
                    sgb, ma = sgbr.next(), mar.next()
                    k.load(sp, sgb, sgb[:], sgbT[i * 128:(i + 1) * 128, o0:o0 + TB])
                    k.load(sp, ma, ma[:], maT[i * 128:(i + 1) * 128, o0:o0 + TB])
                    k.op(dve, lambda: nc.vector.tensor_tensor(out=sg[:], in0=sg[:], in1=z[:], op=ALU.mult), reads=(sg, z), writes=(sg,))
                    k.op(dve, lambda: nc.vector.tensor_tensor(out=sg[:], in0=sg[:], in1=sgb[:], op=ALU.mult), reads=(sg, sgb), writes=(sg,))
                    st = st_bf.next()
                    k.op(dve, lambda: nc.vector.tensor_tensor(out=st[:], in0=sg[:], in1=ma[:], op=ALU.add), reads=(sg, ma), writes=(st,))
                    k.store(sp, st, mT[i * 128:(i + 1) * 128, o0:o0 + TB], st[:])

            gemm(w_glu, 16, tiles, lambda kt: yb[:, kt, :], (yb,), cb, maxw=256)
        k.phase_barrier(allb)

    if STOP_AFTER <= 5:
        k.finish()
        return
    k.dram_barrier()

    with contextlib.ExitStack() as ph:
        def psb(shape, dt, name):
            k.nbuf += 1
            t = ph.enter_context(nc.sbuf_tensor("%s_%d" % (name, k.nbuf), list(shape), dt))
            return Buf(k, t, name)
        allb = []

        def prg(n, shape, dt, name):
            r = Ring([psb(shape, dt, name + str(i)) for i in range(n)])
            allb.extend(r.bufs)
            return r
        WR['ring'] = prg(3, [128, 8192], BF16, "wslot")
        actb = prg(1, [128, 32, TB], BF16, "actb").bufs[0]
        acc = prg(1, [128, 32, TB], F32, "acc").bufs[0]
        Rsb = prg(1, [128, TB], F32, "Rsb5").bufs[0]
        sq = prg(2, [128, TB], F32, "sq5")
        xtr = prg(2, [128, TB], F32, "xt5")
        tfr = prg(2, [128, TB], F32, "tf5")
        ytok = prg(1, [128, D], F32, "ytok").bufs[0]

        def acc_sumsq(i):
            s_ = sq.next()
            sumsq_tile(acc[:, i, :], acc, i == 0, i == 31, s_)

        def resid(j, ggv, srcT, store_to):
            for i in range(32):
                xt = xtr.next()
                k.load(sp, xt, xt[:], srcT[i * 128:(i + 1) * 128, o0:o0 + TB])
                k.op(dve, lambda i=i: nc.vector.scalar_tensor_tensor(out=acc[:, i, :], in0=acc[:, i, :], scalar=ggv[:, j, i:i + 1],
                                                                   in1=Rsb[:], op0=ALU.mult, op1=ALU.mult),
                     reads=(acc, ggv, Rsb), writes=(acc,))
                k.op(dve, lambda i=i, xt=xt: nc.vector.tensor_tensor(out=acc[:, i, :], in0=acc[:, i, :], in1=xt[:], op=ALU.add),
                     reads=(acc, xt), writes=(acc,))
                if store_to is not None:
                    k.store(sp, acc, store_to[i * 128:(i + 1) * 128, o0:o0 + TB], acc[:, i, :])

        for blk in range(TO // TB):
            o0 = blk * TB
            j = 0 if o0 < NOWN else 1
            k.load(sp, actb, actb[:], mT.rearrange("(t p) n -> p t n", p=128)[:, :, o0:o0 + TB])

            def cb_out(bank, M, tag):
                i = tag
                k.op(act, lambda: nc.scalar.copy(out=acc[:, i, :], in_=bank.ap[:, :]), reads=(bank,), writes=(acc,) if i == 0 else ())
            gemm(w_out, 32, [(i * 128, 128, i) for i in range(32)], lambda kt: actb[:, kt, :], (actb,), cb_out)
            acc.w = {act.key: Tok(act.key, act.sem, act.n)}
            for i in range(32):
                acc_sumsq(i)
            rstd_from_bank(PR, D, Rsb)
            resid(j, sc_gg_mix, xT, x1T)
            for i in range(32):
                acc_sumsq(i)
            rstd_from_bank(PR, D, Rsb)
            for i in range(32):
                tf = tfr.next()
                k.op(dve, lambda i=i, tf=tf: nc.vector.tensor_tensor(out=tf[:], in0=acc[:, i, :], in1=Rsb[:], op=ALU.mult),
                     reads=(acc, Rsb), writes=(tf,))
                st = st_bf.next()
                k.op(act, lambda i=i, tf=tf, j=j, st=st: nc.scalar.activation(out=st[:], in_=tf[:], func=AF.Identity,
                                                                             bias=sc_sh_mlp[:, j, i:i + 1], scale=sc_a_mlp[:, j, i:i + 1]),
                     reads=(tf, sc_sh_mlp, sc_a_mlp), writes=(st,))
                k.store(sp, st, h2T[i * 128:(i + 1) * 128, o0:o0 + TB], st[:])

        k.dram_barrier()
        k.phase_barrier(allb)
    with contextlib.ExitStack() as ph:
        def psb(shape, dt, name):
            k.nbuf += 1
            t = ph.enter_context(nc.sbuf_tensor("%s_%d" % (name, k.nbuf), list(shape), dt))
            return Buf(k, t, name)
        allb = []

        def prg(n, shape, dt, name):
            r = Ring([psb(shape, dt, name + str(i)) for i in range(n)])
            allb.extend(r.bufs)
            return r
        WR['ring'] = prg(3, [128, 8192], BF16, "wslot")
        TB1 = 1024 if TO % 1024 == 0 else 512
        h2b = prg(1, [128, 32, TB1], BF16, "h2b").bufs[0]
        tfr = prg(3, [128, TB], F32, "tf6")
        for blk in range(TO // TB1):
            o0 = blk * TB1
            k.load(sp, h2b, h2b[:], h2T.rearrange("(t p) n -> p t n", p=128)[:, :, o0:o0 + TB1])

            def cb_ff1(bank, M, tag, hf=0, o0=o0):
                i = tag
                tf = tfr.next()
                k.op(act, lambda: nc.scalar.activation(out=tf[:], in_=bank.ap[:, :], func=AF.Relu), reads=(bank,), writes=(tf,))
                st = st_bf.next()
                k.op(dve, lambda: nc.vector.tensor_tensor(out=st[:], in0=tf[:], in1=tf[:], op=ALU.mult), reads=(tf,), writes=(st,))
                k.store(sp, st, a1T[i * 128:(i + 1) * 128, o0 + hf * 512:o0 + hf * 512 + 512], st[:])
            if TB1 == 1024:
                gemm(w_ff1, 32, [(i * 128, 128, i) for i in range(DFF // 128)], lambda kt, hf: h2b[:, kt, hf * 512:(hf + 1) * 512], (h2b,), cb_ff1, ntok=1024)
            else:
                gemm(w_ff1, 32, [(i * 128, 128, i) for i in range(DFF // 128)], lambda kt: h2b[:, kt, :], (h2b,), cb_ff1)
        k.dram_barrier()
        k.phase_barrier(allb)
    with contextlib.ExitStack() as ph:
        def psb(shape, dt, name):
            k.nbuf += 1
            t = ph.enter_context(nc.sbuf_tensor("%s_%d" % (name, k.nbuf), list(shape), dt))
            return Buf(k, t, name)
        allb = []

        def prg(n, shape, dt, name):
            r = Ring([psb(shape, dt, name + str(i)) for i in range(n)])
            allb.extend(r.bufs)
            return r
        WR['ring'] = prg(3, [128, 8192], BF16, "wslot")
        a1r = prg(2, [128, 16, TB], BF16, "a1r")
        acc = prg(1, [128, 32, TB], F32, "acc").bufs[0]
        Rsb = prg(1, [128, TB], F32, "Rsb5").bufs[0]
        sq = prg(2, [128, TB], F32, "sq5")
        xtr = prg(2, [128, TB], F32, "xt5")
        ytok = prg(1, [128, D], F32, "ytok").bufs[0]

        def acc_sumsq(i):
            s_ = sq.next()
            sumsq_tile(acc[:, i, :], acc, i == 0, i == 31, s_)

        def resid(j, ggv, srcT, store_to):
            for i in range(32):
                xt = xtr.next()
                k.load(sp, xt, xt[:], srcT[i * 128:(i + 1) * 128, o0:o0 + TB])
                k.op(dve, lambda i=i: nc.vector.scalar_tensor_tensor(out=acc[:, i, :], in0=acc[:, i, :], scalar=ggv[:, j, i:i + 1],
                                                                   in1=Rsb[:], op0=ALU.mult, op1=ALU.mult),
                     reads=(acc, ggv, Rsb), writes=(acc,))
                k.op(dve, lambda i=i, xt=xt: nc.vector.tensor_tensor(out=acc[:, i, :], in0=acc[:, i, :], in1=xt[:], op=ALU.add),
                     reads=(acc, xt), writes=(acc,))

        for blk in range(TO // TB):
            o0 = blk * TB
            j = 0 if o0 < NOWN else 1
            for kc in range(8):
                ab = a1r.next()
                k.load(sp, ab, ab[:], a1T.rearrange("(t p) n -> p t n", p=128)[:, kc * 16:(kc + 1) * 16, o0:o0 + TB])

                def cb_ff2(bank, M, tag, kc=kc):
                    i = tag
                    if kc == 0:
                        k.op(act, lambda: nc.scalar.copy(out=acc[:, i, :], in_=bank.ap[:, :]), reads=(bank,), writes=(acc,))
                    else:
                        k.op(dve, lambda: nc.vector.tensor_tensor(out=acc[:, i, :], in0=acc[:, i, :], in1=bank.ap[:, :], op=ALU.add),
                             reads=(acc, bank), writes=(acc,))
                gemm(w_ff2[kc * 2048:(kc + 1) * 2048, :], 16, [(i * 128, 128, i) for i in range(32)], lambda kt, ab=ab: ab[:, kt, :], (ab,), cb_ff2)
            for i in range(32):
                acc_sumsq(i)
            rstd_from_bank(PR, D, Rsb)
            resid(j, sc_gg_mlp, x1T, None)
            for tt in range(4):
                for i4 in range(8):
                    bank = PM.next()
                    for ii in range(4):
                        i = i4 * 4 + ii
                        k.op(pe, lambda i=i, ii=ii, tt=tt, bank=bank: nc.tensor.transpose(
                            bank.ap[:, ii * 128:(ii + 1) * 128], acc[:, i, tt * 128:(tt + 1) * 128], ident[:]),
                            reads=(acc, ident), writes=(bank,), signal=(ii == 3))
                    k.op(act, lambda i4=i4, bank=bank: nc.scalar.copy(out=ytok[:, i4 * 512:(i4 + 1) * 512], in_=bank.ap[:, :]),
                         reads=(bank,), writes=(ytok,))
                k.store(sp, ytok, y_own[o0 + tt * 128:o0 + (tt + 1) * 128, :], ytok[:])
        k.phase_barrier(allb)

    k.finish()


def _rope_tables(pos):
    rows = 4096 // 64
    n_freq = 16
    inv_freq = (10000.0 ** (-np.arange(n_freq, dtype=np.float32) / n_freq)).astype(np.float32)
    row = (pos // 64).astype(np.float32)
    col = (pos % 64).astype(np.float32)
    ang = np.concatenate([row[:, None] * inv_freq, col[:, None] * inv_freq], axis=-1)
    ang = np.concatenate([ang, ang], axis=-1)
    return np.cos(ang).astype(np.float32), np.sin(ang).astype(np.float32)


def core_inputs(inp, core):
    b, half = core // 2, core % 2
    xs = np.asarray(inp["x_sample"][b])
    pos = np.arange(4096)
    if half == 1:
        order = pos
    else:
        order = pos[::-1]
    xp = np.asarray(inp["x_prompt"][4 * core:4 * core + 4])
    if half == 0:
        xp = xp[:, ::-1]
    x_all = np.concatenate([xs[order], xp.reshape(NPR, D)], axis=0)
    cos, sin = _rope_tables(order)
    perm = np.zeros((64, 64), np.float32)
    for m in range(32):
        perm[m + 32, m] = -1.0
        perm[m, m + 32] = 1.0
    sq = lambda a: np.ascontiguousarray(np.asarray(a)[0])
    d = {
        "x_all": np.ascontiguousarray(x_all),
        "cvec": np.ascontiguousarray(np.stack([np.asarray(inp["c"][b]), np.asarray(inp["c_ctx"])])),
        "ckv_ctx": sq(inp["cache_ckv"][b]),
        "krope_ctx": sq(inp["cache_krope"][b]),
        "cosT": np.ascontiguousarray(cos.T), "sinT": np.ascontiguousarray(sin.T),
        "cident": np.eye(128, dtype=np.float32), "cperm": perm,
        "w_mod": sq(inp["w_mod"]), "b_mod": np.asarray(inp["b_mod"]).reshape(1, -1),
        "g_pre_mix": sq(inp["g_pre_mix"]), "w_in": sq(inp["w_in"]), "g_q_lat": sq(inp["g_q_lat"]),
        "g_kv_lat": sq(inp["g_kv_lat"]), "w_uq": sq(inp["w_uq"]), "w_ukv": sq(inp["w_ukv"]),
        "w_glu": sq(inp["w_glu"]), "w_out": sq(inp["w_out"]), "g_post_mix": sq(inp["g_post_mix"]),
        "g_pre_mlp": sq(inp["g_pre_mlp"]), "w_ff1": sq(inp["w_ff1"]), "w_ff2": sq(inp["w_ff2"]),
        "g_post_mlp": sq(inp["g_post_mlp"]), "s5_d": sq(inp["s5_d"]),
    }
    dsel = [0, 1] if half == 1 else [1, 0]
    for nm in ("s5_lam_re", "s5_lam_im", "s5_log_dt", "s5_b_re", "s5_b_im", "s5_c_re", "s5_c_im"):
        d[nm] = np.ascontiguousarray(np.asarray(inp[nm])[0][dsel])
    d["s5_h0"] = np.ascontiguousarray(np.asarray(inp["state_s5"])[b, 0][dsel])
    return d


def kernel(**inputs):
    nc = build()
    in_maps = [core_inputs(inputs, c) for c in range(8)]
    res = run_bass_kernel_spmd(nc, in_maps, core_ids=list(range(8)))
    R = res.results
    y_prompt = np.zeros((32, 256, D), np.float32)
    y_sample = np.zeros((4, 4096, D), np.float32)
    n_ckv = np.zeros((32, 1, 256, NKV), np.float32)
    n_kr = np.zeros((32, 1, 256, NR), np.float32)
    n_s5 = np.zeros((32, 1, 2, 2, 128, 64), np.float32)
    for c in range(8):
        b, half = c // 2, c % 2
        r = R[c]
        yo = np.asarray(r["y_own"])
        ys, yp = yo[:NOWN], yo[NOWN:].reshape(4, 256, D)
        ck = np.asarray(r["ckv_new"]).reshape(4, 256, NKV)
        kr = np.asarray(r["krope_new"]).reshape(4, 256, NR)
        s5 = np.asarray(r["s5_new"]).reshape(4, 2, 2, 128, 64)
        if half == 0:
            y_sample[b, :2048] = ys[::-1]
            yp, ck, kr = yp[:, ::-1], ck[:, ::-1], kr[:, ::-1]
            s5 = s5[:, ::-1]
        else:
            y_sample[b, 2048:] = ys
        y_prompt[4 * c:4 * c + 4] = yp
        n_ckv[4 * c:4 * c + 4, 0] = ck
        n_kr[4 * c:4 * c + 4, 0] = kr
        n_s5[4 * c:4 * c + 4, 0] = s5
    return (y_prompt, y_sample, n_ckv, n_kr, n_s5)


TC = 16


class _S5:
    def __init__(self, k, nc, env):
        self.k, self.nc, self.env = k, nc, env
        self.bufs = []

    def sb(self, shape, dt, name):
        b = self.env["psb"](shape, dt, name)
        self.bufs.append(b)
        return b

    def alloc_persist(self):
        self.Akr = self.sb([128, 2, 9, 64], F32, "Akr")
        self.Aki = self.sb([128, 2, 9, 64], F32, "Aki")
        self.Akn = self.sb([128, 2, 9, 64], F32, "Akn")
        self.h0 = self.sb([128, 2, 2, 64], F32, "h0")
        self.Hfin = self.sb([128, max(1, NPR // 256), 2, 2, 64], F32, "Hfin")

    def pre(self, P):
        k, nc = self.k, self.nc
        dve, act, pe, sp = k.dve, k.act, k.pe, k.sp
        nslow = dict(allow_slow_non_contiguous=True)
        ident, PS = self.env["ident"], self.env["PS"]
        PM = self.env["PM"]
        V = nc.vector
        with contextlib.ExitStack() as ph:
            tmpb = []

            def t(shape, dt, name):
                k.nbuf += 1
                tt = ph.enter_context(nc.sbuf_tensor("%s_%d" % (name, k.nbuf), list(shape), dt))
                b = Buf(k, tt, name)
                tmpb.append(b)
                return b

            def s64(name):
                return t([128, 64], F32, name)
            lr, li, ldt = s64("lr"), s64("li"), s64("ldt")
            dt_, er, th, kf, r_, x_, x2 = (s64(n) for n in ("dt", "er", "th", "kf", "r", "x", "x2"))
            ki = t([128, 64], mybir.dt.int32, "ki")
            sn, cs, cc, ss_, sc_, tm = (s64(n) for n in ("sn", "cs", "cc", "ss", "sc", "tm"))
            ar, ai, den, am1, wr, wi, t1, t2 = (s64(n) for n in ("ar", "ai", "den", "am1", "wr", "wi", "t1", "t2"))
            pr = t([128, 17, 64], F32, "pr")
            pi_ = t([128, 17, 64], F32, "pi")
            Bre, Bim, Cre, Cim = (t([128, 64, 16], F32, n) for n in ("Bre", "Bim", "Cre", "Cim"))
            Bbr, Bbi, Xr, Xi, T1, T2 = (t([128, 64, 16], F32, n) for n in ("Bbr", "Bbi", "Xr", "Xi", "T1", "T2"))
            XbdFr, XbdFi = t([128, 64, 32], F32, "XbdFr"), t([128, 64, 32], F32, "XbdFi")
            Xbdr, Xbdi = t([128, 64, 32], BF16, "Xbdr"), t([128, 64, 32], BF16, "Xbdi")
            Ybdr, Ybdi = t([128, 64, 32], BF16, "Ybdr"), t([128, 64, 32], BF16, "Ybdi")
            Cbdr, Cbdn = t([128, 64, 32], BF16, "Cbdr"), t([128, 64, 32], BF16, "Cbdn")
            kst = Ring([t([128, 128], BF16, "kst%d" % i) for i in range(3)])
            ctr = Ring([t([128, 128], F32, "ctile%d" % i) for i in range(2)])

            def tt_(out, a, b, op, rd, wrb):
                k.op(dve, lambda: V.tensor_tensor(out=out, in0=a, in1=b, op=op), reads=rd, writes=wrb)

            def ts_(out, a, s1, s2, op0, op1, rd, wrb):
                if op1 is None:
                    k.op(dve, lambda: V.tensor_scalar(out=out, in0=a, scalar1=s1, scalar2=None, op0=op0), reads=rd, writes=wrb)
                else:
                    k.op(dve, lambda: V.tensor_scalar(out=out, in0=a, scalar1=s1, scalar2=s2, op0=op0, op1=op1), reads=rd, writes=wrb)

            def cmul(orr, oi, ar_, ai_, br_, bi_, rd, wr_bufs, tA, tB):
                tt_(tA.ap[:], ar_, br_, ALU.mult, rd, (tA,))
                tt_(tB.ap[:], ai_, bi_, ALU.mult, rd, (tB,))
                tt_(orr, tA.ap[:], tB.ap[:], ALU.subtract, (tA, tB), wr_bufs[0:1])
                tt_(tA.ap[:], ar_, bi_, ALU.mult, rd, (tA,))
                tt_(tB.ap[:], ai_, br_, ALU.mult, rd, (tB,))
                tt_(oi, tA.ap[:], tB.ap[:], ALU.add, (tA, tB), wr_bufs[1:2])

            for d in range(2):
                for g2 in range(2):
                    ps_ = slice(g2 * 64, (g2 + 1) * 64)
                    ld = k.load if g2 == 0 else k.loadp
                    ld(sp, lr, lr.ap[ps_, :], P["lam_re"][d].rearrange("(q g) p -> g p q", g=2)[g2], **nslow)
                    ld(sp, li, li.ap[ps_, :], P["lam_im"][d].rearrange("(q g) p -> g p q", g=2)[g2], **nslow)
                    ld(sp, ldt, ldt.ap[ps_, :], P["log_dt"][d].rearrange("(q g) -> g q", g=2)[g2].partition_broadcast(64), **nslow)
                    ld(sp, Bre, Bre.ap[ps_, :, :], P["b_re"][d].rearrange("(q g) p j -> g p q j", g=2)[g2], **nslow)
                    ld(sp, Bim, Bim.ap[ps_, :, :], P["b_im"][d].rearrange("(q g) p j -> g p q j", g=2)[g2], **nslow)
                    for ri in range(2):
                        (k.load if (g2 == 0 and ri == 0 and d == 0) else k.loadp)(
                            sp, self.h0, self.h0.ap[ps_, d, ri, :], P["h0"][d, ri].rearrange("(q g) p -> g p q", g=2)[g2], **nslow)
                for (dstC, srcC) in ((Cre, P["c_re"]), (Cim, P["c_im"])):
                    for q8 in range(8):
                        ctile = ctr.next()
                        for g2 in range(2):
                            (k.load if g2 == 0 else k.loadp)(
                                sp, ctile, ctile.ap[:, g2 * 64:(g2 + 1) * 64],
                                srcC[d].rearrange("(q g) j p -> g q j p", g=2)[g2][q8 * 8:(q8 + 1) * 8])
                        bank = PM.next()
                        k.op(pe, lambda bank=bank, ctile=ctile: nc.tensor.transpose(bank.ap[:, 0:128], ctile.ap[:], ident[:]),
                             reads=(ctile, ident), writes=(bank,))
                        k.op(act, lambda bank=bank, dstC=dstC, q8=q8: nc.scalar.copy(
                            out=dstC.ap[:, q8 * 8:(q8 + 1) * 8, :], in_=bank.ap[:, 0:128].rearrange("p (a b) -> p a b", b=16)),
                            reads=(bank,), writes=(dstC,))
                yield
                k.op(act, lambda: nc.scalar.activation(out=dt_.ap[:], in_=ldt.ap[:], func=AF.Exp), reads=(ldt,), writes=(dt_,))
                tt_(t1.ap[:], lr.ap[:], dt_.ap[:], ALU.mult, (lr, dt_), (t1,))
                k.op(act, lambda: nc.scalar.activation(out=er.ap[:], in_=t1.ap[:], func=AF.Exp), reads=(t1,), writes=(er,))
                tt_(th.ap[:], li.ap[:], dt_.ap[:], ALU.mult, (li, dt_), (th,))
                ts_(kf.ap[:], th.ap[:], 1.0 / (2 * math.pi), None, ALU.mult, None, (th,), (kf,))
                k.op(dve, lambda: V.tensor_copy(out=ki.ap[:], in_=kf.ap[:]), reads=(kf,), writes=(ki,))
                k.op(dve, lambda: V.tensor_copy(out=kf.ap[:], in_=ki.ap[:]), reads=(ki,), writes=(kf,))
                k.op(dve, lambda: V.scalar_tensor_tensor(out=r_.ap[:], in0=kf.ap[:], scalar=-2 * math.pi, in1=th.ap[:],
                                                         op0=ALU.mult, op1=ALU.add), reads=(kf, th), writes=(r_,))
                ts_(x_.ap[:], r_.ap[:], 1.0 / 32, None, ALU.mult, None, (r_,), (x_,))
                tt_(x2.ap[:], x_.ap[:], x_.ap[:], ALU.mult, (x_,), (x2,))
                ts_(sn.ap[:], x2.ap[:], -1.0 / 42, 1.0, ALU.mult, ALU.add, (x2,), (sn,))
                for cdiv in (20.0, 6.0):
                    tt_(tm.ap[:], x2.ap[:], sn.ap[:], ALU.mult, (x2, sn), (tm,))
                    ts_(sn.ap[:], tm.ap[:], -1.0 / cdiv, 1.0, ALU.mult, ALU.add, (tm,), (sn,))
                tt_(sn.ap[:], sn.ap[:], x_.ap[:], ALU.mult, (sn, x_), (sn,))
                ts_(cs.ap[:], x2.ap[:], -1.0 / 56, 1.0, ALU.mult, ALU.add, (x2,), (cs,))
                for cdiv in (30.0, 12.0, 2.0):
                    tt_(tm.ap[:], x2.ap[:], cs.ap[:], ALU.mult, (x2, cs), (tm,))
                    ts_(cs.ap[:], tm.ap[:], -1.0 / cdiv, 1.0, ALU.mult, ALU.add, (tm,), (cs,))
                for _ in range(5):
                    tt_(cc.ap[:], cs.ap[:], cs.ap[:], ALU.mult, (cs,), (cc,))
                    tt_(ss_.ap[:], sn.ap[:], sn.ap[:], ALU.mult, (sn,), (ss_,))
                    tt_(sc_.ap[:], sn.ap[:], cs.ap[:], ALU.mult, (sn, cs), (sc_,))
                    tt_(cs.ap[:], cc.ap[:], ss_.ap[:], ALU.subtract, (cc, ss_), (cs,))
                    ts_(sn.ap[:], sc_.ap[:], 2.0, None, ALU.mult, None, (sc_,), (sn,))
                tt_(ar.ap[:], er.ap[:], cs.ap[:], ALU.mult, (er, cs), (ar,))
                tt_(ai.ap[:], er.ap[:], sn.ap[:], ALU.mult, (er, sn), (ai,))
                yield
                tt_(t1.ap[:], lr.ap[:], lr.ap[:], ALU.mult, (lr,), (t1,))
                tt_(t2.ap[:], li.ap[:], li.ap[:], ALU.mult, (li,), (t2,))
                tt_(den.ap[:], t1.ap[:], t2.ap[:], ALU.add, (t1, t2), (den,))
                k.op(dve, lambda: V.reciprocal(out=den.ap[:], in_=den.ap[:]), reads=(den,), writes=(den,))
                ts_(am1.ap[:], ar.ap[:], -1.0, None, ALU.add, None, (ar,), (am1,))
                tt_(t1.ap[:], am1.ap[:], lr.ap[:], ALU.mult, (am1, lr), (t1,))
                tt_(t2.ap[:], ai.ap[:], li.ap[:], ALU.mult, (ai, li), (t2,))
                tt_(wr.ap[:], t1.ap[:], t2.ap[:], ALU.add, (t1, t2), (wr,))
                tt_(wr.ap[:], wr.ap[:], den.ap[:], ALU.mult, (wr, den), (wr,))
                tt_(t1.ap[:], ai.ap[:], lr.ap[:], ALU.mult, (ai, lr), (t1,))
                tt_(t2.ap[:], am1.ap[:], li.ap[:], ALU.mult, (am1, li), (t2,))
                tt_(wi.ap[:], t1.ap[:], t2.ap[:], ALU.subtract, (t1, t2), (wi,))
                tt_(wi.ap[:], wi.ap[:], den.ap[:], ALU.mult, (wi, den), (wi,))

                def bc(b):
                    return b.ap[:, :].unsqueeze(2).to_broadcast([128, 64, 16])
                cmul(Bbr.ap[:], Bbi.ap[:], bc(wr), bc(wi), Bre.ap[:], Bim.ap[:], (wr, wi, Bre, Bim), (Bbr, Bbi), T1, T2)
                yield
                k.op(dve, lambda: V.memset(pr.ap[:, 0, :], 1.0), writes=(pr,))
                k.op(dve, lambda: V.memset(pi_.ap[:, 0, :], 0.0), writes=(pi_,))
                for n in range(1, 17):
                    cmul(pr.ap[:, n, :], pi_.ap[:, n, :], pr.ap[:, n - 1, :], pi_.ap[:, n - 1, :], ar.ap[:], ai.ap[:],
                         (pr, pi_, ar, ai), (pr, pi_), cc, ss_)
                yield
                Akr, Aki, Akn = self.Akr, self.Aki, self.Akn
                k.op(dve, lambda: V.tensor_copy(out=Akr.ap[:, d, 0, :], in_=pr.ap[:, 16, :]), reads=(pr,), writes=(Akr,))
                k.op(dve, lambda: V.tensor_copy(out=Aki.ap[:, d, 0, :], in_=pi_.ap[:, 16, :]), reads=(pi_,), writes=(Aki,))
                for kk in range(1, 9):
                    cmul(Akr.ap[:, d, kk, :], Aki.ap[:, d, kk, :], Akr.ap[:, d, kk - 1, :], Aki.ap[:, d, kk - 1, :],
                         Akr.ap[:, d, kk - 1, :], Aki.ap[:, d, kk - 1, :], (Akr, Aki), (Akr, Aki), cc, ss_)
                ts_(Akn.ap[:, d, :, :], Aki.ap[:, d, :, :], -1.0, None, ALU.mult, None, (Aki,), (Akn,))
                for (bd, src, sgn) in ((Cbdr, Cre, 1.0), (Cbdn, Cim, -1.0)):
                    k.op(dve, lambda bd=bd: V.memset(bd.ap[:], 0.0), writes=(bd,))
                    for g2 in range(2):
                        ps_ = slice(g2 * 64, (g2 + 1) * 64)
                        ts_(bd.ap[ps_, :, g2 * 16:(g2 + 1) * 16], src.ap[ps_, :, :], sgn, None, ALU.mult, None, (src,), (bd,))
                for bd in (XbdFr, XbdFi, Ybdr, Ybdi):
                    k.op(dve, lambda bd=bd: V.memset(bd.ap[:], 0.0), writes=(bd,))
                for n in range(17):
                    prn = pr.ap[:, n, :].unsqueeze(2).to_broadcast([128, 64, 16])
                    pin = pi_.ap[:, n, :].unsqueeze(2).to_broadcast([128, 64, 16])
                    if n <= 15:
                        cmul(Xr.ap[:], Xi.ap[:], prn, pin, Bbr.ap[:], Bbi.ap[:], (pr, pi_, Bbr, Bbi), (Xr, Xi), T1, T2)
                        for (bdF, bd, src) in ((XbdFr, Xbdr, Xr), (XbdFi, Xbdi, Xi)):
                            for g2 in range(2):
                                ps_ = slice(g2 * 64, (g2 + 1) * 64)
                                k.op(dve, lambda bdF=bdF, src=src, ps_=ps_, g2=g2: V.tensor_copy(
                                    out=bdF.ap[ps_, :, g2 * 16:(g2 + 1) * 16], in_=src.ap[ps_, :, :]), reads=(src,), writes=(bdF,))
                            k.op(dve, lambda bdF=bdF, bd=bd: V.tensor_copy(out=bd.ap[:], in_=bdF.ap[:]), reads=(bdF,), writes=(bd,))
                        for G in range(16):
                            bank = PM.next()
                            k.op(dve, lambda bank=bank: V.memset(bank.ap[:, 0:128], 0.0), writes=(bank,))
                            for q in range(4):
                                Q = 4 * G + q
                                k.op(pe, lambda bank=bank, q=q, Q=Q: nc.tensor.matmul(
                                    bank.ap[32 * q:32 * q + 32, 32 * q:32 * q + 32], Xbdr.ap[:, Q, :], Cbdr.ap[:, Q, :],
                                    start=True, stop=False, tile_position=(0, 32 * q)),
                                    reads=(Xbdr, Cbdr), writes=(bank,), signal=False)
                                k.op(pe, lambda bank=bank, q=q, Q=Q: nc.tensor.matmul(
                                    bank.ap[32 * q:32 * q + 32, 32 * q:32 * q + 32], Xbdi.ap[:, Q, :], Cbdn.ap[:, Q, :],
                                    start=False, stop=True, tile_position=(0, 32 * q)),
                                    reads=(Xbdi, Cbdn), writes=(bank,), signal=(q == 3))
                            st = kst.next()
                            k.op(act, lambda st=st, bank=bank: nc.scalar.copy(out=st.ap[:], in_=bank.ap[:, 0:128]), reads=(bank,), writes=(st,))
                            k.store(sp, st, P["Kst"][d, G, n], st.ap[:])
                            yield
                            for ri, bdF in enumerate((XbdFr, XbdFi)):
                                bank = PM.next()
                                k.op(pe, lambda bank=bank, bdF=bdF, G=G: nc.tensor.transpose(
                                    bank.ap[:, 0:128], bdF.ap[:, 4 * G:4 * G + 4, :].rearrange("p a b -> p (a b)"), ident[:]),
                                    reads=(bdF, ident), writes=(bank,))
                                st = kst.next()
                                k.op(act, lambda st=st, bank=bank: nc.scalar.copy(out=st.ap[:], in_=bank.ap[:, 0:128]), reads=(bank,), writes=(st,))
                                k.store(sp, st, P["M1"][d, G, n, ri], st.ap[:])
                    if n >= 1:
                        cmul(Xr.ap[:], Xi.ap[:], prn, pin, Cre.ap[:], Cim.ap[:], (pr, pi_, Cre, Cim), (Xr, Xi), T1, T2)
                        for g2 in range(2):
                            ps_ = slice(g2 * 64, (g2 + 1) * 64)
                            k.op(dve, lambda ps_=ps_, g2=g2: V.tensor_copy(out=Ybdr.ap[ps_, :, g2 * 16:(g2 + 1) * 16], in_=Xr.ap[ps_, :, :]),
                                 reads=(Xr,), writes=(Ybdr,))
                            ts_(Ybdi.ap[ps_, :, g2 * 16:(g2 + 1) * 16], Xi.ap[ps_, :, :], -1.0, None, ALU.mult, None, (Xi,), (Ybdi,))
                        k.store(sp, Ybdr, P["M2"][d, n, 0], Ybdr.ap[:].rearrange("p a b -> p (a b)"))
                        k.store(sp, Ybdi, P["M2"][d, n, 1], Ybdi.ap[:].rearrange("p a b -> p (a b)"))
                        yield
            k.phase_barrier(tmpb)
        yield

    def setup_main(self, P):
        k, nc = self.k, self.nc
        self.P = P
        S = NOTH + NOWN
        self.NPS = NPR // 256
        self.Kst_sb2 = [self.sb([128, 16, 128], BF16, "Kst_sb") for _ in range(2)]
        self.M1_sb = self.sb([128, 16, 2, 128], BF16, "M1_sb")
        self.M2_sb2 = [self.sb([128, 16, 2, 128], BF16, "M2_sb") for _ in range(2)]
        nsmax = S // TC + 1
        self.Zs = [[[self.sb([128, nsmax], F32, "Zs") for _ in range(2)] for _ in range(2)] for _ in range(4)]
        self.Zp = [[[self.sb([128, self.NPS, 17], F32, "Zp") for _ in range(2)] for _ in range(2)] for _ in range(4)]
        self.Hp2 = [[[self.sb([128, TO // TC], BF16, "Hp") for _ in range(2)] for _ in range(4)] for _ in range(2)]

    def stage(self, which, G, d, ub, ya):
        k, nc = self.k, self.nc
        dve, act, pe, sp, pool = k.dve, k.act, k.pe, k.sp, k.pool
        V = nc.vector
        P = self.P
        gbank, PM = self.env["gbank"], self.env["PM"]
        S = NOTH + NOWN
        NPS = self.NPS
        NT = TO // TC
        par = d
        Kst_sb, M1_sb, M2_sb = self.Kst_sb2[par], self.M1_sb, self.M2_sb2[par]
        Hp = self.Hp2[par]
        if True:
            fwd = (d == 0)
            tokA0 = 0 if fwd else NOTH
            nS = (S - tokA0) // TC
            NA = (T5 - tokA0) // TC
            if which == 1:
                k.load(sp, M1_sb, M1_sb.ap[:], P["M1"][d, G].rearrange("n r p c -> p n r c"))
                k.load(sp, Kst_sb, Kst_sb.ap[:], P["Kst"][d, G].rearrange("n p c -> p n c"))
                k.load(sp, M2_sb, M2_sb.ap[:], P["M2"][d, 1:17, :, :, G * 128:(G + 1) * 128].rearrange("n r p c -> p n r c"))
            uall = ub.ap[:, tokA0:T5].rearrange("p (c s) -> p c s", s=TC)
            uown = ub.ap[:, NOTH:T5].rearrange("p (c s) -> p c s", s=TC)
            for q in (range(4) if which == 1 else []):
                Q = 4 * G + q
                rows = slice(32 * q, 32 * q + 32)
                for ri in range(2):
                    bank = gbank.next()
                    for s in range(TC):
                        n = (TC - 1 - s) if fwd else s
                        k.op(pe, lambda s=s, n=n, bank=bank, ri=ri: nc.tensor.matmul(
                            bank.ap[:, 0:NA], M1_sb.ap[rows, n, ri, :], uall[rows, :, s],
                            start=(s == 0), stop=(s == TC - 1), tile_position=(32 * q, 0)),
                            reads=(M1_sb, ub), writes=(bank,), signal=(s == TC - 1))
                    zs, zp = self.Zs[q][0][ri], self.Zp[q][0][ri]
                    so = 1 if fwd else 0
                    k.op(act, lambda bank=bank, zs=zs, so=so: nc.scalar.copy(out=zs.ap[:, so:so + nS], in_=bank.ap[:, 0:nS]),
                         reads=(bank,), writes=(zs,))
                    hpos = 0 if fwd else nS
                    k.op(dve, lambda zs=zs, hpos=hpos, ri=ri, Q=Q: V.tensor_copy(out=zs.ap[:, hpos:hpos + 1], in_=self.h0.ap[:, d, ri, Q:Q + 1]),
                         reads=(self.h0,), writes=(zs,))
                    if NPS:
                        k.op(act, lambda bank=bank, zp=zp, so=so: nc.scalar.copy(
                            out=zp.ap[:, :, so:so + 16], in_=bank.ap[:, nS:nS + NPS * 16].rearrange("p (a b) -> p a b", b=16)),
                            reads=(bank,), writes=(zp,))
                        hp_ = 0 if fwd else 16
                        k.op(dve, lambda zp=zp, hp_=hp_: V.memset(zp.ap[:, :, hp_:hp_ + 1], 0.0), writes=(zp,))

                def scan(Z, L, nd3):
                    pp, kk, sh = 0, 0, 1
                    while sh < L:
                        src, dst = Z[pp], Z[1 - pp]
                        Ar, Ai, An = (self.Akr.ap[:, d, kk, Q:Q + 1], self.Aki.ap[:, d, kk, Q:Q + 1], self.Akn.ap[:, d, kk, Q:Q + 1])

                        def v(b, lo, hi):
                            return b.ap[:, :, lo:hi] if nd3 else b.ap[:, lo:hi]
                        if fwd:
                            o_lo, o_hi, i_lo, i_hi, c_lo, c_hi = sh, L, 0, L - sh, 0, sh
                        else:
                            o_lo, o_hi, i_lo, i_hi, c_lo, c_hi = 0, L - sh, sh, L, L - sh, L
                        rd = (src[0], src[1], self.Akr, self.Aki, self.Akn)
                        k.op(dve, lambda: V.scalar_tensor_tensor(out=v(dst[0], o_lo, o_hi), in0=v(src[0], i_lo, i_hi), scalar=Ar,
                                                                 in1=v(src[0], o_lo, o_hi), op0=ALU.mult, op1=ALU.add), reads=rd, writes=(dst[0],))
                        k.op(dve, lambda: V.scalar_tensor_tensor(out=v(dst[0], o_lo, o_hi), in0=v(src[1], i_lo, i_hi), scalar=An,
                                                                 in1=v(dst[0], o_lo, o_hi), op0=ALU.mult, op1=ALU.add), reads=rd + (dst[0],), writes=(dst[0],))
                        k.op(dve, lambda: V.scalar_tensor_tensor(out=v(dst[1], o_lo, o_hi), in0=v(src[1], i_lo, i_hi), scalar=Ar,
                                                                 in1=v(src[1], o_lo, o_hi), op0=ALU.mult, op1=ALU.add), reads=rd, writes=(dst[1],))
                        k.op(dve, lambda: V.scalar_tensor_tensor(out=v(dst[1], o_lo, o_hi), in0=v(src[0], i_lo, i_hi), scalar=Ai,
                                                                 in1=v(dst[1], o_lo, o_hi), op0=ALU.mult, op1=ALU.add), reads=rd + (dst[1],), writes=(dst[1],))
                        for ri in range(2):
                            k.op(act, lambda ri=ri: nc.scalar.copy(out=v(dst[ri], c_lo, c_hi), in_=v(src[ri], c_lo, c_hi)),
                                 reads=(src[ri],), writes=(dst[ri],))
                        pp, kk, sh = 1 - pp, kk + 1, sh * 2
                    return pp
                pS = scan(self.Zs[q], nS + 1, False)
                Rs = self.Zs[q][pS]
                if fwd:
                    s_lo = NOTH // TC
                else:
                    s_lo = 1
                for ri in range(2):
                    k.op(act, lambda ri=ri, Rs=Rs, s_lo=s_lo: nc.scalar.copy(out=Hp[q][ri].ap[:, 0:NOWN // TC],
                                                                             in_=Rs[ri].ap[:, s_lo:s_lo + NOWN // TC]),
                         reads=(Rs[ri],), writes=(Hp[q][ri],))
                if NPS:
                    pP = scan(self.Zp[q], 17, True)
                    Rp = self.Zp[q][pP]
                    p_lo = 0 if fwd else 1
                    fin = 16 if fwd else 0
                    for ri in range(2):
                        k.op(act, lambda ri=ri, Rp=Rp, p_lo=p_lo: nc.scalar.copy(
                            out=Hp[q][ri].ap[:, NOWN // TC:NT].rearrange("p (a b) -> p a b", b=16), in_=Rp[ri].ap[:, :, p_lo:p_lo + 16]),
                            reads=(Rp[ri],), writes=(Hp[q][ri],))
                        k.op(dve, lambda ri=ri, Rp=Rp, fin=fin, Q=Q: V.tensor_copy(out=self.Hfin.ap[:, :, d, ri, Q:Q + 1], in_=Rp[ri].ap[:, :, fin:fin + 1]),
                             reads=(Rp[ri],), writes=(self.Hfin,))
            if which == 1:
                return
            yav = ya.ap[:, :].rearrange("p (c s) -> p c s", s=TC)
            for so_ in range(TC):
                bank = gbank.next()
                srcs = list(range(0, so_ + 1)) if fwd else list(range(so_, TC))
                for ii, sp_ in enumerate(srcs):
                    lag = abs(so_ - sp_)
                    k.op(pe, lambda ii=ii, sp_=sp_, lag=lag, bank=bank: nc.tensor.matmul(
                        bank.ap[:, 0:NT], Kst_sb.ap[:, lag, :], uown[:, :, sp_], start=(ii == 0), stop=False),
                        reads=(Kst_sb, ub), writes=(bank,), signal=False)
                n = (so_ + 1) if fwd else (TC - so_)
                for q in range(4):
                    for ri in range(2):
                        last = (q == 3 and ri == 1)
                        k.op(pe, lambda q=q, ri=ri, n=n, bank=bank, last=last: nc.tensor.matmul(
                            bank.ap[32 * q:32 * q + 32, 0:NT], M2_sb.ap[:, n - 1, ri, 32 * q:32 * q + 32], Hp[q][ri].ap[:, :],
                            start=False, stop=(ri == 1), tile_position=(0, 32 * q)),
                            reads=(M2_sb, Hp[q][ri]), writes=(bank,), signal=last)
                k.op(dve, lambda so_=so_, bank=bank: V.tensor_tensor(out=yav[:, :, so_], in0=yav[:, :, so_], in1=bank.ap[:, 0:NT], op=ALU.add),
                     reads=(bank, ya), writes=(ya,))

    def finish(self):
        k = self.k
        if not self.NPS:
            return
        out = self.P["s5_new"]
        for g2 in range(2):
            for sq_ in range(self.NPS):
                for d in range(2):
                    for ri in range(2):
                        k.store(k.sp, self.Hfin, out[sq_, d, ri].rearrange("(q g) p -> g p q", g=2)[g2],
                                self.Hfin.ap[g2 * 64:(g2 + 1) * 64, sq_, d, ri, :], allow_slow_non_contiguous=True)


def _s5_tile(k, nc, env):
    s5 = k.s5
    s5.env["psb"] = env["psb"]
    s5.setup_main(env["s5P"])
    return s5
```
